# Optimizing a Trainium2 kernel written in Bass

```python
import jax, jax.numpy as jnp
from jax import lax
import numpy as np

D_MODEL = 1024
BATCH = 16
SEQ = 2048
DEPTH = 2
DEC_BATCH = 32
DEC_SEQ = 16
PAST_LEN = 2048

CHUNK = 64
N_MIXERS = 2
N_CONV_LAYERS = (DEPTH + 1) // 2
N_DN_LAYERS = DEPTH // 2
D_FF = 2816
CONV_W = 31
DN_HEADS = 8
DN_HEAD_DIM = 128
DN_WIDTH = DN_HEADS * DN_HEAD_DIM
SHORT_CONV_W = 4
DN_IN_WIDTH = 4 * DN_WIDTH + 2 * DN_HEADS
N_MEM = 256
XA_HEADS = 4
XA_HEAD_DIM = D_MODEL // XA_HEADS
EPS = 1e-6

kernel_name = "streaming_conformer_deltanet_step"

F32 = jnp.float32


def rms_norm(x, g):
    xf = x.astype(F32)
    y = xf * lax.rsqrt(jnp.mean(xf * xf, axis=-1, keepdims=True) + EPS)
    return (y * g.astype(F32)).astype(x.dtype)


def layer_norm(x, g, b):
    xf = x.astype(F32)
    xc = xf - jnp.mean(xf, axis=-1, keepdims=True)
    y = xc * lax.rsqrt(jnp.mean(xc * xc, axis=-1, keepdims=True) + EPS)
    return (y * g.astype(F32) + b.astype(F32)).astype(x.dtype)


def l2_normalize(x):
    return x * lax.rsqrt(jnp.sum(x * x, axis=-1, keepdims=True) + EPS)


def swiglu(h, w_gate, w_up, w_down):
    return (jax.nn.silu(h @ w_gate) * (h @ w_up)) @ w_down


def causal_depthwise_conv(x_hist, w):
    return lax.conv_general_dilated(
        x_hist, w[:, None, :].astype(x_hist.dtype), window_strides=(1,), padding='VALID',
        dimension_numbers=('NWC', 'WIO', 'NWC'), feature_group_count=x_hist.shape[-1])


def conformer_conv_module(h, conv_buf, w_in, b_in, dw, dw_b, ln_g, ln_b, w_out, b_out):
    u = h @ w_in + b_in
    val, gate = jnp.split(u, 2, axis=-1)
    u = val * jax.nn.sigmoid(gate)
    hist = jnp.concatenate([conv_buf.astype(u.dtype), u], axis=1)
    c = causal_depthwise_conv(hist, dw) + dw_b
    c = jax.nn.silu(layer_norm(c, ln_g, ln_b))
    return c @ w_out + b_out, hist[:, -(CONV_W - 1):]


def chunk_gated_delta_rule(q, k, v, g, beta, S0):
    B, L, H, _ = q.shape
    DV = v.shape[-1]
    C = min(CHUNK, L)
    n = -(-L // C)
    pad = n * C - L
    if pad:
        padf = lambda t: jnp.pad(t, [(0, 0), (0, pad)] + [(0, 0)] * (t.ndim - 2))
        q, k, v, g, beta = padf(q), padf(k), padf(v), padf(g), padf(beta)

    def blocks(t):
        t = t.reshape((B, n, C) + t.shape[2:])
        return jnp.moveaxis(t, [1, 3], [0, 2])

    q, k, v, g, beta = blocks(q), blocks(k), blocks(v), blocks(g), blocks(beta)
    gc = jnp.cumsum(g, axis=-1)
    idx = jnp.arange(C)
    causal = idx[:, None] >= idx[None, :]
    strict = idx[:, None] > idx[None, :]
    decay = jnp.exp(jnp.where(causal, gc[..., :, None] - gc[..., None, :], -jnp.inf))
    kb = k * beta[..., None]
    kk = jnp.einsum('nbhik,nbhjk->nbhij', kb, k) * decay
    a_mat = jnp.where(strict, kk, 0.0) + jnp.eye(C, dtype=F32)
    rhs = jnp.concatenate([v * beta[..., None], kb * jnp.exp(gc)[..., None]], axis=-1)
    sol = lax.linalg.triangular_solve(a_mat, rhs, left_side=True, lower=True, unit_diagonal=True)
    v_corr, k_cum = sol[..., :DV], sol[..., DV:]
    qk = jnp.einsum('nbhik,nbhjk->nbhij', q, k) * decay
    q_dec = q * jnp.exp(gc)[..., None]
    k_dec = k * jnp.exp(gc[..., -1:] - gc)[..., None]
    g_end = jnp.exp(gc[..., -1])[..., None, None]

    def step(S, xs):
        qk_i, q_i, kc_i, vc_i, kd_i, ge_i = xs
        u = vc_i - jnp.einsum('bhck,bhkv->bhcv', kc_i, S)
        o = jnp.einsum('bhck,bhkv->bhcv', q_i, S) + jnp.einsum('bhij,bhjv->bhiv', qk_i, u)
        S = S * ge_i + jnp.einsum('bhck,bhcv->bhkv', kd_i, u)
        return S, o

    S, o = lax.scan(step, S0, (qk, q_dec, k_cum, v_corr, k_dec, g_end))
    o = jnp.moveaxis(o, [0, 2], [1, 3]).reshape(B, n * C, H, DV)[:, :L]
    return o, S


def gated_deltanet(h, conv_buf, S0, w_in, conv_w, a_log, dt_bias, norm_g, w_out):
    B, L, _ = h.shape
    proj = h @ w_in
    qkv_raw = proj[..., :3 * DN_WIDTH]
    gate = proj[..., 3 * DN_WIDTH:4 * DN_WIDTH]
    a_in = proj[..., 4 * DN_WIDTH:4 * DN_WIDTH + DN_HEADS]
    b_in = proj[..., 4 * DN_WIDTH + DN_HEADS:]
    hist = jnp.concatenate([conv_buf.astype(qkv_raw.dtype), qkv_raw], axis=1)
    qkv = jax.nn.silu(causal_depthwise_conv(hist, conv_w)).astype(F32)
    q, k, v = jnp.split(qkv, 3, axis=-1)
    q = l2_normalize(q.reshape(B, L, DN_HEADS, DN_HEAD_DIM)) * (DN_HEAD_DIM ** -0.5)
    k = l2_normalize(k.reshape(B, L, DN_HEADS, DN_HEAD_DIM))
    v = v.reshape(B, L, DN_HEADS, DN_HEAD_DIM)
    beta = jax.nn.sigmoid(b_in.astype(F32))
    g = -jnp.exp(a_log.astype(F32)) * jax.nn.softplus(a_in.astype(F32) + dt_bias.astype(F32))
    o, S = chunk_gated_delta_rule(q, k, v, g, beta, S0.astype(F32))
    o = rms_norm(o, norm_g) * jax.nn.silu(gate.astype(F32).reshape(B, L, DN_HEADS, DN_HEAD_DIM))
    y = o.reshape(B, L, DN_WIDTH).astype(h.dtype) @ w_out
    return y, S, hist[:, -(SHORT_CONV_W - 1):]


def memory_cross_attention(h, mem_k, mem_v, wq, wo):
    B, L, _ = h.shape
    q = (h @ wq).reshape(B, L, XA_HEADS, XA_HEAD_DIM)
    s = jnp.einsum('blhd,bmhd->bhlm', q, mem_k).astype(F32) * (XA_HEAD_DIM ** -0.5)
    p = jax.nn.softmax(s, axis=-1).astype(mem_v.dtype)
    o = jnp.einsum('bhlm,bmhd->blhd', p, mem_v).reshape(B, L, D_MODEL)
    return o @ wo


def trunk(x, mem_k, mem_v, conv_bufs, dn_states, dn_bufs, P):
    new_conv, new_states, new_dn_bufs = [], [], []
    for i in range(DEPTH):
        h = rms_norm(x, P['norm_pre'][i, 0])
        f = swiglu(h, P['ffn_w_gate'][i, 0], P['ffn_w_up'][i, 0], P['ffn_w_down'][i, 0])
        x = x + 0.5 * rms_norm(f, P['norm_post'][i, 0])
        h = rms_norm(x, P['norm_pre'][i, 1])
        j = i // N_MIXERS
        if i % N_MIXERS == 0:
            y, buf = conformer_conv_module(
                h, conv_bufs[j], P['ca_w_in'][j], P['ca_b_in'][j], P['ca_dw'][j], P['ca_dw_b'][j],
                P['ca_ln_g'][j], P['ca_ln_b'][j], P['ca_w_out'][j], P['ca_b_out'][j])
            new_conv.append(buf)
        else:
            y, S, buf = gated_deltanet(
                h, dn_bufs[j], dn_states[j], P['dn_w_in'][j], P['dn_conv_w'][j], P['dn_a_log'][j],
                P['dn_dt_bias'][j], P['dn_norm_g'][j], P['dn_w_out'][j])
            new_states.append(S)
            new_dn_bufs.append(buf)
        x = x + rms_norm(y, P['norm_post'][i, 1])
        h = rms_norm(x, P['norm_pre'][i, 2])
        a = memory_cross_attention(h, mem_k[i], mem_v[i], P['xa_wq'][i], P['xa_wo'][i])
        x = x + rms_norm(a, P['norm_post'][i, 2])
        h = rms_norm(x, P['norm_pre'][i, 3])
        f = swiglu(h, P['ffn_w_gate'][i, 1], P['ffn_w_up'][i, 1], P['ffn_w_down'][i, 1])
        x = x + 0.5 * rms_norm(f, P['norm_post'][i, 3])
    return x, jnp.stack(new_conv), jnp.stack(new_states), jnp.stack(new_dn_bufs)


def setup_inputs(seed: int = 0) -> dict:
    key = jax.random.key(seed)
    ks = jax.random.split(key, 32)
    nrm = lambda i, shape, scale: jax.random.normal(ks[i], shape, F32) * scale
    dt = jnp.exp(jax.random.uniform(ks[28], (N_DN_LAYERS, DN_HEADS), F32, np.log(1e-3), np.log(1e-1)))
    return {
        'x_prompt': nrm(0, (BATCH, SEQ, D_MODEL), 1.0),
        'x_sample': nrm(1, (DEC_BATCH, DEC_SEQ, D_MODEL), 1.0),
        'mem_prompt': nrm(2, (BATCH, N_MEM, D_MODEL), 1.0),
        'cache_conv_a': nrm(3, (N_CONV_LAYERS, DEC_BATCH, CONV_W - 1, D_MODEL), 0.5),
        'state_dn': nrm(4, (N_DN_LAYERS, DEC_BATCH, DN_HEADS, DN_HEAD_DIM, DN_HEAD_DIM), 0.1),
        'cache_dn_conv': nrm(5, (N_DN_LAYERS, DEC_BATCH, SHORT_CONV_W - 1, 3 * DN_WIDTH), 1.0),
        'cache_mem_k': nrm(6, (DEPTH, DEC_BATCH, N_MEM, XA_HEADS, XA_HEAD_DIM), 1.0),
        'cache_mem_v': nrm(7, (DEPTH, DEC_BATCH, N_MEM, XA_HEADS, XA_HEAD_DIM), 1.0),
        'norm_pre': 1.0 + nrm(8, (DEPTH, 4, D_MODEL), 0.02),
        'norm_post': 1.0 + nrm(9, (DEPTH, 4, D_MODEL), 0.02),
        'ffn_w_gate': nrm(10, (DEPTH, 2, D_MODEL, D_FF), D_MODEL ** -0.5),
        'ffn_w_up': nrm(11, (DEPTH, 2, D_MODEL, D_FF), D_MODEL ** -0.5),
        'ffn_w_down': nrm(12, (DEPTH, 2, D_FF, D_MODEL), D_FF ** -0.5),
        'xa_wq': nrm(13, (DEPTH, D_MODEL, D_MODEL), D_MODEL ** -0.5),
        'xa_wk': nrm(14, (DEPTH, D_MODEL, D_MODEL), D_MODEL ** -0.5),
        'xa_wv': nrm(15, (DEPTH, D_MODEL, D_MODEL), D_MODEL ** -0.5),
        'xa_wo': nrm(16, (DEPTH, D_MODEL, D_MODEL), D_MODEL ** -0.5),
        'ca_w_in': nrm(17, (N_CONV_LAYERS, D_MODEL, 2 * D_MODEL), D_MODEL ** -0.5),
        'ca_b_in': nrm(18, (N_CONV_LAYERS, 2 * D_MODEL), 0.01),
        'ca_dw': nrm(19, (N_CONV_LAYERS, CONV_W, D_MODEL), CONV_W ** -0.5),
        'ca_dw_b': nrm(20, (N_CONV_LAYERS, D_MODEL), 0.01),
        'ca_ln_g': 1.0 + nrm(21, (N_CONV_LAYERS, D_MODEL), 0.02),
        'ca_ln_b': nrm(22, (N_CONV_LAYERS, D_MODEL), 0.01),
        'ca_w_out': nrm(23, (N_CONV_LAYERS, D_MODEL, D_MODEL), D_MODEL ** -0.5),
        'ca_b_out': nrm(24, (N_CONV_LAYERS, D_MODEL), 0.01),
        'dn_w_in': nrm(25, (N_DN_LAYERS, D_MODEL, DN_IN_WIDTH), D_MODEL ** -0.5),
        'dn_conv_w': nrm(26, (N_DN_LAYERS, SHORT_CONV_W, 3 * DN_WIDTH), SHORT_CONV_W ** -0.5),
        'dn_a_log': jnp.log(jax.random.uniform(ks[27], (N_DN_LAYERS, DN_HEADS), F32, 1.0, 16.0)),
        'dn_dt_bias': dt + jnp.log(-jnp.expm1(-dt)),
        'dn_norm_g': 1.0 + nrm(29, (N_DN_LAYERS, DN_HEAD_DIM), 0.02),
        'dn_w_out': nrm(30, (N_DN_LAYERS, DN_WIDTH, D_MODEL), DN_WIDTH ** -0.5),
    }


def reference(x_prompt, x_sample, mem_prompt, cache_conv_a, state_dn, cache_dn_conv, cache_mem_k, cache_mem_v,
              norm_pre, norm_post, ffn_w_gate, ffn_w_up, ffn_w_down, xa_wq, xa_wk, xa_wv, xa_wo,
              ca_w_in, ca_b_in, ca_dw, ca_dw_b, ca_ln_g, ca_ln_b, ca_w_out, ca_b_out,
              dn_w_in, dn_conv_w, dn_a_log, dn_dt_bias, dn_norm_g, dn_w_out):
    P = dict(norm_pre=norm_pre, norm_post=norm_post, ffn_w_gate=ffn_w_gate, ffn_w_up=ffn_w_up,
             ffn_w_down=ffn_w_down, xa_wq=xa_wq, xa_wo=xa_wo,
             ca_w_in=ca_w_in, ca_b_in=ca_b_in, ca_dw=ca_dw, ca_dw_b=ca_dw_b, ca_ln_g=ca_ln_g,
             ca_ln_b=ca_ln_b, ca_w_out=ca_w_out, ca_b_out=ca_b_out,
             dn_w_in=dn_w_in, dn_conv_w=dn_conv_w, dn_a_log=dn_a_log, dn_dt_bias=dn_dt_bias,
             dn_norm_g=dn_norm_g, dn_w_out=dn_w_out)
    Bp = x_prompt.shape[0]
    p_mem_k = jnp.einsum('bmd,lde->lbme', mem_prompt, xa_wk).reshape(DEPTH, Bp, N_MEM, XA_HEADS, XA_HEAD_DIM)
    p_mem_v = jnp.einsum('bmd,lde->lbme', mem_prompt, xa_wv).reshape(DEPTH, Bp, N_MEM, XA_HEADS, XA_HEAD_DIM)
    zero_conv = jnp.zeros((N_CONV_LAYERS, Bp, CONV_W - 1, D_MODEL), x_prompt.dtype)
    zero_state = jnp.zeros((N_DN_LAYERS, Bp, DN_HEADS, DN_HEAD_DIM, DN_HEAD_DIM), F32)
    zero_dn_conv = jnp.zeros((N_DN_LAYERS, Bp, SHORT_CONV_W - 1, 3 * DN_WIDTH), x_prompt.dtype)
    y_prompt, p_conv_a, p_state_dn, p_dn_conv = trunk(
        x_prompt, p_mem_k, p_mem_v, zero_conv, zero_state, zero_dn_conv, P)
    y_sample, s_conv_a, s_state_dn, s_dn_conv = trunk(
        x_sample, cache_mem_k, cache_mem_v, cache_conv_a, state_dn, cache_dn_conv, P)
    return (y_prompt, y_sample, p_conv_a, p_state_dn, p_dn_conv, p_mem_k, p_mem_v, s_conv_a, s_state_dn, s_dn_conv)
```

```python
from contextlib import ExitStack
import math
import numpy as np
import concourse.bass as bass
import concourse.mybir as mybir
from concourse.bass_utils import run_bass_kernel_spmd

F32 = mybir.dt.float32
BF16 = mybir.dt.bfloat16
AF = mybir.ActivationFunctionType
ALU = mybir.AluOpType

D = 1024
DFF = 2816
NJ = 22
NMEM = 256
CONVW = 31
EPS = 1e-6
SEM_CHUNK = 30000
SLOT_COLS = 11264
NSLOT = 2
ARENA = 45056


class Buf:
    __slots__ = ("name", "w", "readers", "pre", "excl")

    def __init__(self, name, pre=(), excl=False):
        self.name = name
        self.excl = excl
        self.w = None
        self.readers = {}
        self.pre = pre


class Eng:
    def __init__(self, name):
        self.name = name
        self.ops = []
        self.count = 0
        self.waited = {}


class Lane:
    def __init__(self, key):
        self.key = key
        self.count = 0


class Sched:
    def __init__(self):
        self.eng = {n: Eng(n) for n in ("pe", "act", "dve", "pool", "sp")}
        self.sem_keys = []
        self.lanes = {}
        self.cur_stream = None
        self.fence_all = False
        self.last = {}

    def _prog_key(self, e, idx):
        k = ("prog", e.name, idx // SEM_CHUNK)
        if k not in self.sem_keys:
            self.sem_keys.append(k)
        return k

    def lane(self, name):
        if name not in self.lanes:
            k = ("lane", name)
            self.sem_keys.append(k)
            self.lanes[name] = Lane(k)
        return self.lanes[name]

    def _deps(self, e, reads, writes, extra):
        deps = {}

        def add(ev):
            if ev is None:
                return
            k, v = ev
            if e.name == "pe" and k[0] == "prog" and k[1] == "pe":
                return
            if deps.get(k, 0) < v:
                deps[k] = v

        for b in reads:
            add(b.w)
            for ev in b.pre:
                add(ev)
        for b in writes:
            add(b.w)
            for ev in b.pre:
                add(ev)
            for k, v in b.readers.items():
                add((k, v))
        for ev in extra:
            add(ev)
        out = []
        for k, v in deps.items():
            if e.waited.get(k, 0) < v:
                e.waited[k] = v
                out.append((k, v))
        return out

    def _commit(self, ev, reads, writes):
        for b in writes:
            b.w = ev
            b.readers = {}
        k, v = ev
        for b in reads:
            if b.readers.get(k, 0) < v:
                b.readers[k] = v

    def op(self, engine, fn, reads=(), writes=()):
        e = self.eng[engine]
        ex = [b for b in reads if b.excl]
        if ex:
            writes = list(writes) + ex
        waits = self._deps(e, reads, writes, ())
        idx = e.count
        e.count += 1
        k = self._prog_key(e, idx)
        ev = (k, idx % SEM_CHUNK + 1)
        self._commit(ev, reads, writes)
        e.ops.append((fn, waits, (k, 1)))
        self.last[(self.cur_stream, engine)] = ev
        return ev

    def dma(self, engine, lane_name, fn, reads=(), writes=()):
        e = self.eng[engine]
        ln = self.lane(lane_name)
        prev = (ln.key, ln.count) if ln.count else None
        waits = self._deps(e, reads, writes, (prev,))
        ln.count += 16
        ev = (ln.key, ln.count)
        self._commit(ev, reads, writes)
        e.ops.append((fn, waits, (ln.key, 16)))
        if not lane_name.startswith("w"):
            self.last[(self.cur_stream, lane_name)] = ev
        return ev

    def fence(self):
        return tuple(ev for (st, _), ev in self.last.items() if st == self.cur_stream or self.fence_all)

    def wait_all(self, engine, events):
        e = self.eng[engine]
        waits = self._deps(e, (), (), events)
        e.ops.append((None, waits, None))

    def check_deadlock(self):
        val = {}
        ptr = {n: 0 for n in self.eng}
        progress = True
        while progress:
            progress = False
            for n, e in self.eng.items():
                while ptr[n] < len(e.ops):
                    fn, waits, inc = e.ops[ptr[n]]
                    if any(val.get(k, 0) < v for k, v in waits):
                        break
                    if inc is not None:
                        val[inc[0]] = val.get(inc[0], 0) + inc[1]
                    ptr[n] += 1
                    progress = True
        stuck = {n: (ptr[n], len(e.ops)) for n, e in self.eng.items() if ptr[n] < len(e.ops)}
        if stuck:
            msg = []
            for n in stuck:
                fn, waits, inc = self.eng[n].ops[ptr[n]]
                msg.append("%s@%d waits %s" % (n, ptr[n], [(k, v, val.get(k, 0)) for k, v in waits if val.get(k, 0) < v]))
            raise RuntimeError("DEADLOCK: " + "; ".join(msg))

    def emit(self, nc, stack):
        self.check_deadlock()
        sems = {}
        for k in self.sem_keys:
            sems[k] = stack.enter_context(nc.semaphore("_".join(str(x) for x in k)))
        block = stack.enter_context(nc.Block())
        handles = {"pe": block.tensor, "act": block.scalar, "dve": block.vector,
                   "pool": block.gpsimd, "sp": block.sync}

        def make(e):
            def body(h):
                for fn, waits, inc in e.ops:
                    for k, v in waits:
                        h.wait_ge(sems[k], v)
                    if fn is None:
                        continue
                    fn(h).then_inc(sems[inc[0]], inc[1])
            return body

        for name, e in self.eng.items():
            if e.ops:
                handles[name](make(e))


V_NPRE = 0
V_CBIN = 64
V_CDW = 80
V_CDWB = 328
V_CLNG = 336
V_CLNB = 344
V_DNCW = 352
V_DNG = 448
V_EPS = 449
V_LNHALF = 450
V_ONE = 451
V_ZERO = 452
V_LNQS = 453
NV = 460
NCST = 512 + 6 * 64


def _modeA(W, m0, m1):
    K, N = W.shape
    a = W.reshape(K // 128, 128, N // 128, 128)[:, :, m0:m1, :]
    return np.ascontiguousarray(a.transpose(1, 2, 0, 3)).reshape(128, -1)


def _modeB(W, c0, c1):
    K, N = W.shape
    a = W.reshape(K // 128, 128, N)[:, :, c0:c1]
    return np.ascontiguousarray(a.transpose(1, 0, 2)).reshape(128, -1)


def build_tiles(inp):
    tiles = {}
    for l in range(2):
        for f in range(2):
            Wg, Wu, Wd = inp["ffn_w_gate"][l, f], inp["ffn_w_up"][l, f], inp["ffn_w_down"][l, f]
            ww = np.stack([Wg, Wu]).reshape(2, 8, 128, NJ, 128)
            ww = ww.transpose(2, 3, 0, 1, 4)
            for g in range(5):
                j0, j1 = 5 * g, min(NJ, 5 * g + 5)
                tiles[("U", l, f, g)] = np.ascontiguousarray(ww[:, j0:j1]).reshape(128, -1)
            for nh in range(2):
                tiles[("D", l, f, nh)] = _modeB(Wd, nh * 512, nh * 512 + 512)
        tiles[("wkA", l)] = _modeA(inp["xa_wk"][l], 0, 8)
        tiles[("wkB", l)] = _modeB(inp["xa_wk"][l], 0, D)
        tiles[("wvB", l)] = _modeB(inp["xa_wv"][l], 0, D)
        tiles[("wq", l)] = _modeA(inp["xa_wq"][l], 0, 8)
        tiles[("wo", l)] = _modeB(inp["xa_wo"][l], 0, D)
    win = inp["ca_w_in"][0]
    a = win.reshape(8, 128, 2, 8, 128)
    a = a.transpose(1, 3, 2, 0, 4)
    for g in range(2):
        tiles[("cin", g)] = np.ascontiguousarray(a[:, 4 * g:4 * g + 4]).reshape(128, -1)
    tiles[("cout",)] = _modeB(inp["ca_w_out"][0], 0, D)
    dw = inp["ca_dw"][0]
    ar = np.arange(128)
    for g in range(4):
        t = np.zeros((128, 2, CONVW, 128), np.float32)
        for ci in range(2):
            c = 2 * g + ci
            t[ar, ci, :, ar] = dw[:, c * 128:(c + 1) * 128].T
        tiles[("cdw", g)] = t.reshape(128, -1)
    dwin = inp["dn_w_in"][0]
    for i, nm in enumerate(("dq", "dk", "dv", "dg")):
        tiles[(nm,)] = _modeA(dwin[:, i * D:(i + 1) * D], 0, 8)
    tiles[("dab",)] = _modeB(dwin[:, 4 * D:4 * D + 16], 0, 16)
    tiles[("dout",)] = _modeB(inp["dn_w_out"][0], 0, D)
    offs = {}
    o = 0
    for k, v in tiles.items():
        assert v.shape[1] <= SLOT_COLS
        offs[k] = (o, v.shape[1])
        o += v.shape[1]
    wpack = np.concatenate([v.astype(np.float32) for v in tiles.values()], axis=1)
    return np.ascontiguousarray(wpack), offs


def tile_offsets():
    sizes = {}
    for l in range(2):
        for f in range(2):
            for g in range(5):
                sizes[("U", l, f, g)] = (min(NJ, 5 * g + 5) - 5 * g) * 2048
            for nh in range(2):
                sizes[("D", l, f, nh)] = NJ * 512
        for nm in ("wkA", "wkB", "wvB", "wq", "wo"):
            sizes[(nm, l)] = 8192
    for g in range(2):
        sizes[("cin", g)] = 8192
    sizes[("cout",)] = 8192
    for g in range(4):
        sizes[("cdw", g)] = 2 * CONVW * 128
    for nm in ("dq", "dk", "dv", "dg"):
        sizes[(nm,)] = 8192
    sizes[("dab",)] = 128
    sizes[("dout",)] = 8192
    offs = {}
    o = 0
    for k, n in sizes.items():
        offs[k] = (o, n)
        o += n
    return offs, o


def pcol(v):
    return np.ascontiguousarray(v.reshape(-1, 128).T)


def build_vecs(inp):
    vec = np.zeros((128, NV), np.float32)
    for l in range(2):
        for j in range(4):
            vec[:, V_NPRE + (l * 4 + j) * 8:V_NPRE + (l * 4 + j) * 8 + 8] = pcol(inp["norm_pre"][l, j])
    vec[:, V_CBIN:V_CBIN + 16] = pcol(inp["ca_b_in"][0])
    for j in range(CONVW):
        vec[:, V_CDW + j * 8:V_CDW + j * 8 + 8] = pcol(inp["ca_dw"][0, j])
    vec[:, V_CDWB:V_CDWB + 8] = pcol(inp["ca_dw_b"][0])
    vec[:, V_CLNG:V_CLNG + 8] = pcol(inp["ca_ln_g"][0])
    vec[:, V_CLNB:V_CLNB + 8] = pcol(inp["ca_ln_b"][0])
    for j in range(4):
        vec[:, V_DNCW + j * 24:V_DNCW + j * 24 + 24] = pcol(inp["dn_conv_w"][0, j])
    vec[:, V_DNG] = inp["dn_norm_g"][0]
    vec[:, V_EPS] = EPS
    vec[:, V_LNHALF] = math.log(0.5)
    vec[:, V_ONE] = 1.0
    vec[:, V_ZERO] = 0.0
    vec[:, V_LNQS] = math.log(128 ** -0.5)
    bvec = np.concatenate([inp["norm_post"].reshape(8, D), inp["ca_b_out"].reshape(1, D)], 0).astype(np.float32)
    hsm = np.concatenate([inp["dn_a_log"][0], inp["dn_dt_bias"][0]])[None, :].astype(np.float32)
    cst = np.zeros((128, NCST), np.float32)
    idx = np.arange(128)
    cst[:, 0:128] = np.eye(128)
    cst[:, 128:256] = (idx[:, None] <= idx[None, :])
    cst[:, 256:384] = (idx[:, None] > idx[None, :])
    cst[:, 384:512] = 1.0
    i64 = np.arange(64)
    for l in range(6):
        b = 2 ** l
        i, j = i64[:, None], i64[None, :]
        cst[0:64, 512 + l * 64:512 + (l + 1) * 64] = ((i // (2 * b) == j // (2 * b)) & ((i // b) % 2 == 1) & ((j // b) % 2 == 0))
    return vec, np.ascontiguousarray(bvec), hsm, cst


class Unit:
    pass


class Ctx:
    pass


class Builder:
    def __init__(self, cfg):
        self.cfg = cfg
        self.nc = bass.Bass("TRN2", target_bir_lowering=False)
        self.S = Sched()
        self.out_events = []
        self.cur_fence = ()
        self.olane = 0
        self.ilane = 0
        self._c = None

    @property
    def c(self):
        return self._c

    @c.setter
    def c(self, v):
        self._c = v
        self.S.cur_stream = v.idx

    arena = property(lambda s: s.c.arena)
    hT = property(lambda s: s.c.hT)
    hTB = property(lambda s: s.c.hTB)
    xs_sc = property(lambda s: s.c.xs_sc)
    xs_scB = property(lambda s: s.c.xs_scB)
    junk = property(lambda s: s.c.junk)
    junkB = property(lambda s: s.c.junkB)
    stat = property(lambda s: s.c.stat)
    tmpf = property(lambda s: s.c.tmpf)
    tmpfB = property(lambda s: s.c.tmpfB)
    gpost = property(lambda s: s.c.gpost)
    gpostB = property(lambda s: s.c.gpostB)
    Sf = property(lambda s: s.c.Sf)
    Sb = property(lambda s: s.c.Sb)
    SfB = property(lambda s: s.c.SfB)
    SbB = property(lambda s: s.c.SbB)
    convh = property(lambda s: s.c.convh)
    convhB = property(lambda s: s.c.convhB)
    dnh = property(lambda s: s.c.dnh)
    dnhB = property(lambda s: s.c.dnhB)
    PB = property(lambda s: s.c.PB)

    def abuf(self, name):
        return Buf(name, self.cur_fence)

    def new_phase(self):
        self.cur_fence = self.S.fence()

    def sb(self, name, shape, dt):
        return self.st.enter_context(self.nc.sbuf_tensor("sb_" + name, shape, dt))

    def act(self, out, in_, func, reads, writes, **kw):
        self.S.op("act", lambda h: h.activation(out=out, in_=in_, func=func, **kw), reads, writes)

    def amul(self, out, in_, mul, reads, writes):
        self.S.op("act", lambda h: h.mul(out=out, in_=in_, mul=mul), reads, writes)

    def memset(self, ap, val, writes):
        self.S.op("dve", lambda h: h.memset(ap, val), (), writes)

    def tt(self, out, in0, in1, op, reads, writes):
        self.S.op("dve", lambda h: h.tensor_tensor(out=out, in0=in0, in1=in1, op=op), reads, writes)

    def ts(self, out, in0, s1, s2, op0, op1, reads, writes):
        if s2 is None:
            self.S.op("dve", lambda h: h.tensor_scalar(out=out, in0=in0, scalar1=s1, scalar2=None, op0=op0), reads, writes)
        else:
            self.S.op("dve", lambda h: h.tensor_scalar(out=out, in0=in0, scalar1=s1, scalar2=s2, op0=op0, op1=op1), reads, writes)

    def stt(self, out, in0, scalar, in1, op0, op1, reads, writes):
        self.S.op("dve", lambda h: h.scalar_tensor_tensor(out=out, in0=in0, scalar=scalar, in1=in1, op0=op0, op1=op1), reads, writes)

    def cp(self, out, in_, reads, writes, eng="dve"):
        if eng == "dve":
            self.S.op("dve", lambda h: h.tensor_copy(out=out, in_=in_), reads, writes)
        else:
            self.S.op("act", lambda h: h.activation(out=out, in_=in_, func=AF.Identity), reads, writes)

    def mm(self, out, lhsT, rhs, start, stop, reads, writes):
        self.S.op("pe", lambda h: h.matmul(out, lhsT=lhsT, rhs=rhs, start=start, stop=stop), reads, writes)

    def tr(self, out, in_, ident, reads, writes):
        self.S.op("pe", lambda h: h.transpose(out=out, in_=in_, identity=ident), reads, writes)

    def load(self, out, in_, writes, reads=(), **kw):
        self.ilane = (self.ilane + 1) % 4
        return self.S.dma("sp", "in%d_%d" % (self.c.idx, self.ilane), lambda h: h.dma_start(out=out, in_=in_, **kw), reads, writes)

    def store(self, out, in_, reads, **kw):
        self.olane = (self.olane + 1) % 4
        ev = self.S.dma("sp", "out%d_%d" % (self.c.idx, self.olane), lambda h: h.dma_start(out=out, in_=in_, **kw), reads, [Buf("dram")])
        self.out_events.append(ev)
        return ev

    def ps(self, i, p=128, n=512):
        assert i < self.c.nbank
        i += self.c.bank0
        return self.pd[i // 2][0:p, (i % 2) * 512:(i % 2) * 512 + n]

    def psb(self, i, p=128, n=1024):
        assert i < self.c.nbank
        i += self.c.bank0
        return self.pd[i // 2][0:p, (i % 2) * 512:(i % 2) * 512 + 512].bitcast(BF16)[:, 0:n]

    def vcol(self, c, p=128, n=1):
        return self.vecs[0:p, c:c + n]

    def next_stat(self, n=1):
        assert n <= 16
        r = self.c.statn % 8
        self.c.statn += 1
        self.cur_statB = self.c.statB[r]
        return r * 16

    def wload(self, key):
        if key in self.wcache:
            ent = self.wcache[key]
            ent[2] += 1
            return ent[0], ent[1]
        off, n = self.woffs[key]
        s = self.wcount % NSLOT
        self.wcount += 1
        prev = self.slot_owner[s]
        if prev is not None:
            assert self.wcache[prev][2] == self.nstreams, ("ring slot reused before all streams read it", prev, key)
        slot = self.ring[s]
        buf = self.ringB[s]
        src = self.wpack[:, off:off + n]
        self.S.dma("pool", "w%d" % s,
                   lambda h: h.dma_start(out=slot[:, 0:n], in_=src, max_dma_last_dim=8192),
                   (), [buf])
        self.wcache[key] = [slot, buf, 1]
        self.slot_owner[s] = key
        return slot, buf

    def make_ctx(self, idx, a0, an, bank0, nbank, nseg):
        sb = self.sb
        c = Ctx()
        c.idx = idx
        c.arena = self.arena_all[:, a0:a0 + an]
        c.an = an
        c.bank0, c.nbank = bank0, nbank
        c.PB = self.PBall[bank0:bank0 + nbank]
        n = "c%d" % idx
        c.x = sb("x" + n, [128, 2, D], F32)
        c.xB = Buf("x" + n)
        c.hT = sb("hT" + n, [128, 8, 256], BF16)
        c.hTB = Buf("hT" + n)
        c.xs_sc = sb("xs" + n, [128, 2, D], BF16)
        c.xs_scB = [Buf("xs0" + n), Buf("xs1" + n)]
        c.junk = sb("junk" + n, [128, D], BF16)
        c.junkB = Buf("junk" + n)
        c.stat = sb("stat" + n, [128, 128], F32)
        c.statB = [Buf("stat%d%s" % (i, n)) for i in range(8)]
        c.statn = 0
        c.tmpf = sb("tmpf" + n, [128, 512], F32)
        c.tmpfB = Buf("tmpf" + n)
        c.gpost = sb("gpost" + n, [128, D], F32)
        c.gpostB = Buf("gpost" + n)
        c.convh = sb("convh" + n, [128, 1, 8, 30], F32)
        c.convhB = [Buf("convh" + n)]
        c.dnh = sb("dnh" + n, [128, nseg, 24, 3], F32)
        c.dnhB = [Buf("dnh%d%s" % (i, n)) for i in range(nseg)]
        c.Sf = sb("Sf" + n, [128, 1, 8, 128], F32)
        c.Sb = sb("Sb" + n, [128, 8, 128], BF16)
        c.SfB = Buf("Sf" + n)
        c.SbB = Buf("Sb" + n)
        return c

    def build(self):
        cfg = self.cfg
        nc = self.nc
        NP, PL, NS = cfg["n_pseq"], cfg["plen"], cfg["n_sseq"]
        self.woffs, wtot = tile_offsets()
        dt = nc.dram_tensor

        def din(name, shape):
            return dt(name, shape, F32, kind="ExternalInput").ap()

        def dout(name, shape):
            return dt(name, shape, F32, kind="ExternalOutput").ap()

        self.xp = din("xp", [NP, PL, D])
        self.memp = din("memp", [NP, NMEM, D])
        self.wpack = din("wpack", [128, wtot])
        self.vecs_d = din("vecs", [128, NV])
        self.bvec_d = din("bvec", [9, D])
        self.hsm_d = din("hsm", [1, 16])
        self.cst_d = din("cst", [128, NCST])
        self.yp = dout("yp", [NP, PL, D])
        self.p_conv = dout("p_conv", [NP, 30, D])
        self.p_state = dout("p_state", [NP, 8, 128, 128])
        self.p_dnc = dout("p_dnc", [NP, 3, 3 * D])
        self.p_mk = dout("p_mk", [2, NP, NMEM, D])
        self.p_mv = dout("p_mv", [2, NP, NMEM, D])
        self.kvs = dt("kvs", [2, NP, 128, 4096], BF16, kind="Internal").ap()
        self.kvsB = {(l, q): Buf("kvs%d_%d" % (l, q)) for l in range(2) for q in range(NP)}
        if NS:
            self.xs = din("xs", [NS * 16, D])
            self.cconv = din("cconv", [NS * 30, D])
            self.sdn = din("sdn", [NS, 8, 128, 128])
            self.cdn = din("cdn", [NS * 3, 3 * D])
            self.cmk = din("cmk", [2, NS, NMEM, D])
            self.cmv = din("cmv", [2, NS, NMEM, D])
            self.ys = dout("ys", [NS * 16, D])
            self.s_conv = dout("s_conv", [NS, 30, D])
            self.s_state = dout("s_state", [NS, 8, 128, 128])
            self.s_dnc = dout("s_dnc", [NS, 3, 3 * D])

        with ExitStack() as st:
            self.st = st
            sb = self.sb
            self.pd = [st.enter_context(nc.psum_tensor("pd%d" % i, [128, 1024], F32)) for i in range(4)]
            self.PBall = [Buf("ps%d" % i, excl=True) for i in range(8)]
            self.vecs = sb("vecs", [128, NV], F32)
            self.cst = sb("cstf", [128, NCST], F32)
            self.cstb = sb("cstb", [128, NCST], BF16)
            self.hsm = sb("hsm", [64, 16], F32)
            self.nega = sb("nega", [64, 8], F32)
            self.boutb = sb("boutb", [1, D], BF16)
            self.ring = [sb("ring%d" % i, [128, SLOT_COLS], BF16) for i in range(NSLOT)]
            self.ringB = [Buf("ring%d" % i) for i in range(NSLOT)]
            self.wcount = 0
            self.wcache = {}
            self.slot_owner = [None] * NSLOT
            self.arena_all = sb("arena", [128, ARENA], BF16)
            cB = self.cB = Buf("consts")
            half = ARENA // 2
            ctxs = [self.make_ctx(0, 0, half, 0, 4, max(NS // 2, 1)), self.make_ctx(1, half, half, 4, 4, max(NS // 2, 1))]
            self.c = ctxs[0]

            self.load(self.vecs[:], self.vecs_d, [cB])
            self.load(self.cst[:], self.cst_d, [cB])
            self.load(self.hsm[:], self.hsm_d[0].partition_broadcast(64), [cB])
            boutf = ctxs[0].tmpf[0:1, :]
            for hf in range(2):
                self.load(boutf, self.bvec_d[8:9, hf * 512:(hf + 1) * 512], [ctxs[0].tmpfB])
                self.cp(self.boutb[:, hf * 512:(hf + 1) * 512], boutf, [ctxs[0].tmpfB], [cB])
            self.cp(self.cstb[:], self.cst[:], [cB], [cB])
            self.act(self.nega[:], self.hsm[:, 0:8], AF.Exp, [cB], [cB])
            self.ts(self.nega[:], self.nega[:], -1.0, None, ALU.mult, None, [cB], [cB])
            self.identb = self.cstb[:, 0:128]
            self.identf = self.cst[:, 0:128]
            self.onesb = self.cstb[:, 384:512]

            stop = cfg.get("stop", 99)
            TU = 256
            npass = PL // TU
            for s0 in range(0, NP, 2):
                seqs = list(range(s0, min(NP, s0 + 2)))
                for j in range(npass):
                    units = []
                    for si, s in enumerate(seqs):
                        u = Unit()
                        u.kind, u.seq, u.pos, u.T, u.first, u.last = "P", s, j, TU, j == 0, j == npass - 1
                        u.subs = [(i * 128, 128) for i in range(TU // 128)]
                        u.segs = [(0, TU, 0)]
                        u.c = ctxs[si]
                        units.append(u)
                    self.run_pass(units, stop)
            if NS:
                units = []
                nsp = 2 if NS % 2 == 0 else 1
                per = NS // nsp
                for si in range(nsp):
                    u = Unit()
                    u.kind, u.seq, u.pos, u.T, u.first, u.last = "S", 0, 0, per * 16, True, True
                    u.sbase = si * per
                    u.subs = [(0, per * 16)]
                    u.segs = [(i * 16, 16, i) for i in range(per)]
                    u.c = ctxs[si]
                    units.append(u)
                self.run_pass(units, stop)
            self.S.wait_all("sp", self.out_events)
            self.S.emit(nc, st)
        return nc

    def run_pass(self, units, stop):
        self.nstreams = len(units)
        self.wcache = {}
        self.slot_owner = [None] * NSLOT
        for u in units:
            self.c = u.c
            self.load_x(u)
        phases = [lambda u: self.ffn(u, 0, 0), lambda u: self.conf(u), lambda u: self.xattn(u, 0), lambda u: self.ffn(u, 0, 1),
                  lambda u: self.ffn(u, 1, 0), lambda u: self.gdn(u), lambda u: self.xattn(u, 1), lambda u: self.ffn(u, 1, 1)]
        for i, ph in enumerate(phases):
            if i >= stop:
                break
            gens = []
            for u in units:
                self.c = u.c
                gens.append((u, ph(u)))
            live = list(gens)
            while live:
                nxt = []
                for u, g in live:
                    self.c = u.c
                    try:
                        next(g)
                        nxt.append((u, g))
                    except StopIteration:
                        pass
                live = nxt
        for u in units:
            self.c = u.c
            self.store_x(u)

    def load_x(self, u):
        c = u.c
        if u.kind == "P":
            src = self.xp[u.seq, u.pos * u.T:(u.pos + 1) * u.T, :].rearrange("(s p) d -> p s d", p=128)
            self.load(c.x[:, :, :], src, [c.xB])
        else:
            self.load(c.x[0:u.T, 0, :], self.xs[u.sbase * 16:u.sbase * 16 + u.T, :], [c.xB])

    def store_x(self, u):
        c = u.c
        if u.kind == "P":
            dst = self.yp[u.seq, u.pos * u.T:(u.pos + 1) * u.T, :].rearrange("(s p) d -> p s d", p=128)
            self.store(dst, c.x[:, :, :], [c.xB])
        else:
            self.store(self.ys[u.sbase * 16:u.sbase * 16 + u.T, :], c.x[0:u.T, 0, :], [c.xB])

    def prenorm(self, u, l, j):
        c_ = u.c
        g0 = V_NPRE + (l * 4 + j) * 8
        for si, (t0, nt) in enumerate(u.subs):
            xin = c_.x[0:nt, si, :]
            c = self.next_stat(3)
            sB = self.cur_statB
            st = self.stat
            self.act(self.junk[0:nt, :], xin, AF.Square, [c_.xB], [self.junkB, sB], accum_out=st[0:nt, c:c + 1])
            self.act(st[0:nt, c + 1:c + 2], st[0:nt, c:c + 1], AF.Ln, [sB, self.cB], [sB],
                     scale=1.0 / D, bias=self.vcol(V_EPS, nt))
            self.act(st[0:nt, c + 2:c + 3], st[0:nt, c + 1:c + 2], AF.Exp, [sB], [sB], scale=-0.5)
            k = si % 2
            xs = self.xs_sc[0:nt, k, :]
            self.ts(xs, xin, st[0:nt, c + 2:c + 3], None, ALU.mult, None, [c_.xB, sB], [self.xs_scB[k]])
            bank = 2 + si % 2
            pb = self.psb(bank, 128, 8 * nt)
            for cc in range(8):
                self.tr(pb[:, cc * nt:(cc + 1) * nt], self.xs_sc[0:nt, k, cc * 128:(cc + 1) * 128],
                        self.cstb[0:nt, 0:nt], [self.xs_scB[k], self.cB], [self.PB[bank]])
            gbc = self.vecs[:, g0:g0 + 8].unsqueeze(2).to_broadcast([128, 8, nt])
            self.tt(self.hT[:, :, t0:t0 + nt], pb.rearrange("p (c t) -> p c t", c=8), gbc, ALU.mult,
                    [self.PB[bank], self.cB], [self.hTB])

    def gpost_load(self, row):
        self.load(self.gpost[:, :], self.bvec_d[row].partition_broadcast(128), [self.gpostB])

    def postnorm(self, u, si, nt, banks, half_scale):
        c_ = u.c
        c = self.next_stat(5)
        sB = self.cur_statB
        st = self.stat
        for hf in range(2):
            self.act(self.junk[0:nt, 0:512], self.ps(banks[hf], nt), AF.Square, [self.PB[banks[hf]]],
                     [self.junkB, sB], accum_out=st[0:nt, c + hf:c + hf + 1])
        self.tt(st[0:nt, c + 2:c + 3], st[0:nt, c:c + 1], st[0:nt, c + 1:c + 2], ALU.add, [sB], [sB])
        self.act(st[0:nt, c + 3:c + 4], st[0:nt, c + 2:c + 3], AF.Ln, [sB, self.cB], [sB],
                 scale=1.0 / D, bias=self.vcol(V_EPS, nt))
        self.act(st[0:nt, c + 4:c + 5], st[0:nt, c + 3:c + 4], AF.Exp, [sB, self.cB], [sB], scale=-0.5,
                 bias=self.vcol(V_LNHALF if half_scale else V_ZERO, nt))
        for hf in range(2):
            tmp = self.tmpf[0:nt, :]
            self.tt(tmp, self.ps(banks[hf], nt), self.gpost[0:nt, hf * 512:(hf + 1) * 512], ALU.mult,
                    [self.PB[banks[hf]], self.gpostB], [self.tmpfB])
            xo = c_.x[0:nt, si, hf * 512:(hf + 1) * 512]
            self.stt(xo, tmp, st[0:nt, c + 4:c + 5], xo, ALU.mult, ALU.add, [self.tmpfB, sB, c_.xB], [c_.xB])

    def proj_out(self, u, actT, actBs, slot, slotB, half_scale, bias=False):
        for si, (t0, nt) in enumerate(u.subs):
            banks = (0, 1) if si % 2 == 0 else (2, 3)
            for hf in range(2):
                for k in range(8):
                    self.mm(self.ps(banks[hf], nt), actT[:, k, t0:t0 + nt],
                            slot[:, k * D + hf * 512:k * D + hf * 512 + 512], k == 0, (k == 7) and not bias,
                            [actBs[k], slotB], [self.PB[banks[hf]]])
                if bias:
                    self.mm(self.ps(banks[hf], nt), self.cstb[0:1, 384:384 + nt],
                            self.boutb[0:1, hf * 512:hf * 512 + 512], False, True,
                            [self.cB], [self.PB[banks[hf]]])
            self.postnorm(u, si, nt, banks, half_scale)
            yield

    def ffn(self, u, l, f):
        T = u.T
        self.new_phase()
        self.prenorm(u, l, 0 if f == 0 else 3)
        yield
        hid = self.arena[:, 0:NJ * T].rearrange("p (j t) -> p j t", j=NJ)
        hidB = [self.abuf("hid%d" % j) for j in range(NJ)]
        sg = self.arena[:, NJ * T:NJ * T + 4 * T].bitcast(F32).rearrange("p (k t) -> p k t", k=2)
        sgB = [self.abuf("sg0"), self.abuf("sg1")]
        for g in range(5):
            slot, slotB = self.wload(("U", l, f, g))
            yield
            for jj in range(min(NJ, 5 * g + 5) - 5 * g):
                j = 5 * g + jj
                bg, bu = (0, 1) if j % 2 == 0 else (2, 3)
                for w, bank in ((0, bg), (1, bu)):
                    for k in range(8):
                        o = ((jj * 2 + w) * 8 + k) * 128
                        self.mm(self.ps(bank, 128, T), slot[:, o:o + 128], self.hT[:, k, 0:T], k == 0, k == 7,
                                [slotB, self.hTB], [self.PB[bank]])
                kk = j % 2
                self.act(sg[:, kk, 0:T], self.ps(bg, 128, T), AF.Silu, [self.PB[bg]], [sgB[kk]])
                self.tt(hid[:, j, 0:T], sg[:, kk, 0:T], self.ps(bu, 128, T), ALU.mult,
                        [sgB[kk], self.PB[bu]], [hidB[j]])
                yield
        self.gpost_load(l * 4 + (0 if f == 0 else 3))
        slot0, slot0B = self.wload(("D", l, f, 0))
        yield
        for si, (t0, nt) in enumerate(u.subs):
            for j in range(NJ):
                self.mm(self.ps(si, nt), hid[:, j, t0:t0 + nt], slot0[:, j * 512:(j + 1) * 512], j == 0, j == NJ - 1,
                        [hidB[j], slot0B], [self.PB[si]])
            yield
        slot1, slot1B = self.wload(("D", l, f, 1))
        yield
        for si, (t0, nt) in enumerate(u.subs):
            b1 = 2 + si
            for j in range(NJ):
                self.mm(self.ps(b1, nt), hid[:, j, t0:t0 + nt], slot1[:, j * 512:(j + 1) * 512], j == 0, j == NJ - 1,
                        [hidB[j], slot1B], [self.PB[b1]])
            self.postnorm(u, si, nt, (si, b1), True)
            yield

    def conf(self, u):
        T = u.T
        self.new_phase()
        self.prenorm(u, 0, 1)
        yield
        nseg = len(u.segs)
        L = u.segs[0][1]
        HW = 30 + L
        o = 0
        histb = self.arena[:, o:o + 8 * nseg * HW].rearrange("p (c s w) -> p c s w", c=8, s=nseg)
        o += 8 * nseg * HW
        tail = self.arena[:, o:o + 2 * 8 * nseg * 30].bitcast(F32).rearrange("p (c s w) -> p c s w", c=8, s=nseg)
        o += 2 * 8 * nseg * 30
        cc = self.arena[:, o:o + 2 * 8 * T].bitcast(F32).rearrange("p (c t) -> p c t", c=8)
        o += 2 * 8 * T
        aT = self.arena[:, o:o + 8 * T].rearrange("p (c t) -> p c t", c=8)
        o += 8 * T
        sig = self.arena[:, o:o + 4 * T].bitcast(F32).rearrange("p (k t) -> p k t", k=2)
        o += 4 * T
        sqb = self.arena[:, o:o + 2 * T].rearrange("p (k t) -> p k t", k=2)
        o += 2 * T
        ccb = self.arena[:, o:o + 2 * T].rearrange("p (k t) -> p k t", k=2)
        o += 2 * T
        mean = self.arena[:, o:o + 2 * T].bitcast(F32)
        o += 2 * T
        rstd = self.arena[:, o:o + 2 * T].bitcast(F32)
        o += 2 * T
        ob = self.arena[0:30, o:o + 2 * D].bitcast(F32)
        obB = self.abuf("convout")
        o += 2 * D
        histB = [self.abuf("hist%d" % c) for c in range(8)]
        tailB = [self.abuf("tail%d" % c) for c in range(8)]
        ccB = [self.abuf("cc%d" % c) for c in range(8)]
        sigB = [self.abuf("sig0"), self.abuf("sig1")]
        sqbB = [self.abuf("sqb0"), self.abuf("sqb1")]
        ccbB = [self.abuf("ccb0"), self.abuf("ccb1")]
        stB = self.abuf("lnstat")
        KEEP = 30 - min(L, 30)
        if u.kind == "P":
            if u.first:
                self.memset(histb[:, :, 0, 0:30], 0.0, histB)
            else:
                self.cp(histb[:, :, 0, 0:30], self.convh[:, 0, :, :], [self.convhB[0]], histB)
        else:
            nr = nseg * 30
            ctm = self.arena[0:nr, o:o + 2 * D].bitcast(F32)
            o += 2 * D
            ctmB = self.abuf("ctm")
            self.load(ctm, self.cconv[u.sbase * 30:u.sbase * 30 + nr, :], [ctmB])
            for c in range(8):
                bank = c % 2
                self.tr(self.ps(bank, 128, nr), ctm[:, c * 128:(c + 1) * 128], self.cst[0:nr, 0:nr],
                        [ctmB, self.cB], [self.PB[bank]])
                p3 = self.ps(bank, 128, nr).rearrange("p (s w) -> p s w", s=nseg)
                self.cp(histb[:, c, :, 0:30], p3, [self.PB[bank]], [histB[c]])
                self.cp(tail[:, c, :, 0:KEEP], p3[:, :, 30 - KEEP:30], [self.PB[bank]], [tailB[c]], eng="act")
        assert o <= self.c.an, o
        yield
        NEW = min(L, 30)
        for g in range(2):
            slot, slotB = self.wload(("cin", g))
            yield
            for ci in range(4):
                c = 4 * g + ci
                bv, bg = (0, 1) if c % 2 == 0 else (2, 3)
                for w, bank in ((0, bv), (1, bg)):
                    for k in range(8):
                        oo = ((ci * 2 + w) * 8 + k) * 128
                        self.mm(self.ps(bank, 128, T), slot[:, oo:oo + 128], self.hT[:, k, 0:T], k == 0, k == 7,
                                [slotB, self.hTB], [self.PB[bank]])
                kk = c % 2
                self.act(sig[:, kk, 0:T], self.ps(bg, 128, T), AF.Sigmoid, [self.PB[bg], self.cB], [sigB[kk]],
                         bias=self.vcol(V_CBIN + 8 + c))
                pv3 = self.ps(bv, 128, T).rearrange("p (s w) -> p s w", s=nseg)
                sg3 = sig[:, kk, 0:T].rearrange("p (s w) -> p s w", s=nseg)
                self.stt(histb[:, c, :, 30:30 + L], pv3, self.vcol(V_CBIN + c), sg3,
                         ALU.add, ALU.mult, [self.PB[bv], sigB[kk], self.cB], [histB[c]])
                self.stt(tail[:, c, :, KEEP:30], pv3[:, :, L - NEW:L], self.vcol(V_CBIN + c), sg3[:, :, L - NEW:L],
                         ALU.add, ALU.mult, [self.PB[bv], sigB[kk], self.cB], [tailB[c]])
                yield
        for s, (c0, Ls, sidx) in enumerate(u.segs):
            if u.kind == "P" and not u.last:
                self.cp(self.convh[:, 0, :, :], tail[:, :, s, :], tailB, [self.convhB[0]])
            else:
                dst = self.p_conv[u.seq] if u.kind == "P" else self.s_conv[u.sbase + s]
                for c in range(8):
                    bank = 2 + c % 2
                    self.tr(self.ps(bank, 30, 128), tail[:, c, s, :], self.identf,
                            [tailB[c], self.cB], [self.PB[bank]])
                    self.cp(ob[:, c * 128:(c + 1) * 128], self.ps(bank, 30, 128), [self.PB[bank]], [obB])
                self.store(dst, ob, [obB])
            yield
        for g in range(4):
            slot, slotB = self.wload(("cdw", g))
            yield
            for ci in range(2):
                c = 2 * g + ci
                bank = c % 2
                for s in range(nseg):
                    for j in range(CONVW):
                        oo = (ci * CONVW + j) * 128
                        self.mm(self.ps(bank, 128, T)[:, s * L:(s + 1) * L], slot[:, oo:oo + 128], histb[:, c, s, j:j + L],
                                j == 0, j == CONVW - 1, [slotB, histB[c]], [self.PB[bank]])
                self.act(cc[:, c, 0:T], self.ps(bank, 128, T), AF.Identity, [self.PB[bank], self.cB], [ccB[c]],
                         bias=self.vcol(V_CDWB + c))
                yield
        for c in range(8):
            kk = c % 2
            self.cp(ccb[:, kk, 0:T], cc[:, c, 0:T], [ccB[c]], [ccbB[kk]], eng="act")
            self.act(sqb[:, kk, 0:T], cc[:, c, 0:T], AF.Square, [ccB[c]], [sqbB[kk]])
            self.mm(self.ps(2, 128, T), self.onesb, ccb[:, kk, 0:T], c == 0, c == 7, [ccbB[kk], self.cB], [self.PB[2]])
            self.mm(self.ps(3, 128, T), self.onesb, sqb[:, kk, 0:T], c == 0, c == 7, [sqbB[kk], self.cB], [self.PB[3]])
        yield
        self.amul(mean[:, 0:T], self.ps(2, 128, T), 1.0 / D, [self.PB[2]], [stB])
        self.tt(rstd[:, 0:T], mean[:, 0:T], mean[:, 0:T], ALU.mult, [stB], [stB])
        self.stt(rstd[:, 0:T], self.ps(3, 128, T), 1.0 / D, rstd[:, 0:T], ALU.mult, ALU.subtract, [self.PB[3], stB], [stB])
        self.act(rstd[:, 0:T], rstd[:, 0:T], AF.Ln, [stB, self.cB], [stB], bias=self.vcol(V_EPS))
        self.act(rstd[:, 0:T], rstd[:, 0:T], AF.Exp, [stB], [stB], scale=-0.5)
        yield
        aTBs = [self.abuf("aT%d" % c) for c in range(8)]
        for c in range(8):
            self.tt(cc[:, c, 0:T], cc[:, c, 0:T], mean[:, 0:T], ALU.subtract, [ccB[c], stB], [ccB[c]])
            self.tt(cc[:, c, 0:T], cc[:, c, 0:T], rstd[:, 0:T], ALU.mult, [ccB[c], stB], [ccB[c]])
            self.act(aT[:, c, 0:T], cc[:, c, 0:T], AF.Silu, [ccB[c], self.cB], [aTBs[c]],
                     scale=self.vcol(V_CLNG + c), bias=self.vcol(V_CLNB + c))
            if c % 2 == 1:
                yield
        self.gpost_load(0 * 4 + 1)
        slot, slotB = self.wload(("cout",))
        yield
        yield from self.proj_out(u, aT, aTBs, slot, slotB, False, bias=True)

    def xattn(self, u, l):
        T = u.T
        self.new_phase()
        self.prenorm(u, l, 2)
        yield
        nseg = len(u.segs) if u.kind == "S" else 1
        o = 0
        qT = self.arena[:, o:o + 8 * T].rearrange("p (c t) -> p c t", c=8)
        o += 8 * T
        oT = self.arena[:, o:o + 8 * T].rearrange("p (c t) -> p c t", c=8)
        o += 8 * T
        KT = self.arena[:, o:o + nseg * 2048].rearrange("p (s c m) -> p s c m", s=nseg, c=8)
        o += nseg * 2048
        Vt = self.arena[:, o:o + nseg * 2048].rearrange("p (s c d) -> p s c d", s=nseg, c=2)
        o += nseg * 2048
        memT = self.arena[:, o:o + 2048].rearrange("p (c m) -> p c m", c=8)
        o += 2048
        mtm = self.arena[:, o:o + 4 * D].bitcast(F32).rearrange("p (c d) -> p c d", c=2)
        o += 4 * D
        mtb = self.arena[:, o:o + 2 * D].rearrange("p (c d) -> p c d", c=2)
        o += 2 * D
        pex = self.arena[:, o:o + 2048].bitcast(F32).rearrange("p (k m) -> p k m", k=2)
        o += 2048
        pn = self.arena[:, o:o + 1024].rearrange("p (k m) -> p k m", k=2)
        o += 1024
        pT = self.arena[:, o:o + 4 * 2 * T].rearrange("p (h c t) -> p h c t", h=4, c=2)
        o += 8 * T
        pexb_ = mtm[:, 0, :].rearrange("p (k m) -> p k m", k=2)
        pnb_ = mtb[:, 0, :].rearrange("p (k m) -> p k m", k=2)
        assert o <= self.c.an, o
        kout = mtm
        qTB, KTB, VtB, memTB = self.abuf("qT"), self.abuf("KT"), self.abuf("Vt"), self.abuf("memT")
        oTBs = [self.abuf("oT%d" % k) for k in range(8)]
        mtmB, mtbB = [self.abuf("mtm0"), self.abuf("mtm1")], [self.abuf("mtb0"), self.abuf("mtb1")]
        pexB, pnB = [self.abuf("pex0"), self.abuf("pex1")], [self.abuf("pn0"), self.abuf("pn1")]
        pTB = [self.abuf("pT%d" % h) for h in range(4)]
        koutB = mtmB
        if u.kind == "P" and not u.first:
            kb = self.kvsB[(l, u.seq)]
            self.load(KT[:, 0, :, :], self.kvs[l, u.seq, :, 0:2048].rearrange("p (c m) -> p c m", c=8), [KTB], reads=[kb])
            self.load(Vt[:, 0, :, :], self.kvs[l, u.seq, :, 2048:4096].rearrange("p (c d) -> p c d", c=2), [VtB], reads=[kb])
            yield
        elif u.kind == "P":
            for c2 in range(2):
                self.load(mtm[:, c2, :], self.memp[u.seq, c2 * 128:(c2 + 1) * 128, :], [mtmB[c2]])
                self.cp(mtb[:, c2, :], mtm[:, c2, :], [mtmB[c2]], [mtbB[c2]])
                pb = self.psb(2 + c2, 128, 1024)
                for c in range(8):
                    self.tr(pb[:, c * 128:(c + 1) * 128], mtb[:, c2, c * 128:(c + 1) * 128], self.identb,
                            [mtbB[c2], self.cB], [self.PB[2 + c2]])
                self.cp(memT[:, :, c2 * 128:(c2 + 1) * 128], pb.rearrange("p (c m) -> p c m", c=8),
                        [self.PB[2 + c2]], [memTB])
            slot, slotB = self.wload(("wkA", l))
            yield
            for m in range(8):
                bank = m % 2
                for k in range(8):
                    oo = (m * 8 + k) * 128
                    self.mm(self.ps(bank, 128, 256), slot[:, oo:oo + 128], memT[:, k, :], k == 0, k == 7,
                            [slotB, memTB], [self.PB[bank]])
                self.cp(KT[:, 0, m, :], self.ps(bank, 128, 256), [self.PB[bank]], [KTB], eng="act")
                if m % 2 == 1:
                    yield
            for nm, dst, keep in (("wkB", self.p_mk, False), ("wvB", self.p_mv, True)):
                if not (keep or u.first):
                    continue
                slot, slotB = self.wload((nm, l))
                yield
                for c2 in range(2):
                    banks = (2, 3)
                    for hf in range(2):
                        for k in range(8):
                            self.mm(self.ps(banks[hf]), memT[:, k, c2 * 128:(c2 + 1) * 128],
                                    slot[:, k * D + hf * 512:k * D + hf * 512 + 512], k == 0, k == 7,
                                    [memTB, slotB], [self.PB[banks[hf]]])
                        if u.first:
                            self.cp(kout[:, c2, hf * 512:(hf + 1) * 512], self.ps(banks[hf]), [self.PB[banks[hf]]],
                                    [koutB[c2]], eng="act")
                        if keep:
                            self.cp(Vt[:, 0, c2, hf * 512:(hf + 1) * 512], self.ps(banks[hf]), [self.PB[banks[hf]]], [VtB])
                    if u.first:
                        self.store(dst[l, u.seq, c2 * 128:(c2 + 1) * 128, :], kout[:, c2, :], [koutB[c2]])
                    yield
            kb = self.kvsB[(l, u.seq)]
            KTs, Vts = KT[:, 0, :, :], Vt[:, 0, :, :]
            d1 = self.kvs[l, u.seq, :, 0:2048].rearrange("p (c m) -> p c m", c=8)
            d2 = self.kvs[l, u.seq, :, 2048:4096].rearrange("p (c d) -> p c d", c=2)
            self.olane = (self.olane + 1) % 4
            self.S.dma("sp", "out%d_%d" % (self.c.idx, self.olane), lambda h: h.dma_start(out=d1, in_=KTs), [KTB], [kb])
            self.olane = (self.olane + 1) % 4
            self.S.dma("sp", "out%d_%d" % (self.c.idx, self.olane), lambda h: h.dma_start(out=d2, in_=Vts), [VtB, kb], [kb])
        else:
            for s in range(nseg):
                for c2 in range(2):
                    self.load(mtm[:, c2, :], self.cmk[l, u.sbase + s, c2 * 128:(c2 + 1) * 128, :], [mtmB[c2]])
                    self.cp(mtb[:, c2, :], mtm[:, c2, :], [mtmB[c2]], [mtbB[c2]])
                    pb = self.psb(2 + c2, 128, 1024)
                    for c in range(8):
                        self.tr(pb[:, c * 128:(c + 1) * 128], mtb[:, c2, c * 128:(c + 1) * 128], self.identb,
                                [mtbB[c2], self.cB], [self.PB[2 + c2]])
                    self.cp(KT[:, s, :, c2 * 128:(c2 + 1) * 128], pb.rearrange("p (c m) -> p c m", c=8),
                            [self.PB[2 + c2]], [KTB])
                for c2 in range(2):
                    self.load(mtm[:, c2, :], self.cmv[l, u.sbase + s, c2 * 128:(c2 + 1) * 128, :], [mtmB[c2]])
                    self.cp(Vt[:, s, c2, :], mtm[:, c2, :], [mtmB[c2]], [VtB])
                yield
        slot, slotB = self.wload(("wq", l))
        yield
        for m in range(8):
            bank = m % 2
            for k in range(8):
                oo = (m * 8 + k) * 128
                self.mm(self.ps(bank, 128, T), slot[:, oo:oo + 128], self.hT[:, k, 0:T], k == 0, k == 7,
                        [slotB, self.hTB], [self.PB[bank]])
            self.amul(qT[:, m, 0:T], self.ps(bank, 128, T), 1.0 / 16, [self.PB[bank]], [qTB])
            if m % 2 == 1:
                yield
        if u.kind == "P":
            groups = [(t0, nt, 0) for (t0, nt) in u.subs]
        else:
            groups = [(c0, Ls, s) for s, (c0, Ls, _) in enumerate(u.segs)]
        def attn_group(gi, t0, nt, kv, ba, bb):
            c = self.next_stat(12)
            sB = self.cur_statB
            st = self.stat
            bk = (ba, bb)
            PBk = (self.PB[ba], self.PB[bb])
            pexg, png = pex4[gi % 2], pn4[gi % 2]
            pexGB, pnGB = pex4B[gi % 2], pn4B[gi % 2]
            for hp in range(2):
                for hh in range(2):
                    h = hp * 2 + hh
                    for dc in range(2):
                        self.mm(self.ps(bk[hp], nt)[:, hh * 256:(hh + 1) * 256], qT[:, 2 * h + dc, t0:t0 + nt],
                                KT[:, kv, 2 * h + dc, :], dc == 0, dc == 1, [qTB, KTB], [PBk[hp]])
            yield
            for hp in range(2):
                sc3 = self.ps(bk[hp], nt).rearrange("p (h m) -> p h m", h=2)
                mxo = st[0:nt, c + hp * 2:c + hp * 2 + 2]
                self.S.op("dve", lambda hd, sc3=sc3, mxo=mxo: hd.tensor_reduce(
                    out=mxo, in_=sc3, axis=mybir.AxisListType.X, op=ALU.max, negate=True),
                    [PBk[hp]], [sB, PBk[hp]])
            yield
            for hp in range(2):
                for hh in range(2):
                    h = hp * 2 + hh
                    self.act(pexg[0:nt, hp, hh * 256:(hh + 1) * 256], self.ps(bk[hp], nt)[:, hh * 256:(hh + 1) * 256], AF.Exp,
                             [PBk[hp], sB], [pexGB[hp], sB], bias=st[0:nt, c + h:c + h + 1],
                             accum_out=st[0:nt, c + 4 + h:c + 5 + h])
            yield
            rco, rci = st[0:nt, c + 8:c + 12], st[0:nt, c + 4:c + 8]
            self.S.op("dve", lambda hd, rco=rco, rci=rci: hd.reciprocal(out=rco, in_=rci), [sB], [sB])
            for hp in range(2):
                self.tt(png[0:nt, hp, 0:512].rearrange("p (h m) -> p h m", h=2),
                        pexg[0:nt, hp, 0:512].rearrange("p (h m) -> p h m", h=2),
                        st[0:nt, c + 8 + hp * 2:c + 10 + hp * 2].unsqueeze(2).to_broadcast([nt, 2, 256]), ALU.mult,
                        [pexGB[hp], sB], [pnGB[hp]])
            yield
            for hp in range(2):
                pb = self.psb(bk[hp], 128, 4 * nt)
                for hh in range(2):
                    for mc in range(2):
                        self.tr(pb[:, (hh * 2 + mc) * nt:(hh * 2 + mc + 1) * nt],
                                png[0:nt, hp, hh * 256 + mc * 128:hh * 256 + mc * 128 + 128], self.cstb[0:nt, 0:nt],
                                [pnGB[hp], self.cB], [PBk[hp]])
            yield
            for hp in range(2):
                pb = self.psb(bk[hp], 128, 4 * nt)
                self.cp(pT[:, hp * 2:hp * 2 + 2, :, t0:t0 + nt],
                        pb.rearrange("p (h c t) -> p h c t", h=2, c=2), [PBk[hp]], [pTB[gi % 2][hp * 2], pTB[gi % 2][hp * 2 + 1]],
                        eng="act" if hp == 0 else "dve")
            yield
            for h in range(4):
                for dc in range(2):
                    x_ = (h * 2 + dc) % 2
                    for mc in range(2):
                        self.mm(self.ps(bk[x_], 128, nt), Vt[:, kv, mc, h * 256 + dc * 128:h * 256 + dc * 128 + 128],
                                pT[:, h, mc, t0:t0 + nt], mc == 0, mc == 1, [VtB, pTB[gi % 2][h]], [PBk[x_]])
                    self.cp(oT[:, 2 * h + dc, t0:t0 + nt], self.ps(bk[x_], 128, nt), [PBk[x_]], [oTBs[2 * h + dc]],
                            eng="act" if dc else "dve")
                if h % 2 == 1:
                    yield

        pex4 = [pex, pexb_]
        pn4 = [pn, pnb_]
        pex4B = [pexB, [mtmB[0], mtmB[0]]]
        pn4B = [pnB, [mtbB[0], mtbB[0]]]
        pTB = [pTB, [self.abuf("pTb%d" % h) for h in range(4)]]
        pendg = list(enumerate(groups))
        liveg = []
        freeb = [(0, 1), (2, 3)]
        while pendg or liveg:
            if pendg and freeb:
                gi, (t0, nt, kv) = pendg.pop(0)
                bks = freeb.pop(0)
                liveg.append((bks, attn_group(gi, t0, nt, kv, bks[0], bks[1])))
            nxt = []
            for bks, g_ in liveg:
                try:
                    next(g_)
                    nxt.append((bks, g_))
                except StopIteration:
                    freeb.append(bks)
            liveg = nxt
            yield
        self.gpost_load(l * 4 + 2)
        slot, slotB = self.wload(("wo", l))
        yield
        yield from self.proj_out(u, oT, oTBs, slot, slotB, False)

    def gdn_prep(self, u, n, Z, C, L, qkvT, qkvB, gb, gbB):
        HC = 8 * C
        B = Z["B"]
        ba, bb = Z["bm"]
        PA, PBk = self.PB[ba], self.PB[bb]

        def psa(p=128, nn=512):
            return self.ps(ba, p, nn)

        def psbk(p=128, nn=512):
            return self.ps(bb, p, nn)

        def h3(a):
            return a.rearrange("p (h c) -> p h c", h=8)
        gU, gU2, tA, Dm, DmT, EGb, cf = Z["gU"], Z["gU2"], Z["gU"], Z["Dm"], Z["DmT"], Z["EGb"], Z["cf"]
        M0, Em, Wt, Xt, Yt, QKm = Z["M0"], Z["Em"], Z["Wt"], Z["Xt"], Z["Yt"], Z["QKm"]
        KBe, Kdec, VB, nkc, qd = Z["KBe"], Z["Kdec"], Z["VB"], Z["nkc"], Z["qd"]
        tri_le = self.cst[0:C, 128:128 + C]
        tri_gt = self.cst[0:C, 256:256 + C]
        onesCC = self.cst[0:C, 384:384 + C]
        nlev = int(math.log2(C))
        cols = slice(n * C, (n + 1) * C)
        g_n = gb[0:C, n, 0:8]
        be_n = gb[0:C, n, 8:16]
        self.tt(gU[0:C, :, 0:C], tri_le.unsqueeze(1).to_broadcast([C, 8, C]), g_n.unsqueeze(2).to_broadcast([C, 8, C]),
                ALU.mult, [gbB, self.cB], [B["gU"]])
        yield
        for h in range(8):
            self.mm(psa(C, HC)[:, h * C:(h + 1) * C], gU[0:C, h, 0:C], tri_gt, True, True, [B["gU"], self.cB], [PA])
        gUf = gU2[0:C, 0:HC]
        self.mm(psbk(C, HC), tri_gt, gUf, True, True, [B["gU"], self.cB], [PBk])
        yield
        self.act(Dm[0:C, :, 0:C], h3(psa(C, HC)), AF.Exp, [PA], [B["Dm"]])
        self.act(DmT[0:C, :, 0:C], h3(psbk(C, HC)), AF.Exp, [PBk], [B["DmT"]])
        yield
        self.mm(psa(128, HC), self.cst[0:C, 384:512], gUf, True, True, [B["gU"], self.cB], [PA])
        self.mm(psbk(C, 16)[:, 0:8], tri_le, g_n, True, True, [gbB, self.cB], [PBk])
        self.mm(psbk(C, 16)[:, 8:16], onesCC, g_n, True, True, [gbB, self.cB], [PBk])
        yield
        self.cp(cf[0:C, 32:48], psbk(C, 16), [PBk], [B["cf"]])
        self.act(EGb[:, :, 0:C], h3(psa(128, HC)), AF.Exp, [PA], [B["EGb"]])
        yield
        for h in range(8):
            kT_h = qkvT[:, 8 + h, cols]
            self.mm(psa(C, HC)[:, h * C:(h + 1) * C], kT_h, kT_h, True, True, [qkvB[8 + h]], [PA])
            self.mm(psbk(C, HC)[:, h * C:(h + 1) * C], kT_h, qkvT[:, h, cols], True, True,
                    [qkvB[8 + h], qkvB[h]], [PBk])
        self.act(cf[0:C, 0:8], cf[0:C, 32:40], AF.Exp, [B["cf"]], [B["cf"]])
        self.tt(cf[0:C, 8:16], cf[0:C, 40:48], cf[0:C, 32:40], ALU.subtract, [B["cf"]], [B["cf"]])
        self.act(cf[0:C, 8:16], cf[0:C, 8:16], AF.Exp, [B["cf"]], [B["cf"]])
        self.tt(cf[0:C, 16:24], cf[0:C, 0:8], be_n, ALU.mult, [B["cf"], gbB], [B["cf"]])
        self.ts(cf[0:C, 24:32], be_n, -1.0, None, ALU.mult, None, [gbB], [B["cf"]])
        yield
        self.tt(tA[0:C, :, 0:C], h3(psa(C, HC)), Dm[0:C, :, 0:C], ALU.mult, [PA, B["Dm"]], [B["gU"]])
        self.tt(tA[0:C, :, 0:C], tA[0:C, :, 0:C], cf[0:C, 24:32].unsqueeze(2).to_broadcast([C, 8, C]), ALU.mult,
                [B["gU"], B["cf"]], [B["gU"]])
        self.tt(M0[0:C, :, 0:C], tA[0:C, :, 0:C], tri_gt.unsqueeze(1).to_broadcast([C, 8, C]), ALU.mult,
                [B["gU"], self.cB], [B["M0"]])
        yield
        self.tt(tA[0:C, :, 0:C], h3(psbk(C, HC)), DmT[0:C, :, 0:C], ALU.mult, [PBk, B["DmT"], B["gU"]], [B["gU"]])
        self.tt(QKm[0:C, :, 0:C], tA[0:C, :, 0:C], tri_le.unsqueeze(1).to_broadcast([C, 8, C]), ALU.mult,
                [B["gU"], self.cB], [B["QKm"]])
        mk0 = self.cst[0:C, 512:512 + C].unsqueeze(1).to_broadcast([C, 8, C])
        self.tt(Em[0:C, :, 0:C], M0[0:C, :, 0:C], mk0, ALU.mult, [B["M0"], self.cB], [B["Dm"]])
        yield
        pbN = self.psb(ba, C, HC)
        for h in range(8):
            self.tr(pbN[:, h * C:(h + 1) * C], Em[0:C, h, 0:C], self.cstb[0:C, 0:C], [B["Dm"], self.cB], [PA])
        pbK = self.psb(bb, C, 1024)
        for h in range(8):
            self.tr(pbK[:, h * 128:(h + 1) * 128], qkvT[:, 8 + h, cols], self.identb, [qkvB[8 + h], self.cB], [PBk])
        yield
        self.tt(Wt[0:C, :, 0:C], h3(pbN), self.cst[0:C, 0:C].unsqueeze(1).to_broadcast([C, 8, C]), ALU.add,
                [PA, self.cB], [B["W"]])
        pbK3 = pbK.rearrange("p (h d) -> p h d", h=8)
        self.tt(KBe[0:C], pbK3, cf[0:C, 16:24].unsqueeze(2).to_broadcast([C, 8, 128]), ALU.mult, [PBk, B["cf"]], [B["KBe"]])
        self.tt(Kdec[0:C], pbK3, cf[0:C, 8:16].unsqueeze(2).to_broadcast([C, 8, 128]), ALU.mult, [PBk, B["cf"]], [B["Kdec"]])
        self.tt(qd[:, :, 0:C], qkvT[:, 0:8, cols], EGb[:, :, 0:C], ALU.mult, qkvB[0:8] + [B["EGb"]], [B["qd"]])
        yield
        pbV = self.psb(bb, C, 1024)
        for h in range(8):
            self.tr(pbV[:, h * 128:(h + 1) * 128], qkvT[:, 16 + h, cols], self.identb, [qkvB[16 + h], self.cB], [PBk])
        yield
        pbV3 = pbV.rearrange("p (h d) -> p h d", h=8)
        self.tt(VB[0:C], pbV3, be_n.unsqueeze(2).to_broadcast([C, 8, 128]), ALU.mult, [PBk, gbB], [B["VB"]])
        for lv in range(1, nlev):
            mk = self.cst[0:C, 512 + lv * 64:512 + lv * 64 + C].unsqueeze(1).to_broadcast([C, 8, C])
            self.tt(Em[0:C, :, 0:C], M0[0:C, :, 0:C], mk, ALU.mult, [B["M0"], self.cB], [B["Dm"]])
            pbX = self.psb(ba, C, HC)
            for h in range(8):
                self.tr(pbX[:, h * C:(h + 1) * C], Wt[0:C, h, 0:C], self.cstb[0:C, 0:C], [B["W"], self.cB], [PA])
            yield
            self.cp(Xt[0:C, :, 0:C], h3(pbX), [PA], [B["DmT"]], eng="act")
            for h in range(8):
                self.mm(psbk(C, HC)[:, h * C:(h + 1) * C], Em[0:C, h, 0:C], Wt[0:C, h, 0:C], True, True,
                        [B["Dm"], B["W"]], [PBk])
            yield
            self.cp(Yt[0:C, :, 0:C], h3(psbk(C, HC)), [PBk], [B["gU"]], eng="act")
            yield
            for h in range(8):
                self.mm(psa(C, HC)[:, h * C:(h + 1) * C], Xt[0:C, h, 0:C], Yt[0:C, h, 0:C], True, True,
                        [B["DmT"], B["gU"]], [PA])
            yield
            self.tt(Wt[0:C, :, 0:C], Wt[0:C, :, 0:C], h3(psa(C, HC)), ALU.add, [B["W"], PA], [B["W"]])
            yield
        for h in range(8):
            self.mm(psbk(128, HC)[:, h * C:(h + 1) * C], KBe[0:C, h, :], Wt[0:C, h, 0:C], True, True,
                    [B["KBe"], B["W"]], [PBk])
        yield
        self.amul(nkc[:, :, 0:C], h3(psbk(128, HC)), -1.0, [PBk], [B["nkc"]])
        yield

    def gdn_scan(self, u, n, Z, C, L, oT, oTB):
        HC = 8 * C
        B = Z["B"]
        PB = self.PB

        def h3(a):
            return a.rearrange("p (h c) -> p h c", h=8)
        Wt, QKm, KBe, Kdec, VB, Ub, nkc, qd, EGb = (Z["Wt"], Z["QKm"], Z["KBe"], Z["Kdec"], Z["VB"], Z["Ub"], Z["nkc"],
                                                   Z["qd"], Z["EGb"])
        seg = (n * C) // L
        cols = slice(n * C, (n + 1) * C)
        first_chunk = (n * C) % L == 0 and (u.kind == "S" or u.first)
        Sf = self.Sf[:, 0, :, :]
        if first_chunk:
            if u.kind == "P":
                self.memset(Sf, 0.0, [self.SfB])
            else:
                self.load(Sf, self.sdn[u.sbase + seg].rearrange("h k v -> k h v"), [self.SfB])
            self.cp(self.Sb[:], Sf, [self.SfB], [self.SbB], eng="act")
        for h in range(8):
            bank = 1 + h // 4
            out = self.ps(bank, C)[:, (h % 4) * 128:(h % 4 + 1) * 128]
            self.mm(out, Wt[0:C, h, 0:C], VB[0:C, h, :], True, False, [B["W"], B["VB"]], [PB[bank]])
            self.mm(out, nkc[:, h, 0:C], self.Sb[:, h, :], False, True, [B["nkc"], self.SbB], [PB[bank]])
        yield
        for hf in range(2):
            self.cp(Ub[0:C, hf * 4:hf * 4 + 4, :], self.ps(1 + hf, C).rearrange("p (h d) -> p h d", h=4),
                    [PB[1 + hf]], [B["Ub"]], eng="act" if hf else "dve")
        yield
        for h in range(8):
            out = self.ps(2, 128, HC)[:, h * C:(h + 1) * C]
            self.mm(out, Ub[0:C, h, :], QKm[0:C, h, 0:C], True, False, [B["Ub"], B["QKm"]], [PB[2]])
            self.mm(out, self.Sb[:, h, :], qd[:, h, 0:C], False, True, [self.SbB, B["qd"]], [PB[2]])
        for h in range(8):
            bank = 0 if h < 4 else 3
            self.mm(self.ps(bank)[:, (h % 4) * 128:(h % 4 + 1) * 128], Kdec[0:C, h, :], Ub[0:C, h, :], True, True,
                    [B["Kdec"], B["Ub"]], [PB[bank]])
        yield
        self.cp(oT[:, :, cols], h3(self.ps(2, 128, HC)), [PB[2]], [oTB[n]], eng="act")
        self.tt(Sf, Sf, EGb[:, :, C - 1:C].to_broadcast([128, 8, 128]), ALU.mult, [self.SfB, B["EGb"]], [self.SfB])
        for hf in range(2):
            bank = 0 if hf == 0 else 3
            self.tt(Sf[:, hf * 4:hf * 4 + 4, :], Sf[:, hf * 4:hf * 4 + 4, :],
                    self.ps(bank).rearrange("p (h d) -> p h d", h=4), ALU.add, [self.SfB, PB[bank]], [self.SfB])
        self.cp(self.Sb[:], Sf, [self.SfB], [self.SbB], eng="act")
        last_chunk = ((n + 1) * C) % L == 0 and (u.kind == "S" or u.last)
        if last_chunk:
            dst = self.p_state[u.seq] if u.kind == "P" else self.s_state[u.sbase + seg]
            self.store(dst.rearrange("h k v -> k h v"), Sf, [self.SfB])
        yield

    def gdn(self, u):
        T = u.T
        self.new_phase()
        self.prenorm(u, 1, 1)
        yield
        nseg = len(u.segs)
        L = u.segs[0][1]
        C = min(64, L)
        nch = T // C
        o = 0
        qkvT = self.arena[:, o:o + 24 * T].rearrange("p (c t) -> p c t", c=24)
        o += 24 * T
        gT = self.arena[:, o:o + 8 * T].rearrange("p (c t) -> p c t", c=8)
        o += 8 * T
        oT = self.arena[:, o:o + 8 * T].rearrange("p (c t) -> p c t", c=8)
        o += 8 * T
        gb = self.arena[0:64, o:o + 2 * nch * 16].bitcast(F32).rearrange("p (n c) -> p n c", n=nch)
        o += 2 * nch * 16
        o_fixed = o
        RW = 3 + L
        raw = self.arena[:, o:o + 2 * 3 * nseg * RW].bitcast(F32).rearrange("p (k s w) -> p k s w", k=3, s=nseg)
        o += 6 * nseg * RW
        acc = self.arena[:, o:o + 6 * T].bitcast(F32).rearrange("p (k t) -> p k t", k=3)
        o += 6 * T
        sq = self.arena[:, o:o + 3 * T].rearrange("p (k t) -> p k t", k=3)
        o += 3 * T
        rs = self.arena[:, o:o + 6 * T].bitcast(F32).rearrange("p (k t) -> p k t", k=3)
        o += 6 * T
        tp = self.arena[0:64, o:o + 2 * nch * 8].bitcast(F32).rearrange("p (n c) -> p n c", n=nch)
        o += 2 * nch * 8
        tp2 = self.arena[0:64, o:o + 2 * nch * 8].bitcast(F32).rearrange("p (n c) -> p n c", n=nch)
        o += 2 * nch * 8
        tp3 = self.arena[0:64, o:o + 2 * nch * 8].bitcast(F32).rearrange("p (n c) -> p n c", n=nch)
        o += 2 * nch * 8
        ob = self.arena[0:3, o:o + 2 * D].bitcast(F32)
        o += 2 * D
        qkvB = [self.abuf("qkv%d" % m) for m in range(24)]
        gTB, oTB = self.abuf("gT"), [self.abuf("oT%d" % n) for n in range(nch)]
        rawB, accB = [self.abuf("raw%d" % i) for i in range(3)], [self.abuf("acc%d" % i) for i in range(3)]
        sqB, rsB = [self.abuf("sq%d" % i) for i in range(3)], [self.abuf("rs%d" % i) for i in range(3)]
        obB = self.abuf("dncout")
        gbB = self.abuf("gb")
        if u.kind == "S":
            nr = nseg * 3
            ctm = self.arena[0:nr, o:o + 2 * 3 * D].bitcast(F32)
            o += 2 * 3 * D
            ctmB = self.abuf("dctm")
            self.load(ctm, self.cdn[u.sbase * 3:u.sbase * 3 + nr, :], [ctmB])
            for m in range(24):
                bank = 2 + m % 2
                self.tr(self.ps(bank, 128, nr), ctm[:, m * 128:(m + 1) * 128], self.cst[0:nr, 0:nr],
                        [ctmB, self.cB], [self.PB[bank]])
                for s in range(nseg):
                    self.cp(self.dnh[:, s, m, :], self.ps(bank, 128, nr)[:, s * 3:s * 3 + 3],
                            [self.PB[bank]], [self.dnhB[s]])
        elif u.first:
            self.memset(self.dnh[:, 0, :, :], 0.0, [self.dnhB[0]])
        assert o <= self.c.an, o
        yield
        NBUF = 3
        free = list(range(NBUF))
        pend = [(ti, nm, mi) for ti, nm in enumerate(("dq", "dk", "dv", "dg")) for mi in range(8)]
        slots = {}
        live = []

        def proj_chunk(ti, nm, mi, kk):
            slot, slotB = slots[nm]
            m = ti * 8 + mi
            bank = m % 2
            for k in range(8):
                oo = (mi * 8 + k) * 128
                self.mm(self.ps(bank, 128, T), slot[:, oo:oo + 128], self.hT[:, k, 0:T], k == 0, k == 7,
                        [slotB, self.hTB], [self.PB[bank]])
            yield
            if nm == "dg":
                self.act(gT[:, mi, 0:T], self.ps(bank, 128, T), AF.Silu, [self.PB[bank]], [gTB])
                return
            for s_, (c0, Ls, sidx) in enumerate(u.segs):
                self.cp(raw[:, kk, s_, 0:3], self.dnh[:, sidx, m, :], [self.dnhB[sidx]], [rawB[kk]])
            self.cp(raw[:, kk, :, 3:3 + L], self.ps(bank, 128, T).rearrange("p (s w) -> p s w", s=nseg),
                    [self.PB[bank]], [rawB[kk]], eng="act")
            yield
            for s_, (c0, Ls, sidx) in enumerate(u.segs):
                self.cp(self.dnh[:, sidx, m, :], raw[:, kk, s_, L:L + 3], [rawB[kk]], [self.dnhB[sidx]])
            a3 = acc[:, kk, 0:T].rearrange("p (s w) -> p s w", s=nseg)
            self.ts(a3, raw[:, kk, :, 0:L], self.vcol(V_DNCW + m), None, ALU.mult, None, [rawB[kk], self.cB], [accB[kk]])
            for j in range(1, 4):
                self.stt(a3, raw[:, kk, :, j:j + L], self.vcol(V_DNCW + j * 24 + m), a3, ALU.mult, ALU.add,
                         [rawB[kk], accB[kk], self.cB], [accB[kk]])
            yield
            if nm == "dv":
                self.act(qkvT[:, m, 0:T], acc[:, kk, 0:T], AF.Silu, [accB[kk]], [qkvB[m]])
                return
            self.act(rs[:, kk, 0:T], acc[:, kk, 0:T], AF.Exp, [accB[kk]], [rsB[kk]], scale=-1.0)
            self.act(rs[:, kk, 0:T], rs[:, kk, 0:T], AF.Ln, [rsB[kk], self.cB], [rsB[kk]], bias=self.vcol(V_ONE))
            self.act(rs[:, kk, 0:T], rs[:, kk, 0:T], AF.Exp, [rsB[kk]], [rsB[kk]], scale=-1.0)
            yield
            self.tt(acc[:, kk, 0:T], acc[:, kk, 0:T], rs[:, kk, 0:T], ALU.mult, [accB[kk], rsB[kk]], [accB[kk]])
            yield
            self.act(sq[:, kk, 0:T], acc[:, kk, 0:T], AF.Square, [accB[kk]], [sqB[kk]])
            yield
            sbk = 2 + m % 2
            self.mm(self.ps(sbk, 128, T), self.onesb, sq[:, kk, 0:T], True, True, [sqB[kk], self.cB], [self.PB[sbk]])
            yield
            self.act(rs[:, kk, 0:T], self.ps(sbk, 128, T), AF.Ln, [self.PB[sbk], self.cB], [rsB[kk]],
                     bias=self.vcol(V_EPS))
            self.act(rs[:, kk, 0:T], rs[:, kk, 0:T], AF.Exp, [rsB[kk], self.cB], [rsB[kk]], scale=-0.5,
                     bias=self.vcol(V_LNQS if nm == "dq" else V_ZERO))
            yield
            self.tt(qkvT[:, m, 0:T], acc[:, kk, 0:T], rs[:, kk, 0:T], ALU.mult, [accB[kk], rsB[kk]], [qkvB[m]])

        while pend or live:
            if pend and free:
                ti, nm, mi = pend.pop(0)
                if nm not in slots:
                    slots[nm] = self.wload((nm,))
                    yield
                kk = free.pop(0)
                live.append((kk, proj_chunk(ti, nm, mi, kk)))
            nxt = []
            for kk, g_ in live:
                try:
                    next(g_)
                    nxt.append((kk, g_))
                except StopIteration:
                    free.append(kk)
            live = nxt
            yield
        for s, (c0, Ls, sidx) in enumerate(u.segs):
            if u.kind == "P" and not u.last:
                continue
            dst = self.p_dnc[u.seq] if u.kind == "P" else self.s_dnc[u.sbase + s]
            for part in range(3):
                for mm_ in range(8):
                    m = part * 8 + mm_
                    bank = 2 + m % 2
                    self.tr(self.ps(bank, 3, 128), self.dnh[:, sidx, m, :], self.identf, [self.dnhB[sidx], self.cB], [self.PB[bank]])
                    self.cp(ob[:, mm_ * 128:(mm_ + 1) * 128], self.ps(bank, 3, 128), [self.PB[bank]], [obB])
                self.store(dst[:, part * D:(part + 1) * D], ob, [obB])
            yield
        slot, slotB = self.wload(("dab",))
        yield
        for n in range(nch):
            for k in range(8):
                self.mm(self.ps(3, C, nch * 16)[:, n * 16:(n + 1) * 16], self.hT[:, k, n * C:(n + 1) * C],
                        slot[:, k * 16:(k + 1) * 16], k == 0, k == 7, [self.hTB, slotB], [self.PB[3]])
        ab3 = self.ps(3, C, nch * 16).rearrange("p (n c) -> p n c", n=nch)
        dtb = self.hsm[0:C, 8:16].unsqueeze(1).to_broadcast([C, nch, 8])
        self.tt(tp[0:C], ab3[:, :, 0:8], dtb, ALU.add, [self.PB[3], self.cB], [gbB])
        self.act(gb[0:C, :, 8:16], ab3[:, :, 8:16], AF.Sigmoid, [self.PB[3]], [gbB])
        yield
        self.ts(gb[0:C, :, 0:8], tp[0:C], -1.0, None, ALU.mult, None, [gbB], [gbB])
        self.tt(gb[0:C, :, 0:8], gb[0:C, :, 0:8], tp[0:C], ALU.max, [gbB], [gbB])
        self.act(gb[0:C, :, 0:8], gb[0:C, :, 0:8], AF.Exp, [gbB], [gbB], scale=-1.0)
        yield
        yv = gb[0:C, :, 0:8]
        pl = tp2[0:C]
        mk = tp3[0:C]
        self.ts(pl, yv, 0.2, -0.25, ALU.mult, ALU.add, [gbB], [gbB])
        for cst_ in (1.0 / 3, -0.5, 1.0):
            self.tt(pl, pl, yv, ALU.mult, [gbB], [gbB])
            self.ts(pl, pl, cst_, None, ALU.add, None, [gbB], [gbB])
        self.tt(pl, pl, yv, ALU.mult, [gbB], [gbB])
        self.ts(mk, yv, 0.0625, None, ALU.is_lt, None, [gbB], [gbB])
        self.act(yv, yv, AF.Ln, [gbB, self.cB], [gbB], bias=self.vcol(V_ONE, C))
        yield
        self.tt(pl, pl, yv, ALU.subtract, [gbB], [gbB])
        self.tt(pl, pl, mk, ALU.mult, [gbB], [gbB])
        self.tt(yv, yv, pl, ALU.add, [gbB], [gbB])
        self.ts(tp[0:C], tp[0:C], 0.0, None, ALU.max, None, [gbB], [gbB])
        self.tt(gb[0:C, :, 0:8], gb[0:C, :, 0:8], tp[0:C], ALU.add, [gbB], [gbB])
        self.tt(gb[0:C, :, 0:8], gb[0:C, :, 0:8], self.nega[0:C, :].unsqueeze(1).to_broadcast([C, nch, 8]), ALU.mult,
                [gbB, self.cB], [gbB])
        yield
        self.new_phase()
        HC = 8 * C
        c_ = u.c

        def h3(a):
            return a.rearrange("p (h c) -> p h c", h=8)

        def hd(a):
            return a.rearrange("p (h d) -> p h d", h=8)
        o = o_fixed

        def car(n, parts=128):
            nonlocal o
            a = self.arena[0:parts, o:o + n]
            o += n
            return a
        sets = []
        for which in range(2):
            Z = {}
            if which == 0:
                gUr, Dmr, DmTr = car(2 * HC, 64), car(HC, 64), car(HC, 64)
                EGr, cfr = car(2 * HC), car(128, 64)
                M0r, Wr, QKr = car(HC, 64), car(HC, 64), car(HC, 64)
                KBr, Kdr, VBr, Ubr = car(1024, 64), car(1024, 64), car(1024, 64), car(1024, 64)
                nkr, qdr = car(HC), car(HC)
            else:
                EGr, nkr, qdr, cfr = car(2 * HC), car(HC), car(HC), car(128, 64)
                hTf = c_.hT[:, :, :].rearrange("p c t -> p (c t)")
                xsf = c_.xs_sc[:, :, :].rearrange("p k d -> p (k d)")
                gpf = c_.gpost[:, :].bitcast(BF16)
                tmf = c_.tmpf[:, :].bitcast(BF16)
                KBr, Kdr = hTf[0:64, 0:1024], hTf[0:64, 1024:2048]
                VBr, Ubr = xsf[0:64, 0:1024], xsf[0:64, 1024:2048]
                gUr, Dmr, DmTr = gpf[0:64, 0:2 * HC], gpf[0:64, 1024:1024 + HC], gpf[0:64, 1536:1536 + HC]
                M0r, Wr = c_.junk[0:64, 0:HC], c_.junk[0:64, 512:512 + HC]
                QKr = tmf[0:64, 0:HC]
            Z["gU2"] = gUr.bitcast(F32)
            Z["gU"] = h3(Z["gU2"])
            Z["Yt"] = h3(gUr[:, 0:HC])
            Z["Dm"], Z["Em"] = h3(Dmr), h3(Dmr)
            Z["DmT"], Z["Xt"] = h3(DmTr), h3(DmTr)
            Z["EGb"] = h3(EGr.bitcast(F32))
            Z["cf"] = cfr.bitcast(F32)
            Z["M0"], Z["Wt"], Z["QKm"] = h3(M0r), h3(Wr), h3(QKr)
            Z["KBe"], Z["Kdec"], Z["VB"], Z["Ub"] = hd(KBr), hd(Kdr), hd(VBr), hd(Ubr)
            Z["nkc"], Z["qd"] = h3(nkr), h3(qdr)
            Z["B"] = {n_: self.abuf(n_ + str(which)) for n_ in ("gU", "Dm", "DmT", "EGb", "cf", "M0", "W", "QKm", "KBe", "Kdec",
                                                                "VB", "Ub", "nkc", "qd")}
            Z["bm"] = (0, 1) if which == 0 else (2, 3)
            sets.append(Z)
        assert o <= self.c.an, o
        PB = self.PB
        for n0 in range(0, nch, 2):
            pair = [n_ for n_ in (n0, n0 + 1) if n_ < nch]
            live = [self.gdn_prep(u, n_, sets[i], C, L, qkvT, qkvB, gb, gbB) for i, n_ in enumerate(pair)]
            while live:
                nxt = []
                for g_ in live:
                    try:
                        next(g_)
                        nxt.append(g_)
                    except StopIteration:
                        pass
                    yield
                live = nxt
            for i, n_ in enumerate(pair):
                yield from self.gdn_scan(u, n_, sets[i], C, L, oT, oTB)
        allB = [b_ for Z in sets[1:] for b_ in Z["B"].values()]
        self.memset(c_.hT[0:1, 0, 0:2], 0.0, [c_.hTB] + allB)
        self.memset(c_.xs_sc[0:1, 0, 0:2], 0.0, [c_.xs_scB[0], c_.xs_scB[1]] + allB)
        self.memset(c_.gpost[0:1, 0:2], 0.0, [c_.gpostB] + allB)
        self.memset(c_.junk[0:1, 0:2], 0.0, [c_.junkB] + allB)
        self.memset(c_.tmpf[0:1, 0:2], 0.0, [c_.tmpfB] + allB)
        yield
        self.new_phase()
        sqB, rsB, accB = [self.abuf("sq0"), self.abuf("sq1")], [self.abuf("rs0"), self.abuf("rs1")], [self.abuf("acc0"), self.abuf("acc1")]
        for h in range(8):
            kk = h % 2
            self.act(sq[:, kk, 0:T], oT[:, h, 0:T], AF.Square, oTB, [sqB[kk]])
            sbk = 2 + h % 2
            self.mm(self.ps(sbk, 128, T), self.onesb, sq[:, kk, 0:T], True, True, [sqB[kk], self.cB], [PB[sbk]])
            self.act(rs[:, kk, 0:T], self.ps(sbk, 128, T), AF.Ln, [PB[sbk], self.cB], [rsB[kk]], scale=1.0 / 128,
                     bias=self.vcol(V_EPS))
            self.act(rs[:, kk, 0:T], rs[:, kk, 0:T], AF.Exp, [rsB[kk]], [rsB[kk]], scale=-0.5)
            self.tt(acc[:, kk, 0:T], oT[:, h, 0:T], rs[:, kk, 0:T], ALU.mult, oTB + [rsB[kk]], [accB[kk]])
            self.stt(qkvT[:, h, 0:T], acc[:, kk, 0:T], self.vcol(V_DNG), gT[:, h, 0:T], ALU.mult, ALU.mult,
                     [accB[kk], gTB, self.cB], [qkvB[h]])
            if h % 2 == 1:
                yield
        self.gpost_load(1 * 4 + 1)
        slot, slotB = self.wload(("dout",))
        yield
        yield from self.proj_out(u, qkvT, qkvB[0:8], slot, slotB, False)


_CACHE = {}


def _program(cfg_key):
    if cfg_key not in _CACHE:
        _CACHE[cfg_key] = Builder(dict(cfg_key)).build()
    return _CACHE[cfg_key]


def run_cores(inp, n_cores, n_pseq, plen, n_sseq, stop=99):
    cfg = (("n_pseq", n_pseq), ("plen", plen), ("n_sseq", n_sseq), ("stop", stop))
    nc = _program(cfg)
    f = lambda a: np.ascontiguousarray(np.asarray(a, dtype=np.float32))
    wpack, offs = build_tiles({k: np.asarray(v, np.float32) for k, v in inp.items()})
    o2, _ = tile_offsets()
    assert offs == o2
    vec, bvec, hsm, cst = build_vecs({k: np.asarray(v, np.float32) for k, v in inp.items()})
    in_maps = []
    for c in range(n_cores):
        ps = slice(c * n_pseq, (c + 1) * n_pseq)
        m = {"xp": f(inp["x_prompt"][ps]), "memp": f(inp["mem_prompt"][ps]), "wpack": wpack, "vecs": vec,
             "bvec": bvec, "hsm": hsm, "cst": cst}
        if n_sseq:
            ss = slice(c * n_sseq, (c + 1) * n_sseq)
            m["xs"] = f(inp["x_sample"][ss]).reshape(n_sseq * 16, D)
            m["cconv"] = f(inp["cache_conv_a"][0, ss]).reshape(n_sseq * 30, D)
            m["sdn"] = f(inp["state_dn"][0, ss])
            m["cdn"] = f(inp["cache_dn_conv"][0, ss]).reshape(n_sseq * 3, 3 * D)
            m["cmk"] = f(inp["cache_mem_k"][:, ss]).reshape(2, n_sseq, NMEM, D)
            m["cmv"] = f(inp["cache_mem_v"][:, ss]).reshape(2, n_sseq, NMEM, D)
        in_maps.append(m)
    import os as _os
    if _os.environ.get("KTRACE"):
        res = run_bass_kernel_spmd(nc, in_maps, core_ids=list(range(n_cores)), trace=True)
        print("EXEC_NS", res.exec_time_ns)
    else:
        res = run_bass_kernel_spmd(nc, in_maps, core_ids=list(range(n_cores)))
    R = res.results
    cat = lambda k, ax=0: np.concatenate([r[k] for r in R], axis=ax)
    out = {}
    out["y_prompt"] = cat("yp")
    out["p_conv_a"] = cat("p_conv")[None]
    out["p_state_dn"] = cat("p_state")[None]
    out["p_dn_conv"] = cat("p_dnc")[None]
    out["p_mem_k"] = cat("p_mk", 1).reshape(2, -1, NMEM, 4, 256)
    out["p_mem_v"] = cat("p_mv", 1).reshape(2, -1, NMEM, 4, 256)
    if n_sseq:
        out["y_sample"] = cat("ys").reshape(-1, 16, D)
        out["s_conv_a"] = cat("s_conv")[None]
        out["s_state_dn"] = cat("s_state")[None]
        out["s_dn_conv"] = cat("s_dnc")[None]
    return out


def kernel(**inputs):
    o = run_cores(inputs, 8, 2, 2048, 4)
    return (o["y_prompt"], o["y_sample"], o["p_conv_a"], o["p_state_dn"], o["p_dn_conv"], o["p_mem_k"], o["p_mem_v"],
            o["s_conv_a"], o["s_state_dn"], o["s_dn_conv"])
```

```python
from contextlib import ExitStack
import math
import numpy as np
import concourse.bass as bass
import concourse.mybir as mybir
from concourse.bass_utils import run_bass_kernel_spmd

F32 = mybir.dt.float32
BF16 = mybir.dt.bfloat16
AF = mybir.ActivationFunctionType
ALU = mybir.AluOpType

D = 1024
DFF = 2816
NJ = 22
NMEM = 256
CONVW = 31
EPS = 1e-6
SEM_CHUNK = 30000
SLOT_COLS = 11264
NSLOT = 2
ARENA = 45056


class Buf:
    __slots__ = ("name", "w", "readers", "pre", "excl")

    def __init__(self, name, pre=(), excl=False):
        self.name = name
        self.excl = excl
        self.w = None
        self.readers = {}
        self.pre = pre


class Eng:
    def __init__(self, name):
        self.name = name
        self.ops = []
        self.count = 0
        self.waited = {}


class Lane:
    def __init__(self, key):
        self.key = key
        self.count = 0


class Sched:
    def __init__(self):
        self.eng = {n: Eng(n) for n in ("pe", "act", "dve", "pool", "sp")}
        self.sem_keys = []
        self.lanes = {}
        self.cur_stream = None
        self.fence_all = False
        self.last = {}

    def _prog_key(self, e, idx):
        k = ("prog", e.name, idx // SEM_CHUNK)
        if k not in self.sem_keys:
            self.sem_keys.append(k)
        return k

    def lane(self, name):
        if name not in self.lanes:
            k = ("lane", name)
            self.sem_keys.append(k)
            self.lanes[name] = Lane(k)
        return self.lanes[name]

    def _deps(self, e, reads, writes, extra):
        deps = {}

        def add(ev):
            if ev is None:
                return
            k, v = ev
            if e.name == "pe" and k[0] == "prog" and k[1] == "pe":
                return
            if deps.get(k, 0) < v:
                deps[k] = v

        for b in reads:
            add(b.w)
            for ev in b.pre:
                add(ev)
        for b in writes:
            add(b.w)
            for ev in b.pre:
                add(ev)
            for k, v in b.readers.items():
                add((k, v))
        for ev in extra:
            add(ev)
        out = []
        for k, v in deps.items():
            if e.waited.get(k, 0) < v:
                e.waited[k] = v
                out.append((k, v))
        return out

    def _commit(self, ev, reads, writes):
        for b in writes:
            b.w = ev
            b.readers = {}
        k, v = ev
        for b in reads:
            if b.readers.get(k, 0) < v:
                b.readers[k] = v

    def op(self, engine, fn, reads=(), writes=()):
        e = self.eng[engine]
        ex = [b for b in reads if b.excl]
        if ex:
            writes = list(writes) + ex
        waits = self._deps(e, reads, writes, ())
        idx = e.count
        e.count += 1
        k = self._prog_key(e, idx)
        ev = (k, idx % SEM_CHUNK + 1)
        self._commit(ev, reads, writes)
        e.ops.append((fn, waits, (k, 1)))
        self.last[(self.cur_stream, engine)] = ev
        return ev

    def dma(self, engine, lane_name, fn, reads=(), writes=()):
        e = self.eng[engine]
        ln = self.lane(lane_name)
        prev = (ln.key, ln.count) if ln.count else None
        waits = self._deps(e, reads, writes, (prev,))
        ln.count += 16
        ev = (ln.key, ln.count)
        self._commit(ev, reads, writes)
        e.ops.append((fn, waits, (ln.key, 16)))
        if not lane_name.startswith("w"):
            self.last[(self.cur_stream, lane_name)] = ev
        return ev

    def fence(self):
        return tuple(ev for (st, _), ev in self.last.items() if st == self.cur_stream or self.fence_all)

    def wait_all(self, engine, events):
        e = self.eng[engine]
        waits = self._deps(e, (), (), events)
        e.ops.append((None, waits, None))

    def check_deadlock(self):
        val = {}
        ptr = {n: 0 for n in self.eng}
        progress = True
        while progress:
            progress = False
            for n, e in self.eng.items():
                while ptr[n] < len(e.ops):
                    fn, waits, inc = e.ops[ptr[n]]
                    if any(val.get(k, 0) < v for k, v in waits):
                        break
                    if inc is not None:
                        val[inc[0]] = val.get(inc[0], 0) + inc[1]
                    ptr[n] += 1
                    progress = True
        stuck = {n: (ptr[n], len(e.ops)) for n, e in self.eng.items() if ptr[n] < len(e.ops)}
        if stuck:
            msg = []
            for n in stuck:
                fn, waits, inc = self.eng[n].ops[ptr[n]]
                msg.append("%s@%d waits %s" % (n, ptr[n], [(k, v, val.get(k, 0)) for k, v in waits if val.get(k, 0) < v]))
            raise RuntimeError("DEADLOCK: " + "; ".join(msg))

    def emit(self, nc, stack):
        self.check_deadlock()
        sems = {}
        for k in self.sem_keys:
            sems[k] = stack.enter_context(nc.semaphore("_".join(str(x) for x in k)))
        block = stack.enter_context(nc.Block())
        handles = {"pe": block.tensor, "act": block.scalar, "dve": block.vector,
                   "pool": block.gpsimd, "sp": block.sync}

        def make(e):
            def body(h):
                for fn, waits, inc in e.ops:
                    for k, v in waits:
                        h.wait_ge(sems[k], v)
                    if fn is None:
                        continue
                    fn(h).then_inc(sems[inc[0]], inc[1])
            return body

        for name, e in self.eng.items():
            if e.ops:
                handles[name](make(e))


V_NPRE = 0
V_CBIN = 64
V_CDW = 80
V_CDWB = 328
V_CLNG = 336
V_CLNB = 344
V_DNCW = 352
V_DNG = 448
V_EPS = 449
V_LNHALF = 450
V_ONE = 451
V_ZERO = 452
V_LNQS = 453
NV = 460
NCST = 512 + 6 * 64


def _modeA(W, m0, m1):
    K, N = W.shape
    a = W.reshape(K // 128, 128, N // 128, 128)[:, :, m0:m1, :]
    return np.ascontiguousarray(a.transpose(1, 2, 0, 3)).reshape(128, -1)


def _modeB(W, c0, c1):
    K, N = W.shape
    a = W.reshape(K // 128, 128, N)[:, :, c0:c1]
    return np.ascontiguousarray(a.transpose(1, 0, 2)).reshape(128, -1)


def build_tiles(inp):
    tiles = {}
    for l in range(2):
        for f in range(2):
            Wg, Wu, Wd = inp["ffn_w_gate"][l, f], inp["ffn_w_up"][l, f], inp["ffn_w_down"][l, f]
            ww = np.stack([Wg, Wu]).reshape(2, 8, 128, NJ, 128)
            ww = ww.transpose(2, 3, 0, 1, 4)
            for g in range(5):
                j0, j1 = 5 * g, min(NJ, 5 * g + 5)
                tiles[("U", l, f, g)] = np.ascontiguousarray(ww[:, j0:j1]).reshape(128, -1)
            for nh in range(2):
                tiles[("D", l, f, nh)] = _modeB(Wd, nh * 512, nh * 512 + 512)
        tiles[("wkA", l)] = _modeA(inp["xa_wk"][l], 0, 8)
        tiles[("wkB", l)] = _modeB(inp["xa_wk"][l], 0, D)
        tiles[("wvB", l)] = _modeB(inp["xa_wv"][l], 0, D)
        tiles[("wq", l)] = _modeA(inp["xa_wq"][l], 0, 8)
        tiles[("wo", l)] = _modeB(inp["xa_wo"][l], 0, D)
    win = inp["ca_w_in"][0]
    a = win.reshape(8, 128, 2, 8, 128)
    a = a.transpose(1, 3, 2, 0, 4)
    for g in range(2):
        tiles[("cin", g)] = np.ascontiguousarray(a[:, 4 * g:4 * g + 4]).reshape(128, -1)
    tiles[("cout",)] = _modeB(inp["ca_w_out"][0], 0, D)
    dw = inp["ca_dw"][0]
    ar = np.arange(128)
    for g in range(4):
        t = np.zeros((128, 2, CONVW, 128), np.float32)
        for ci in range(2):
            c = 2 * g + ci
            t[ar, ci, :, ar] = dw[:, c * 128:(c + 1) * 128].T
        tiles[("cdw", g)] = t.reshape(128, -1)
    dwin = inp["dn_w_in"][0]
    for i, nm in enumerate(("dq", "dk", "dv", "dg")):
        tiles[(nm,)] = _modeA(dwin[:, i * D:(i + 1) * D], 0, 8)
    tiles[("dab",)] = _modeB(dwin[:, 4 * D:4 * D + 16], 0, 16)
    tiles[("dout",)] = _modeB(inp["dn_w_out"][0], 0, D)
    offs = {}
    o = 0
    for k, v in tiles.items():
        assert v.shape[1] <= SLOT_COLS
        offs[k] = (o, v.shape[1])
        o += v.shape[1]
    wpack = np.concatenate([v.astype(np.float32) for v in tiles.values()], axis=1)
    return np.ascontiguousarray(wpack), offs


def tile_offsets():
    sizes = {}
    for l in range(2):
        for f in range(2):
            for g in range(5):
                sizes[("U", l, f, g)] = (min(NJ, 5 * g + 5) - 5 * g) * 2048
            for nh in range(2):
                sizes[("D", l, f, nh)] = NJ * 512
        for nm in ("wkA", "wkB", "wvB", "wq", "wo"):
            sizes[(nm, l)] = 8192
    for g in range(2):
        sizes[("cin", g)] = 8192
    sizes[("cout",)] = 8192
    for g in range(4):
        sizes[("cdw", g)] = 2 * CONVW * 128
    for nm in ("dq", "dk", "dv", "dg"):
        sizes[(nm,)] = 8192
    sizes[("dab",)] = 128
    sizes[("dout",)] = 8192
    offs = {}
    o = 0
    for k, n in sizes.items():
        offs[k] = (o, n)
        o += n
    return offs, o


def pcol(v):
    return np.ascontiguousarray(v.reshape(-1, 128).T)


def build_vecs(inp):
    vec = np.zeros((128, NV), np.float32)
    for l in range(2):
        for j in range(4):
            vec[:, V_NPRE + (l * 4 + j) * 8:V_NPRE + (l * 4 + j) * 8 + 8] = pcol(inp["norm_pre"][l, j])
    vec[:, V_CBIN:V_CBIN + 16] = pcol(inp["ca_b_in"][0])
    for j in range(CONVW):
        vec[:, V_CDW + j * 8:V_CDW + j * 8 + 8] = pcol(inp["ca_dw"][0, j])
    vec[:, V_CDWB:V_CDWB + 8] = pcol(inp["ca_dw_b"][0])
    vec[:, V_CLNG:V_CLNG + 8] = pcol(inp["ca_ln_g"][0])
    vec[:, V_CLNB:V_CLNB + 8] = pcol(inp["ca_ln_b"][0])
    for j in range(4):
        vec[:, V_DNCW + j * 24:V_DNCW + j * 24 + 24] = pcol(inp["dn_conv_w"][0, j])
    vec[:, V_DNG] = inp["dn_norm_g"][0]
    vec[:, V_EPS] = EPS
    vec[:, V_LNHALF] = math.log(0.5)
    vec[:, V_ONE] = 1.0
    vec[:, V_ZERO] = 0.0
    vec[:, V_LNQS] = math.log(128 ** -0.5)
    bvec = np.concatenate([inp["norm_post"].reshape(8, D), inp["ca_b_out"].reshape(1, D)], 0).astype(np.float32)
    hsm = np.concatenate([inp["dn_a_log"][0], inp["dn_dt_bias"][0]])[None, :].astype(np.float32)
    cst = np.zeros((128, NCST), np.float32)
    idx = np.arange(128)
    cst[:, 0:128] = np.eye(128)
    cst[:, 128:256] = (idx[:, None] <= idx[None, :])
    cst[:, 256:384] = (idx[:, None] > idx[None, :])
    cst[:, 384:512] = 1.0
    i64 = np.arange(64)
    for l in range(6):
        b = 2 ** l
        i, j = i64[:, None], i64[None, :]
        cst[0:64, 512 + l * 64:512 + (l + 1) * 64] = ((i // (2 * b) == j // (2 * b)) & ((i // b) % 2 == 1) & ((j // b) % 2 == 0))
    return vec, np.ascontiguousarray(bvec), hsm, cst


class Unit:
    pass


class Ctx:
    pass


class Builder:
    def __init__(self, cfg):
        self.cfg = cfg
        self.nc = bass.Bass("TRN2", target_bir_lowering=False)
        self.S = Sched()
        self.out_events = []
        self.cur_fence = ()
        self.olane = 0
        self.ilane = 0
        self._c = None

    @property
    def c(self):
        return self._c

    @c.setter
    def c(self, v):
        self._c = v
        self.S.cur_stream = v.idx

    arena = property(lambda s: s.c.arena)
    hT = property(lambda s: s.c.hT)
    hTB = property(lambda s: s.c.hTB)
    xs_sc = property(lambda s: s.c.xs_sc)
    xs_scB = property(lambda s: s.c.xs_scB)
    junk = property(lambda s: s.c.junk)
    junkB = property(lambda s: s.c.junkB)
    stat = property(lambda s: s.c.stat)
    tmpf = property(lambda s: s.c.tmpf)
    tmpfB = property(lambda s: s.c.tmpfB)
    gpost = property(lambda s: s.c.gpost)
    gpostB = property(lambda s: s.c.gpostB)
    Sf = property(lambda s: s.c.Sf)
    Sb = property(lambda s: s.c.Sb)
    SfB = property(lambda s: s.c.SfB)
    SbB = property(lambda s: s.c.SbB)
    convh = property(lambda s: s.c.convh)
    convhB = property(lambda s: s.c.convhB)
    dnh = property(lambda s: s.c.dnh)
    dnhB = property(lambda s: s.c.dnhB)
    PB = property(lambda s: s.c.PB)

    def abuf(self, name):
        return Buf(name, self.cur_fence)

    def new_phase(self):
        self.cur_fence = self.S.fence()

    def sb(self, name, shape, dt):
        return self.st.enter_context(self.nc.sbuf_tensor("sb_" + name, shape, dt))

    def act(self, out, in_, func, reads, writes, **kw):
        self.S.op("act", lambda h: h.activation(out=out, in_=in_, func=func, **kw), reads, writes)

    def amul(self, out, in_, mul, reads, writes):
        self.S.op("act", lambda h: h.mul(out=out, in_=in_, mul=mul), reads, writes)

    def memset(self, ap, val, writes):
        self.S.op("dve", lambda h: h.memset(ap, val), (), writes)

    def tt(self, out, in0, in1, op, reads, writes):
        self.S.op("dve", lambda h: h.tensor_tensor(out=out, in0=in0, in1=in1, op=op), reads, writes)

    def ts(self, out, in0, s1, s2, op0, op1, reads, writes):
        if s2 is None:
            self.S.op("dve", lambda h: h.tensor_scalar(out=out, in0=in0, scalar1=s1, scalar2=None, op0=op0), reads, writes)
        else:
            self.S.op("dve", lambda h: h.tensor_scalar(out=out, in0=in0, scalar1=s1, scalar2=s2, op0=op0, op1=op1), reads, writes)

    def stt(self, out, in0, scalar, in1, op0, op1, reads, writes):
        self.S.op("dve", lambda h: h.scalar_tensor_tensor(out=out, in0=in0, scalar=scalar, in1=in1, op0=op0, op1=op1), reads, writes)

    def cp(self, out, in_, reads, writes, eng="dve"):
        if eng == "dve":
            self.S.op("dve", lambda h: h.tensor_copy(out=out, in_=in_), reads, writes)
        else:
            self.S.op("act", lambda h: h.activation(out=out, in_=in_, func=AF.Identity), reads, writes)

    def mm(self, out, lhsT, rhs, start, stop, reads, writes):
        self.S.op("pe", lambda h: h.matmul(out, lhsT=lhsT, rhs=rhs, start=start, stop=stop), reads, writes)

    def tr(self, out, in_, ident, reads, writes):
        self.S.op("pe", lambda h: h.transpose(out=out, in_=in_, identity=ident), reads, writes)

    def load(self, out, in_, writes, reads=(), **kw):
        self.ilane = (self.ilane + 1) % 4
        return self.S.dma("sp", "in%d_%d" % (self.c.idx, self.ilane), lambda h: h.dma_start(out=out, in_=in_, **kw), reads, writes)

    def store(self, out, in_, reads, **kw):
        self.olane = (self.olane + 1) % 4
        ev = self.S.dma("sp", "out%d_%d" % (self.c.idx, self.olane), lambda h: h.dma_start(out=out, in_=in_, **kw), reads, [Buf("dram")])
        self.out_events.append(ev)
        return ev

    def ps(self, i, p=128, n=512):
        assert i < self.c.nbank
        i += self.c.bank0
        return self.pd[i // 2][0:p, (i % 2) * 512:(i % 2) * 512 + n]

    def psb(self, i, p=128, n=1024):
        assert i < self.c.nbank
        i += self.c.bank0
        return self.pd[i // 2][0:p, (i % 2) * 512:(i % 2) * 512 + 512].bitcast(BF16)[:, 0:n]

    def vcol(self, c, p=128, n=1):
        return self.vecs[0:p, c:c + n]

    def next_stat(self, n=1):
        assert n <= 16
        r = self.c.statn % 8
        self.c.statn += 1
        self.cur_statB = self.c.statB[r]
        return r * 16

    def wload(self, key):
        if key in self.wcache:
            ent = self.wcache[key]
            ent[2] += 1
            return ent[0], ent[1]
        off, n = self.woffs[key]
        s = self.wcount % NSLOT
        self.wcount += 1
        prev = self.slot_owner[s]
        if prev is not None:
            assert self.wcache[prev][2] == self.nstreams, ("ring slot reused before all streams read it", prev, key)
        slot = self.ring[s]
        buf = self.ringB[s]
        src = self.wpack[:, off:off + n]
        self.S.dma("pool", "w%d" % s,
                   lambda h: h.dma_start(out=slot[:, 0:n], in_=src, max_dma_last_dim=8192),
                   (), [buf])
        self.wcache[key] = [slot, buf, 1]
        self.slot_owner[s] = key
        return slot, buf

    def make_ctx(self, idx, a0, an, bank0, nbank, nseg):
        sb = self.sb
        c = Ctx()
        c.idx = idx
        c.arena = self.arena_all[:, a0:a0 + an]
        c.an = an
        c.bank0, c.nbank = bank0, nbank
        c.PB = self.PBall[bank0:bank0 + nbank]
        n = "c%d" % idx
        c.x = sb("x" + n, [128, 2, D], F32)
        c.xB = Buf("x" + n)
        c.hT = sb("hT" + n, [128, 8, 256], BF16)
        c.hTB = Buf("hT" + n)
        c.xs_sc = sb("xs" + n, [128, 2, D], BF16)
        c.xs_scB = [Buf("xs0" + n), Buf("xs1" + n)]
        c.junk = sb("junk" + n, [128, D], BF16)
        c.junkB = Buf("junk" + n)
        c.stat = sb("stat" + n, [128, 128], F32)
        c.statB = [Buf("stat%d%s" % (i, n)) for i in range(8)]
        c.statn = 0
        c.tmpf = sb("tmpf" + n, [128, 512], F32)
        c.tmpfB = Buf("tmpf" + n)
        c.gpost = sb("gpost" + n, [128, D], F32)
        c.gpostB = Buf("gpost" + n)
        c.convh = sb("convh" + n, [128, 1, 8, 30], F32)
        c.convhB = [Buf("convh" + n)]
        c.dnh = sb("dnh" + n, [128, nseg, 24, 3], F32)
        c.dnhB = [Buf("dnh%d%s" % (i, n)) for i in range(nseg)]
        c.Sf = sb("Sf" + n, [128, 1, 8, 128], F32)
        c.Sb = sb("Sb" + n, [128, 8, 128], BF16)
        c.SfB = Buf("Sf" + n)
        c.SbB = Buf("Sb" + n)
        return c

    def build(self):
        cfg = self.cfg
        nc = self.nc
        NP, PL, NS = cfg["n_pseq"], cfg["plen"], cfg["n_sseq"]
        self.woffs, wtot = tile_offsets()
        dt = nc.dram_tensor

        def din(name, shape):
            return dt(name, shape, F32, kind="ExternalInput").ap()

        def dout(name, shape):
            return dt(name, shape, F32, kind="ExternalOutput").ap()

        self.xp = din("xp", [NP, PL, D])
        self.memp = din("memp", [NP, NMEM, D])
        self.wpack = din("wpack", [128, wtot])
        self.vecs_d = din("vecs", [128, NV])
        self.bvec_d = din("bvec", [9, D])
        self.hsm_d = din("hsm", [1, 16])
        self.cst_d = din("cst", [128, NCST])
        self.yp = dout("yp", [NP, PL, D])
        self.p_conv = dout("p_conv", [NP, 30, D])
        self.p_state = dout("p_state", [NP, 8, 128, 128])
        self.p_dnc = dout("p_dnc", [NP, 3, 3 * D])
        self.p_mk = dout("p_mk", [2, NP, NMEM, D])
        self.p_mv = dout("p_mv", [2, NP, NMEM, D])
        self.kvs = dt("kvs", [2, NP, 128, 4096], BF16, kind="Internal").ap()
        self.kvsB = {(l, q): Buf("kvs%d_%d" % (l, q)) for l in range(2) for q in range(NP)}
        if NS:
            self.xs = din("xs", [NS * 16, D])
            self.cconv = din("cconv", [NS * 30, D])
            self.sdn = din("sdn", [NS, 8, 128, 128])
            self.cdn = din("cdn", [NS * 3, 3 * D])
            self.cmk = din("cmk", [2, NS, NMEM, D])
            self.cmv = din("cmv", [2, NS, NMEM, D])
            self.ys = dout("ys", [NS * 16, D])
            self.s_conv = dout("s_conv", [NS, 30, D])
            self.s_state = dout("s_state", [NS, 8, 128, 128])
            self.s_dnc = dout("s_dnc", [NS, 3, 3 * D])

        with ExitStack() as st:
            self.st = st
            sb = self.sb
            self.pd = [st.enter_context(nc.psum_tensor("pd%d" % i, [128, 1024], F32)) for i in range(4)]
            self.PBall = [Buf("ps%d" % i, excl=True) for i in range(8)]
            self.vecs = sb("vecs", [128, NV], F32)
            self.cst = sb("cstf", [128, NCST], F32)
            self.cstb = sb("cstb", [128, NCST], BF16)
            self.hsm = sb("hsm", [64, 16], F32)
            self.nega = sb("nega", [64, 8], F32)
            self.boutb = sb("boutb", [1, D], BF16)
            self.ring = [sb("ring%d" % i, [128, SLOT_COLS], BF16) for i in range(NSLOT)]
            self.ringB = [Buf("ring%d" % i) for i in range(NSLOT)]
            self.wcount = 0
            self.wcache = {}
            self.slot_owner = [None] * NSLOT
            self.arena_all = sb("arena", [128, ARENA], BF16)
            cB = self.cB = Buf("consts")
            half = ARENA // 2
            ctxs = [self.make_ctx(0, 0, half, 0, 4, max(NS // 2, 1)), self.make_ctx(1, half, half, 4, 4, max(NS // 2, 1))]
            self.c = ctxs[0]

            self.load(self.vecs[:], self.vecs_d, [cB])
            self.load(self.cst[:], self.cst_d, [cB])
            self.load(self.hsm[:], self.hsm_d[0].partition_broadcast(64), [cB])
            boutf = ctxs[0].tmpf[0:1, :]
            for hf in range(2):
                self.load(boutf, self.bvec_d[8:9, hf * 512:(hf + 1) * 512], [ctxs[0].tmpfB])
                self.cp(self.boutb[:, hf * 512:(hf + 1) * 512], boutf, [ctxs[0].tmpfB], [cB])
            self.cp(self.cstb[:], self.cst[:], [cB], [cB])
            self.act(self.nega[:], self.hsm[:, 0:8], AF.Exp, [cB], [cB])
            self.ts(self.nega[:], self.nega[:], -1.0, None, ALU.mult, None, [cB], [cB])
            self.identb = self.cstb[:, 0:128]
            self.identf = self.cst[:, 0:128]
            self.onesb = self.cstb[:, 384:512]

            stop = cfg.get("stop", 99)
            TU = 256
            npass = PL // TU
            for s0 in range(0, NP, 2):
                seqs = list(range(s0, min(NP, s0 + 2)))
                for j in range(npass):
                    units = []
                    for si, s in enumerate(seqs):
                        u = Unit()
                        u.kind, u.seq, u.pos, u.T, u.first, u.last = "P", s, j, TU, j == 0, j == npass - 1
                        u.subs = [(i * 128, 128) for i in range(TU // 128)]
                        u.segs = [(0, TU, 0)]
                        u.c = ctxs[si]
                        units.append(u)
                    self.run_pass(units, stop)
            if NS:
                units = []
                nsp = 2 if NS % 2 == 0 else 1
                per = NS // nsp
                for si in range(nsp):
                    u = Unit()
                    u.kind, u.seq, u.pos, u.T, u.first, u.last = "S", 0, 0, per * 16, True, True
                    u.sbase = si * per
                    u.subs = [(0, per * 16)]
                    u.segs = [(i * 16, 16, i) for i in range(per)]
                    u.c = ctxs[si]
                    units.append(u)
                self.run_pass(units, stop)
            self.S.wait_all("sp", self.out_events)
            self.S.emit(nc, st)
        return nc

    def run_pass(self, units, stop):
        self.nstreams = len(units)
        self.wcache = {}
        self.slot_owner = [None] * NSLOT
        for u in units:
            self.c = u.c
            self.load_x(u)
        phases = [lambda u: self.ffn(u, 0, 0), lambda u: self.conf(u), lambda u: self.xattn(u, 0), lambda u: self.ffn(u, 0, 1),
                  lambda u: self.ffn(u, 1, 0), lambda u: self.gdn(u), lambda u: self.xattn(u, 1), lambda u: self.ffn(u, 1, 1)]
        for i, ph in enumerate(phases):
            if i >= stop:
                break
            gens = []
            for u in units:
                self.c = u.c
                gens.append((u, ph(u)))
            live = list(gens)
            while live:
                nxt = []
                for u, g in live:
                    self.c = u.c
                    try:
                        next(g)
                        nxt.append((u, g))
                    except StopIteration:
                        pass
                live = nxt
        for u in units:
            self.c = u.c
            self.store_x(u)

    def load_x(self, u):
        c = u.c
        if u.kind == "P":
            src = self.xp[u.seq, u.pos * u.T:(u.pos + 1) * u.T, :].rearrange("(s p) d -> p s d", p=128)
            self.load(c.x[:, :, :], src, [c.xB])
        else:
            self.load(c.x[0:u.T, 0, :], self.xs[u.sbase * 16:u.sbase * 16 + u.T, :], [c.xB])

    def store_x(self, u):
        c = u.c
        if u.kind == "P":
            dst = self.yp[u.seq, u.pos * u.T:(u.pos + 1) * u.T, :].rearrange("(s p) d -> p s d", p=128)
            self.store(dst, c.x[:, :, :], [c.xB])
        else:
            self.store(self.ys[u.sbase * 16:u.sbase * 16 + u.T, :], c.x[0:u.T, 0, :], [c.xB])

    def prenorm(self, u, l, j):
        c_ = u.c
        g0 = V_NPRE + (l * 4 + j) * 8
        st = self.stat
        cs = []
        for si, (t0, nt) in enumerate(u.subs):
            c = self.next_stat(3)
            cs.append((c, self.cur_statB))
        for si, (t0, nt) in enumerate(u.subs):
            c, sB = cs[si]
            self.act(self.junk[0:nt, :], c_.x[0:nt, si, :], AF.Square, [c_.xB], [self.junkB, sB], accum_out=st[0:nt, c:c + 1])
        for si, (t0, nt) in enumerate(u.subs):
            c, sB = cs[si]
            self.act(st[0:nt, c + 1:c + 2], st[0:nt, c:c + 1], AF.Ln, [sB, self.cB], [sB],
                     scale=1.0 / D, bias=self.vcol(V_EPS, nt))
        for si, (t0, nt) in enumerate(u.subs):
            c, sB = cs[si]
            self.act(st[0:nt, c + 2:c + 3], st[0:nt, c + 1:c + 2], AF.Exp, [sB], [sB], scale=-0.5)
        for si, (t0, nt) in enumerate(u.subs):
            c, sB = cs[si]
            k = si % 2
            self.ts(self.xs_sc[0:nt, k, :], c_.x[0:nt, si, :], st[0:nt, c + 2:c + 3], None, ALU.mult, None,
                    [c_.xB, sB], [self.xs_scB[k]])
        for si, (t0, nt) in enumerate(u.subs):
            k = si % 2
            bank = 2 + si % 2
            pb = self.psb(bank, 128, 8 * nt)
            for cc in range(8):
                self.tr(pb[:, cc * nt:(cc + 1) * nt], self.xs_sc[0:nt, k, cc * 128:(cc + 1) * 128],
                        self.cstb[0:nt, 0:nt], [self.xs_scB[k], self.cB], [self.PB[bank]])
        for si, (t0, nt) in enumerate(u.subs):
            bank = 2 + si % 2
            pb = self.psb(bank, 128, 8 * nt)
            gbc = self.vecs[:, g0:g0 + 8].unsqueeze(2).to_broadcast([128, 8, nt])
            self.tt(self.hT[:, :, t0:t0 + nt], pb.rearrange("p (c t) -> p c t", c=8), gbc, ALU.mult,
                    [self.PB[bank], self.cB], [self.hTB])

    def gpost_load(self, row):
        self.load(self.gpost[:, :], self.bvec_d[row].partition_broadcast(128), [self.gpostB])

    def postnorm(self, u, si, nt, banks, half_scale):
        c_ = u.c
        c = self.next_stat(5)
        sB = self.cur_statB
        st = self.stat
        for hf in range(2):
            self.act(self.junk[0:nt, 0:512], self.ps(banks[hf], nt), AF.Square, [self.PB[banks[hf]]],
                     [self.junkB, sB], accum_out=st[0:nt, c + hf:c + hf + 1])
        self.tt(st[0:nt, c + 2:c + 3], st[0:nt, c:c + 1], st[0:nt, c + 1:c + 2], ALU.add, [sB], [sB])
        self.act(st[0:nt, c + 3:c + 4], st[0:nt, c + 2:c + 3], AF.Ln, [sB, self.cB], [sB],
                 scale=1.0 / D, bias=self.vcol(V_EPS, nt))
        self.act(st[0:nt, c + 4:c + 5], st[0:nt, c + 3:c + 4], AF.Exp, [sB, self.cB], [sB], scale=-0.5,
                 bias=self.vcol(V_LNHALF if half_scale else V_ZERO, nt))
        for hf in range(2):
            tmp = self.tmpf[0:nt, :]
            self.tt(tmp, self.ps(banks[hf], nt), self.gpost[0:nt, hf * 512:(hf + 1) * 512], ALU.mult,
                    [self.PB[banks[hf]], self.gpostB], [self.tmpfB])
            xo = c_.x[0:nt, si, hf * 512:(hf + 1) * 512]
            self.stt(xo, tmp, st[0:nt, c + 4:c + 5], xo, ALU.mult, ALU.add, [self.tmpfB, sB, c_.xB], [c_.xB])

    def proj_out(self, u, actT, actBs, slot, slotB, half_scale, bias=False):
        for si, (t0, nt) in enumerate(u.subs):
            banks = (0, 1) if si % 2 == 0 else (2, 3)
            for hf in range(2):
                for k in range(8):
                    self.mm(self.ps(banks[hf], nt), actT[:, k, t0:t0 + nt],
                            slot[:, k * D + hf * 512:k * D + hf * 512 + 512], k == 0, (k == 7) and not bias,
                            [actBs[k], slotB], [self.PB[banks[hf]]])
                if bias:
                    self.mm(self.ps(banks[hf], nt), self.cstb[0:1, 384:384 + nt],
                            self.boutb[0:1, hf * 512:hf * 512 + 512], False, True,
                            [self.cB], [self.PB[banks[hf]]])
            self.postnorm(u, si, nt, banks, half_scale)
            yield

    def ffn(self, u, l, f):
        T = u.T
        self.new_phase()
        self.prenorm(u, l, 0 if f == 0 else 3)
        yield
        hid = self.arena[:, 0:NJ * T].rearrange("p (j t) -> p j t", j=NJ)
        hidB = [self.abuf("hid%d" % j) for j in range(NJ)]
        sg = self.arena[:, NJ * T:NJ * T + 4 * T].bitcast(F32).rearrange("p (k t) -> p k t", k=2)
        sgB = [self.abuf("sg0"), self.abuf("sg1")]
        for g in range(5):
            slot, slotB = self.wload(("U", l, f, g))
            yield
            for jj in range(min(NJ, 5 * g + 5) - 5 * g):
                j = 5 * g + jj
                bg, bu = (0, 1) if j % 2 == 0 else (2, 3)
                for w, bank in ((0, bg), (1, bu)):
                    for k in range(8):
                        o = ((jj * 2 + w) * 8 + k) * 128
                        self.mm(self.ps(bank, 128, T), slot[:, o:o + 128], self.hT[:, k, 0:T], k == 0, k == 7,
                                [slotB, self.hTB], [self.PB[bank]])
                kk = j % 2
                self.act(sg[:, kk, 0:T], self.ps(bg, 128, T), AF.Silu, [self.PB[bg]], [sgB[kk]])
                self.tt(hid[:, j, 0:T], sg[:, kk, 0:T], self.ps(bu, 128, T), ALU.mult,
                        [sgB[kk], self.PB[bu]], [hidB[j]])
                yield
        self.gpost_load(l * 4 + (0 if f == 0 else 3))
        slot0, slot0B = self.wload(("D", l, f, 0))
        yield
        for si, (t0, nt) in enumerate(u.subs):
            for j in range(NJ):
                self.mm(self.ps(si, nt), hid[:, j, t0:t0 + nt], slot0[:, j * 512:(j + 1) * 512], j == 0, j == NJ - 1,
                        [hidB[j], slot0B], [self.PB[si]])
            yield
        slot1, slot1B = self.wload(("D", l, f, 1))
        yield
        for si, (t0, nt) in enumerate(u.subs):
            b1 = 2 + si
            for j in range(NJ):
                self.mm(self.ps(b1, nt), hid[:, j, t0:t0 + nt], slot1[:, j * 512:(j + 1) * 512], j == 0, j == NJ - 1,
                        [hidB[j], slot1B], [self.PB[b1]])
            self.postnorm(u, si, nt, (si, b1), True)
            yield

    def conf(self, u):
        T = u.T
        self.new_phase()
        self.prenorm(u, 0, 1)
        yield
        nseg = len(u.segs)
        L = u.segs[0][1]
        HW = 30 + L
        o = 0
        histb = self.arena[:, o:o + 8 * nseg * HW].rearrange("p (c s w) -> p c s w", c=8, s=nseg)
        o += 8 * nseg * HW
        tail = self.arena[:, o:o + 2 * 8 * nseg * 30].bitcast(F32).rearrange("p (c s w) -> p c s w", c=8, s=nseg)
        o += 2 * 8 * nseg * 30
        cc = self.arena[:, o:o + 2 * 8 * T].bitcast(F32).rearrange("p (c t) -> p c t", c=8)
        o += 2 * 8 * T
        aT = self.arena[:, o:o + 8 * T].rearrange("p (c t) -> p c t", c=8)
        o += 8 * T
        sig = self.arena[:, o:o + 4 * T].bitcast(F32).rearrange("p (k t) -> p k t", k=2)
        o += 4 * T
        sqb = self.arena[:, o:o + 2 * T].rearrange("p (k t) -> p k t", k=2)
        o += 2 * T
        ccb = self.arena[:, o:o + 2 * T].rearrange("p (k t) -> p k t", k=2)
        o += 2 * T
        mean = self.arena[:, o:o + 2 * T].bitcast(F32)
        o += 2 * T
        rstd = self.arena[:, o:o + 2 * T].bitcast(F32)
        o += 2 * T
        ob = self.arena[0:30, o:o + 2 * D].bitcast(F32)
        obB = self.abuf("convout")
        o += 2 * D
        histB = [self.abuf("hist%d" % c) for c in range(8)]
        tailB = [self.abuf("tail%d" % c) for c in range(8)]
        ccB = [self.abuf("cc%d" % c) for c in range(8)]
        sigB = [self.abuf("sig0"), self.abuf("sig1")]
        sqbB = [self.abuf("sqb0"), self.abuf("sqb1")]
        ccbB = [self.abuf("ccb0"), self.abuf("ccb1")]
        stB = self.abuf("lnstat")
        KEEP = 30 - min(L, 30)
        if u.kind == "P":
            if u.first:
                self.memset(histb[:, :, 0, 0:30], 0.0, histB)
            else:
                self.cp(histb[:, :, 0, 0:30], self.convh[:, 0, :, :], [self.convhB[0]], histB)
        else:
            nr = nseg * 30
            ctm = self.arena[0:nr, o:o + 2 * D].bitcast(F32)
            o += 2 * D
            ctmB = self.abuf("ctm")
            self.load(ctm, self.cconv[u.sbase * 30:u.sbase * 30 + nr, :], [ctmB])
            for c in range(8):
                bank = c % 2
                self.tr(self.ps(bank, 128, nr), ctm[:, c * 128:(c + 1) * 128], self.cst[0:nr, 0:nr],
                        [ctmB, self.cB], [self.PB[bank]])
                p3 = self.ps(bank, 128, nr).rearrange("p (s w) -> p s w", s=nseg)
                self.cp(histb[:, c, :, 0:30], p3, [self.PB[bank]], [histB[c]])
                self.cp(tail[:, c, :, 0:KEEP], p3[:, :, 30 - KEEP:30], [self.PB[bank]], [tailB[c]], eng="act")
        assert o <= self.c.an, o
        yield
        NEW = min(L, 30)
        for g in range(2):
            slot, slotB = self.wload(("cin", g))
            yield
            for ci in range(4):
                c = 4 * g + ci
                bv, bg = (0, 1) if c % 2 == 0 else (2, 3)
                for w, bank in ((0, bv), (1, bg)):
                    for k in range(8):
                        oo = ((ci * 2 + w) * 8 + k) * 128
                        self.mm(self.ps(bank, 128, T), slot[:, oo:oo + 128], self.hT[:, k, 0:T], k == 0, k == 7,
                                [slotB, self.hTB], [self.PB[bank]])
                kk = c % 2
                self.act(sig[:, kk, 0:T], self.ps(bg, 128, T), AF.Sigmoid, [self.PB[bg], self.cB], [sigB[kk]],
                         bias=self.vcol(V_CBIN + 8 + c))
                pv3 = self.ps(bv, 128, T).rearrange("p (s w) -> p s w", s=nseg)
                sg3 = sig[:, kk, 0:T].rearrange("p (s w) -> p s w", s=nseg)
                self.stt(histb[:, c, :, 30:30 + L], pv3, self.vcol(V_CBIN + c), sg3,
                         ALU.add, ALU.mult, [self.PB[bv], sigB[kk], self.cB], [histB[c]])
                self.stt(tail[:, c, :, KEEP:30], pv3[:, :, L - NEW:L], self.vcol(V_CBIN + c), sg3[:, :, L - NEW:L],
                         ALU.add, ALU.mult, [self.PB[bv], sigB[kk], self.cB], [tailB[c]])
                yield
        for s, (c0, Ls, sidx) in enumerate(u.segs):
            if u.kind == "P" and not u.last:
                self.cp(self.convh[:, 0, :, :], tail[:, :, s, :], tailB, [self.convhB[0]])
            else:
                dst = self.p_conv[u.seq] if u.kind == "P" else self.s_conv[u.sbase + s]
                for c in range(8):
                    bank = 2 + c % 2
                    self.tr(self.ps(bank, 30, 128), tail[:, c, s, :], self.identf,
                            [tailB[c], self.cB], [self.PB[bank]])
                    self.cp(ob[:, c * 128:(c + 1) * 128], self.ps(bank, 30, 128), [self.PB[bank]], [obB])
                self.store(dst, ob, [obB])
            yield
        for g in range(4):
            slot, slotB = self.wload(("cdw", g))
            yield
            for ci in range(2):
                c = 2 * g + ci
                bank = c % 2
                for s in range(nseg):
                    for j in range(CONVW):
                        oo = (ci * CONVW + j) * 128
                        self.mm(self.ps(bank, 128, T)[:, s * L:(s + 1) * L], slot[:, oo:oo + 128], histb[:, c, s, j:j + L],
                                j == 0, j == CONVW - 1, [slotB, histB[c]], [self.PB[bank]])
                self.act(cc[:, c, 0:T], self.ps(bank, 128, T), AF.Identity, [self.PB[bank], self.cB], [ccB[c]],
                         bias=self.vcol(V_CDWB + c))
                yield
        for c in range(8):
            kk = c % 2
            self.cp(ccb[:, kk, 0:T], cc[:, c, 0:T], [ccB[c]], [ccbB[kk]], eng="act")
            self.act(sqb[:, kk, 0:T], cc[:, c, 0:T], AF.Square, [ccB[c]], [sqbB[kk]])
            self.mm(self.ps(2, 128, T), self.onesb, ccb[:, kk, 0:T], c == 0, c == 7, [ccbB[kk], self.cB], [self.PB[2]])
            self.mm(self.ps(3, 128, T), self.onesb, sqb[:, kk, 0:T], c == 0, c == 7, [sqbB[kk], self.cB], [self.PB[3]])
        yield
        self.amul(mean[:, 0:T], self.ps(2, 128, T), 1.0 / D, [self.PB[2]], [stB])
        self.tt(rstd[:, 0:T], mean[:, 0:T], mean[:, 0:T], ALU.mult, [stB], [stB])
        self.stt(rstd[:, 0:T], self.ps(3, 128, T), 1.0 / D, rstd[:, 0:T], ALU.mult, ALU.subtract, [self.PB[3], stB], [stB])
        self.act(rstd[:, 0:T], rstd[:, 0:T], AF.Ln, [stB, self.cB], [stB], bias=self.vcol(V_EPS))
        self.act(rstd[:, 0:T], rstd[:, 0:T], AF.Exp, [stB], [stB], scale=-0.5)
        yield
        aTBs = [self.abuf("aT%d" % c) for c in range(8)]
        for c in range(8):
            self.tt(cc[:, c, 0:T], cc[:, c, 0:T], mean[:, 0:T], ALU.subtract, [ccB[c], stB], [ccB[c]])
            self.tt(cc[:, c, 0:T], cc[:, c, 0:T], rstd[:, 0:T], ALU.mult, [ccB[c], stB], [ccB[c]])
            self.act(aT[:, c, 0:T], cc[:, c, 0:T], AF.Silu, [ccB[c], self.cB], [aTBs[c]],
                     scale=self.vcol(V_CLNG + c), bias=self.vcol(V_CLNB + c))
            if c % 2 == 1:
                yield
        self.gpost_load(0 * 4 + 1)
        slot, slotB = self.wload(("cout",))
        yield
        yield from self.proj_out(u, aT, aTBs, slot, slotB, False, bias=True)

    def xattn(self, u, l):
        T = u.T
        self.new_phase()
        self.prenorm(u, l, 2)
        yield
        nseg = len(u.segs) if u.kind == "S" else 1
        o = 0
        qT = self.arena[:, o:o + 8 * T].rearrange("p (c t) -> p c t", c=8)
        o += 8 * T
        oT = self.arena[:, o:o + 8 * T].rearrange("p (c t) -> p c t", c=8)
        o += 8 * T
        KT = self.arena[:, o:o + nseg * 2048].rearrange("p (s c m) -> p s c m", s=nseg, c=8)
        o += nseg * 2048
        Vt = self.arena[:, o:o + nseg * 2048].rearrange("p (s c d) -> p s c d", s=nseg, c=2)
        o += nseg * 2048
        memT = self.arena[:, o:o + 2048].rearrange("p (c m) -> p c m", c=8)
        o += 2048
        mtm = self.arena[:, o:o + 4 * D].bitcast(F32).rearrange("p (c d) -> p c d", c=2)
        o += 4 * D
        mtb = self.arena[:, o:o + 2 * D].rearrange("p (c d) -> p c d", c=2)
        o += 2 * D
        pex = self.arena[:, o:o + 2048].bitcast(F32).rearrange("p (k m) -> p k m", k=2)
        o += 2048
        pn = self.arena[:, o:o + 1024].rearrange("p (k m) -> p k m", k=2)
        o += 1024
        pT = self.arena[:, o:o + 4 * 2 * T].rearrange("p (h c t) -> p h c t", h=4, c=2)
        o += 8 * T
        pexb_ = mtm[:, 0, :].rearrange("p (k m) -> p k m", k=2)
        pnb_ = mtb[:, 0, :].rearrange("p (k m) -> p k m", k=2)
        assert o <= self.c.an, o
        kout = mtm
        qTB, KTB, VtB, memTB = self.abuf("qT"), self.abuf("KT"), self.abuf("Vt"), self.abuf("memT")
        oTBs = [self.abuf("oT%d" % k) for k in range(8)]
        mtmB, mtbB = [self.abuf("mtm0"), self.abuf("mtm1")], [self.abuf("mtb0"), self.abuf("mtb1")]
        pexB, pnB = [self.abuf("pex0"), self.abuf("pex1")], [self.abuf("pn0"), self.abuf("pn1")]
        pTB = [self.abuf("pT%d" % h) for h in range(4)]
        koutB = mtmB
        if u.kind == "P" and not u.first:
            kb = self.kvsB[(l, u.seq)]
            self.load(KT[:, 0, :, :], self.kvs[l, u.seq, :, 0:2048].rearrange("p (c m) -> p c m", c=8), [KTB], reads=[kb])
            self.load(Vt[:, 0, :, :], self.kvs[l, u.seq, :, 2048:4096].rearrange("p (c d) -> p c d", c=2), [VtB], reads=[kb])
            yield
        elif u.kind == "P":
            for c2 in range(2):
                self.load(mtm[:, c2, :], self.memp[u.seq, c2 * 128:(c2 + 1) * 128, :], [mtmB[c2]])
                self.cp(mtb[:, c2, :], mtm[:, c2, :], [mtmB[c2]], [mtbB[c2]])
                pb = self.psb(2 + c2, 128, 1024)
                for c in range(8):
                    self.tr(pb[:, c * 128:(c + 1) * 128], mtb[:, c2, c * 128:(c + 1) * 128], self.identb,
                            [mtbB[c2], self.cB], [self.PB[2 + c2]])
                self.cp(memT[:, :, c2 * 128:(c2 + 1) * 128], pb.rearrange("p (c m) -> p c m", c=8),
                        [self.PB[2 + c2]], [memTB])
            slot, slotB = self.wload(("wkA", l))
            yield
            for m in range(8):
                bank = m % 2
                for k in range(8):
                    oo = (m * 8 + k) * 128
                    self.mm(self.ps(bank, 128, 256), slot[:, oo:oo + 128], memT[:, k, :], k == 0, k == 7,
                            [slotB, memTB], [self.PB[bank]])
                self.cp(KT[:, 0, m, :], self.ps(bank, 128, 256), [self.PB[bank]], [KTB], eng="act")
                if m % 2 == 1:
                    yield
            for nm, dst, keep in (("wkB", self.p_mk, False), ("wvB", self.p_mv, True)):
                if not (keep or u.first):
                    continue
                slot, slotB = self.wload((nm, l))
                yield
                for c2 in range(2):
                    banks = (2, 3)
                    for hf in range(2):
                        for k in range(8):
                            self.mm(self.ps(banks[hf]), memT[:, k, c2 * 128:(c2 + 1) * 128],
                                    slot[:, k * D + hf * 512:k * D + hf * 512 + 512], k == 0, k == 7,
                                    [memTB, slotB], [self.PB[banks[hf]]])
                        if u.first:
                            self.cp(kout[:, c2, hf * 512:(hf + 1) * 512], self.ps(banks[hf]), [self.PB[banks[hf]]],
                                    [koutB[c2]], eng="act")
                        if keep:
                            self.cp(Vt[:, 0, c2, hf * 512:(hf + 1) * 512], self.ps(banks[hf]), [self.PB[banks[hf]]], [VtB])
                    if u.first:
                        self.store(dst[l, u.seq, c2 * 128:(c2 + 1) * 128, :], kout[:, c2, :], [koutB[c2]])
                    yield
            kb = self.kvsB[(l, u.seq)]
            KTs, Vts = KT[:, 0, :, :], Vt[:, 0, :, :]
            d1 = self.kvs[l, u.seq, :, 0:2048].rearrange("p (c m) -> p c m", c=8)
            d2 = self.kvs[l, u.seq, :, 2048:4096].rearrange("p (c d) -> p c d", c=2)
            self.olane = (self.olane + 1) % 4
            self.S.dma("sp", "out%d_%d" % (self.c.idx, self.olane), lambda h: h.dma_start(out=d1, in_=KTs), [KTB], [kb])
            self.olane = (self.olane + 1) % 4
            self.S.dma("sp", "out%d_%d" % (self.c.idx, self.olane), lambda h: h.dma_start(out=d2, in_=Vts), [VtB, kb], [kb])
        else:
            for s in range(nseg):
                for c2 in range(2):
                    self.load(mtm[:, c2, :], self.cmk[l, u.sbase + s, c2 * 128:(c2 + 1) * 128, :], [mtmB[c2]])
                    self.cp(mtb[:, c2, :], mtm[:, c2, :], [mtmB[c2]], [mtbB[c2]])
                    pb = self.psb(2 + c2, 128, 1024)
                    for c in range(8):
                        self.tr(pb[:, c * 128:(c + 1) * 128], mtb[:, c2, c * 128:(c + 1) * 128], self.identb,
                                [mtbB[c2], self.cB], [self.PB[2 + c2]])
                    self.cp(KT[:, s, :, c2 * 128:(c2 + 1) * 128], pb.rearrange("p (c m) -> p c m", c=8),
                            [self.PB[2 + c2]], [KTB])
                for c2 in range(2):
                    self.load(mtm[:, c2, :], self.cmv[l, u.sbase + s, c2 * 128:(c2 + 1) * 128, :], [mtmB[c2]])
                    self.cp(Vt[:, s, c2, :], mtm[:, c2, :], [mtmB[c2]], [VtB])
                yield
        slot, slotB = self.wload(("wq", l))
        yield
        for m in range(8):
            bank = m % 2
            for k in range(8):
                oo = (m * 8 + k) * 128
                self.mm(self.ps(bank, 128, T), slot[:, oo:oo + 128], self.hT[:, k, 0:T], k == 0, k == 7,
                        [slotB, self.hTB], [self.PB[bank]])
            self.amul(qT[:, m, 0:T], self.ps(bank, 128, T), 1.0 / 16, [self.PB[bank]], [qTB])
            if m % 2 == 1:
                yield
        if u.kind == "P":
            groups = [(t0, nt, 0) for (t0, nt) in u.subs]
        else:
            groups = [(c0, Ls, s) for s, (c0, Ls, _) in enumerate(u.segs)]
        def attn_group(gi, t0, nt, kv, ba, bb):
            c = self.next_stat(12)
            sB = self.cur_statB
            st = self.stat
            bk = (ba, bb)
            PBk = (self.PB[ba], self.PB[bb])
            pexg, png = pex4[gi % 2], pn4[gi % 2]
            pexGB, pnGB = pex4B[gi % 2], pn4B[gi % 2]
            for hp in range(2):
                for hh in range(2):
                    h = hp * 2 + hh
                    for dc in range(2):
                        self.mm(self.ps(bk[hp], nt)[:, hh * 256:(hh + 1) * 256], qT[:, 2 * h + dc, t0:t0 + nt],
                                KT[:, kv, 2 * h + dc, :], dc == 0, dc == 1, [qTB, KTB], [PBk[hp]])
            yield
            for hp in range(2):
                sc3 = self.ps(bk[hp], nt).rearrange("p (h m) -> p h m", h=2)
                mxo = st[0:nt, c + hp * 2:c + hp * 2 + 2]
                self.S.op("dve", lambda hd, sc3=sc3, mxo=mxo: hd.tensor_reduce(
                    out=mxo, in_=sc3, axis=mybir.AxisListType.X, op=ALU.max, negate=True),
                    [PBk[hp]], [sB, PBk[hp]])
            yield
            for hp in range(2):
                for hh in range(2):
                    h = hp * 2 + hh
                    self.act(pexg[0:nt, hp, hh * 256:(hh + 1) * 256], self.ps(bk[hp], nt)[:, hh * 256:(hh + 1) * 256], AF.Exp,
                             [PBk[hp], sB], [pexGB[hp], sB], bias=st[0:nt, c + h:c + h + 1],
                             accum_out=st[0:nt, c + 4 + h:c + 5 + h])
            yield
            rco, rci = st[0:nt, c + 8:c + 12], st[0:nt, c + 4:c + 8]
            self.S.op("dve", lambda hd, rco=rco, rci=rci: hd.reciprocal(out=rco, in_=rci), [sB], [sB])
            for hp in range(2):
                self.tt(png[0:nt, hp, 0:512].rearrange("p (h m) -> p h m", h=2),
                        pexg[0:nt, hp, 0:512].rearrange("p (h m) -> p h m", h=2),
                        st[0:nt, c + 8 + hp * 2:c + 10 + hp * 2].unsqueeze(2).to_broadcast([nt, 2, 256]), ALU.mult,
                        [pexGB[hp], sB], [pnGB[hp]])
            yield
            for hp in range(2):
                pb = self.psb(bk[hp], 128, 4 * nt)
                for hh in range(2):
                    for mc in range(2):
                        self.tr(pb[:, (hh * 2 + mc) * nt:(hh * 2 + mc + 1) * nt],
                                png[0:nt, hp, hh * 256 + mc * 128:hh * 256 + mc * 128 + 128], self.cstb[0:nt, 0:nt],
                                [pnGB[hp], self.cB], [PBk[hp]])
            yield
            for hp in range(2):
                pb = self.psb(bk[hp], 128, 4 * nt)
                self.cp(pT[:, hp * 2:hp * 2 + 2, :, t0:t0 + nt],
                        pb.rearrange("p (h c t) -> p h c t", h=2, c=2), [PBk[hp]], [pTB[gi % 2][hp * 2], pTB[gi % 2][hp * 2 + 1]],
                        eng="act" if hp == 0 else "dve")
            yield
            for h in range(4):
                for dc in range(2):
                    x_ = (h * 2 + dc) % 2
                    for mc in range(2):
                        self.mm(self.ps(bk[x_], 128, nt), Vt[:, kv, mc, h * 256 + dc * 128:h * 256 + dc * 128 + 128],
                                pT[:, h, mc, t0:t0 + nt], mc == 0, mc == 1, [VtB, pTB[gi % 2][h]], [PBk[x_]])
                    self.cp(oT[:, 2 * h + dc, t0:t0 + nt], self.ps(bk[x_], 128, nt), [PBk[x_]], [oTBs[2 * h + dc]],
                            eng="act" if dc else "dve")
                if h % 2 == 1:
                    yield

        pex4 = [pex, pexb_]
        pn4 = [pn, pnb_]
        pex4B = [pexB, [mtmB[0], mtmB[0]]]
        pn4B = [pnB, [mtbB[0], mtbB[0]]]
        pTB = [pTB, [self.abuf("pTb%d" % h) for h in range(4)]]
        pendg = list(enumerate(groups))
        liveg = []
        freeb = [(0, 1), (2, 3)]
        while pendg or liveg:
            if pendg and freeb:
                gi, (t0, nt, kv) = pendg.pop(0)
                bks = freeb.pop(0)
                liveg.append((bks, attn_group(gi, t0, nt, kv, bks[0], bks[1])))
            nxt = []
            for bks, g_ in liveg:
                try:
                    next(g_)
                    nxt.append((bks, g_))
                except StopIteration:
                    freeb.append(bks)
            liveg = nxt
            yield
        self.gpost_load(l * 4 + 2)
        slot, slotB = self.wload(("wo", l))
        yield
        yield from self.proj_out(u, oT, oTBs, slot, slotB, False)

    def gdn_prep(self, u, n, Z, C, L, qkvT, qkvB, gb, gbB):
        HC = 8 * C
        B = Z["B"]
        ba, bb = Z["bm"]
        PA, PBk = self.PB[ba], self.PB[bb]

        def psa(p=128, nn=512):
            return self.ps(ba, p, nn)

        def psbk(p=128, nn=512):
            return self.ps(bb, p, nn)

        def h3(a):
            return a.rearrange("p (h c) -> p h c", h=8)
        gU, gU2, tA, Dm, DmT, EGb, cf = Z["gU"], Z["gU2"], Z["gU"], Z["Dm"], Z["DmT"], Z["EGb"], Z["cf"]
        M0, Em, Wt, Xt, Yt, QKm = Z["M0"], Z["Em"], Z["Wt"], Z["Xt"], Z["Yt"], Z["QKm"]
        KBe, Kdec, VB, nkc, qd = Z["KBe"], Z["Kdec"], Z["VB"], Z["nkc"], Z["qd"]
        tri_le = self.cst[0:C, 128:128 + C]
        tri_gt = self.cst[0:C, 256:256 + C]
        onesCC = self.cst[0:C, 384:384 + C]
        nlev = int(math.log2(C))
        cols = slice(n * C, (n + 1) * C)
        g_n = gb[0:C, n, 0:8]
        be_n = gb[0:C, n, 8:16]
        self.tt(gU[0:C, :, 0:C], tri_le.unsqueeze(1).to_broadcast([C, 8, C]), g_n.unsqueeze(2).to_broadcast([C, 8, C]),
                ALU.mult, [gbB, self.cB], [B["gU"]])
        yield
        for h in range(8):
            self.mm(psa(C, HC)[:, h * C:(h + 1) * C], gU[0:C, h, 0:C], tri_gt, True, True, [B["gU"], self.cB], [PA])
        gUf = gU2[0:C, 0:HC]
        self.mm(psbk(C, HC), tri_gt, gUf, True, True, [B["gU"], self.cB], [PBk])
        yield
        self.act(Dm[0:C, :, 0:C], h3(psa(C, HC)), AF.Exp, [PA], [B["Dm"]])
        self.act(DmT[0:C, :, 0:C], h3(psbk(C, HC)), AF.Exp, [PBk], [B["DmT"]])
        yield
        self.mm(psa(128, HC), self.cst[0:C, 384:512], gUf, True, True, [B["gU"], self.cB], [PA])
        self.mm(psbk(C, 16)[:, 0:8], tri_le, g_n, True, True, [gbB, self.cB], [PBk])
        self.mm(psbk(C, 16)[:, 8:16], onesCC, g_n, True, True, [gbB, self.cB], [PBk])
        yield
        self.cp(cf[0:C, 32:48], psbk(C, 16), [PBk], [B["cf"]])
        self.act(EGb[:, :, 0:C], h3(psa(128, HC)), AF.Exp, [PA], [B["EGb"]])
        yield
        for h in range(8):
            kT_h = qkvT[:, 8 + h, cols]
            self.mm(psa(C, HC)[:, h * C:(h + 1) * C], kT_h, kT_h, True, True, [qkvB[8 + h]], [PA])
            self.mm(psbk(C, HC)[:, h * C:(h + 1) * C], kT_h, qkvT[:, h, cols], True, True,
                    [qkvB[8 + h], qkvB[h]], [PBk])
        self.act(cf[0:C, 0:8], cf[0:C, 32:40], AF.Exp, [B["cf"]], [B["cf"]])
        self.tt(cf[0:C, 8:16], cf[0:C, 40:48], cf[0:C, 32:40], ALU.subtract, [B["cf"]], [B["cf"]])
        self.act(cf[0:C, 8:16], cf[0:C, 8:16], AF.Exp, [B["cf"]], [B["cf"]])
        self.tt(cf[0:C, 16:24], cf[0:C, 0:8], be_n, ALU.mult, [B["cf"], gbB], [B["cf"]])
        self.ts(cf[0:C, 24:32], be_n, -1.0, None, ALU.mult, None, [gbB], [B["cf"]])
        yield
        self.tt(tA[0:C, :, 0:C], h3(psa(C, HC)), Dm[0:C, :, 0:C], ALU.mult, [PA, B["Dm"]], [B["gU"]])
        self.tt(tA[0:C, :, 0:C], tA[0:C, :, 0:C], cf[0:C, 24:32].unsqueeze(2).to_broadcast([C, 8, C]), ALU.mult,
                [B["gU"], B["cf"]], [B["gU"]])
        self.tt(M0[0:C, :, 0:C], tA[0:C, :, 0:C], tri_gt.unsqueeze(1).to_broadcast([C, 8, C]), ALU.mult,
                [B["gU"], self.cB], [B["M0"]])
        yield
        self.tt(tA[0:C, :, 0:C], h3(psbk(C, HC)), DmT[0:C, :, 0:C], ALU.mult, [PBk, B["DmT"], B["gU"]], [B["gU"]])
        self.tt(QKm[0:C, :, 0:C], tA[0:C, :, 0:C], tri_le.unsqueeze(1).to_broadcast([C, 8, C]), ALU.mult,
                [B["gU"], self.cB], [B["QKm"]])
        mk0 = self.cst[0:C, 512:512 + C].unsqueeze(1).to_broadcast([C, 8, C])
        self.tt(Em[0:C, :, 0:C], M0[0:C, :, 0:C], mk0, ALU.mult, [B["M0"], self.cB], [B["Dm"]])
        yield
        pbN = self.psb(ba, C, HC)
        for h in range(8):
            self.tr(pbN[:, h * C:(h + 1) * C], Em[0:C, h, 0:C], self.cstb[0:C, 0:C], [B["Dm"], self.cB], [PA])
        pbK = self.psb(bb, C, 1024)
        for h in range(8):
            self.tr(pbK[:, h * 128:(h + 1) * 128], qkvT[:, 8 + h, cols], self.identb, [qkvB[8 + h], self.cB], [PBk])
        yield
        self.tt(Wt[0:C, :, 0:C], h3(pbN), self.cst[0:C, 0:C].unsqueeze(1).to_broadcast([C, 8, C]), ALU.add,
                [PA, self.cB], [B["W"]])
        pbK3 = pbK.rearrange("p (h d) -> p h d", h=8)
        self.tt(KBe[0:C], pbK3, cf[0:C, 16:24].unsqueeze(2).to_broadcast([C, 8, 128]), ALU.mult, [PBk, B["cf"]], [B["KBe"]])
        self.tt(Kdec[0:C], pbK3, cf[0:C, 8:16].unsqueeze(2).to_broadcast([C, 8, 128]), ALU.mult, [PBk, B["cf"]], [B["Kdec"]])
        self.tt(qd[:, :, 0:C], qkvT[:, 0:8, cols], EGb[:, :, 0:C], ALU.mult, qkvB[0:8] + [B["EGb"]], [B["qd"]])
        yield
        pbV = self.psb(bb, C, 1024)
        for h in range(8):
            self.tr(pbV[:, h * 128:(h + 1) * 128], qkvT[:, 16 + h, cols], self.identb, [qkvB[16 + h], self.cB], [PBk])
        yield
        pbV3 = pbV.rearrange("p (h d) -> p h d", h=8)
        self.tt(VB[0:C], pbV3, be_n.unsqueeze(2).to_broadcast([C, 8, 128]), ALU.mult, [PBk, gbB], [B["VB"]])
        for lv in range(1, nlev):
            mk = self.cst[0:C, 512 + lv * 64:512 + lv * 64 + C].unsqueeze(1).to_broadcast([C, 8, C])
            self.tt(Em[0:C, :, 0:C], M0[0:C, :, 0:C], mk, ALU.mult, [B["M0"], self.cB], [B["Dm"]])
            pbX = self.psb(ba, C, HC)
            for h in range(8):
                self.tr(pbX[:, h * C:(h + 1) * C], Wt[0:C, h, 0:C], self.cstb[0:C, 0:C], [B["W"], self.cB], [PA])
            yield
            self.cp(Xt[0:C, :, 0:C], h3(pbX), [PA], [B["DmT"]], eng="act")
            for h in range(8):
                self.mm(psbk(C, HC)[:, h * C:(h + 1) * C], Em[0:C, h, 0:C], Wt[0:C, h, 0:C], True, True,
                        [B["Dm"], B["W"]], [PBk])
            yield
            self.cp(Yt[0:C, :, 0:C], h3(psbk(C, HC)), [PBk], [B["gU"]], eng="act")
            yield
            for h in range(8):
                self.mm(psa(C, HC)[:, h * C:(h + 1) * C], Xt[0:C, h, 0:C], Yt[0:C, h, 0:C], True, True,
                        [B["DmT"], B["gU"]], [PA])
            yield
            self.tt(Wt[0:C, :, 0:C], Wt[0:C, :, 0:C], h3(psa(C, HC)), ALU.add, [B["W"], PA], [B["W"]])
            yield
        for h in range(8):
            self.mm(psbk(128, HC)[:, h * C:(h + 1) * C], KBe[0:C, h, :], Wt[0:C, h, 0:C], True, True,
                    [B["KBe"], B["W"]], [PBk])
        yield
        self.amul(nkc[:, :, 0:C], h3(psbk(128, HC)), -1.0, [PBk], [B["nkc"]])
        yield

    def gdn_scan(self, u, n, Z, C, L, oT, oTB):
        HC = 8 * C
        B = Z["B"]
        PB = self.PB

        def h3(a):
            return a.rearrange("p (h c) -> p h c", h=8)
        Wt, QKm, KBe, Kdec, VB, Ub, nkc, qd, EGb = (Z["Wt"], Z["QKm"], Z["KBe"], Z["Kdec"], Z["VB"], Z["Ub"], Z["nkc"],
                                                   Z["qd"], Z["EGb"])
        seg = (n * C) // L
        cols = slice(n * C, (n + 1) * C)
        first_chunk = (n * C) % L == 0 and (u.kind == "S" or u.first)
        Sf = self.Sf[:, 0, :, :]
        if first_chunk:
            if u.kind == "P":
                self.memset(Sf, 0.0, [self.SfB])
            else:
                self.load(Sf, self.sdn[u.sbase + seg].rearrange("h k v -> k h v"), [self.SfB])
            self.cp(self.Sb[:], Sf, [self.SfB], [self.SbB], eng="act")
        for h in range(8):
            bank = 1 + h // 4
            out = self.ps(bank, C)[:, (h % 4) * 128:(h % 4 + 1) * 128]
            self.mm(out, Wt[0:C, h, 0:C], VB[0:C, h, :], True, False, [B["W"], B["VB"]], [PB[bank]])
            self.mm(out, nkc[:, h, 0:C], self.Sb[:, h, :], False, True, [B["nkc"], self.SbB], [PB[bank]])
        yield
        for hf in range(2):
            self.cp(Ub[0:C, hf * 4:hf * 4 + 4, :], self.ps(1 + hf, C).rearrange("p (h d) -> p h d", h=4),
                    [PB[1 + hf]], [B["Ub"]], eng="act" if hf else "dve")
        yield
        for h in range(8):
            out = self.ps(2, 128, HC)[:, h * C:(h + 1) * C]
            self.mm(out, Ub[0:C, h, :], QKm[0:C, h, 0:C], True, False, [B["Ub"], B["QKm"]], [PB[2]])
            self.mm(out, self.Sb[:, h, :], qd[:, h, 0:C], False, True, [self.SbB, B["qd"]], [PB[2]])
        for h in range(8):
            bank = 0 if h < 4 else 3
            self.mm(self.ps(bank)[:, (h % 4) * 128:(h % 4 + 1) * 128], Kdec[0:C, h, :], Ub[0:C, h, :], True, True,
                    [B["Kdec"], B["Ub"]], [PB[bank]])
        yield
        self.cp(oT[:, :, cols], h3(self.ps(2, 128, HC)), [PB[2]], [oTB[n]], eng="act")
        self.tt(Sf, Sf, EGb[:, :, C - 1:C].to_broadcast([128, 8, 128]), ALU.mult, [self.SfB, B["EGb"]], [self.SfB])
        for hf in range(2):
            bank = 0 if hf == 0 else 3
            self.tt(Sf[:, hf * 4:hf * 4 + 4, :], Sf[:, hf * 4:hf * 4 + 4, :],
                    self.ps(bank).rearrange("p (h d) -> p h d", h=4), ALU.add, [self.SfB, PB[bank]], [self.SfB])
        self.cp(self.Sb[:], Sf, [self.SfB], [self.SbB], eng="act")
        last_chunk = ((n + 1) * C) % L == 0 and (u.kind == "S" or u.last)
        if last_chunk:
            dst = self.p_state[u.seq] if u.kind == "P" else self.s_state[u.sbase + seg]
            self.store(dst.rearrange("h k v -> k h v"), Sf, [self.SfB])
        yield

    def gdn(self, u):
        T = u.T
        self.new_phase()
        self.prenorm(u, 1, 1)
        yield
        nseg = len(u.segs)
        L = u.segs[0][1]
        C = min(64, L)
        nch = T // C
        o = 0
        qkvT = self.arena[:, o:o + 24 * T].rearrange("p (c t) -> p c t", c=24)
        o += 24 * T
        gT = self.arena[:, o:o + 8 * T].rearrange("p (c t) -> p c t", c=8)
        o += 8 * T
        oT = self.arena[:, o:o + 8 * T].rearrange("p (c t) -> p c t", c=8)
        o += 8 * T
        gb = self.arena[0:64, o:o + 2 * nch * 16].bitcast(F32).rearrange("p (n c) -> p n c", n=nch)
        o += 2 * nch * 16
        o_fixed = o
        RW = 3 + L
        raw = self.arena[:, o:o + 2 * 3 * nseg * RW].bitcast(F32).rearrange("p (k s w) -> p k s w", k=3, s=nseg)
        o += 6 * nseg * RW
        acc = self.arena[:, o:o + 6 * T].bitcast(F32).rearrange("p (k t) -> p k t", k=3)
        o += 6 * T
        sq = self.arena[:, o:o + 3 * T].rearrange("p (k t) -> p k t", k=3)
        o += 3 * T
        rs = self.arena[:, o:o + 6 * T].bitcast(F32).rearrange("p (k t) -> p k t", k=3)
        o += 6 * T
        tp = self.arena[0:64, o:o + 2 * nch * 8].bitcast(F32).rearrange("p (n c) -> p n c", n=nch)
        o += 2 * nch * 8
        tp2 = self.arena[0:64, o:o + 2 * nch * 8].bitcast(F32).rearrange("p (n c) -> p n c", n=nch)
        o += 2 * nch * 8
        tp3 = self.arena[0:64, o:o + 2 * nch * 8].bitcast(F32).rearrange("p (n c) -> p n c", n=nch)
        o += 2 * nch * 8
        ob = self.arena[0:3, o:o + 2 * D].bitcast(F32)
        o += 2 * D
        qkvB = [self.abuf("qkv%d" % m) for m in range(24)]
        gTB, oTB = self.abuf("gT"), [self.abuf("oT%d" % n) for n in range(nch)]
        rawB, accB = [self.abuf("raw%d" % i) for i in range(3)], [self.abuf("acc%d" % i) for i in range(3)]
        sqB, rsB = [self.abuf("sq%d" % i) for i in range(3)], [self.abuf("rs%d" % i) for i in range(3)]
        obB = self.abuf("dncout")
        gbB = self.abuf("gb")
        if u.kind == "S":
            nr = nseg * 3
            ctm = self.arena[0:nr, o:o + 2 * 3 * D].bitcast(F32)
            o += 2 * 3 * D
            ctmB = self.abuf("dctm")
            self.load(ctm, self.cdn[u.sbase * 3:u.sbase * 3 + nr, :], [ctmB])
            for m in range(24):
                bank = 2 + m % 2
                self.tr(self.ps(bank, 128, nr), ctm[:, m * 128:(m + 1) * 128], self.cst[0:nr, 0:nr],
                        [ctmB, self.cB], [self.PB[bank]])
                for s in range(nseg):
                    self.cp(self.dnh[:, s, m, :], self.ps(bank, 128, nr)[:, s * 3:s * 3 + 3],
                            [self.PB[bank]], [self.dnhB[s]])
        elif u.first:
            self.memset(self.dnh[:, 0, :, :], 0.0, [self.dnhB[0]])
        assert o <= self.c.an, o
        yield
        NBUF = 3
        free = list(range(NBUF))
        pend = [(ti, nm, mi) for ti, nm in enumerate(("dq", "dk", "dv", "dg")) for mi in range(8)]
        slots = {}
        live = []

        def proj_chunk(ti, nm, mi, kk):
            slot, slotB = slots[nm]
            m = ti * 8 + mi
            bank = m % 2
            for k in range(8):
                oo = (mi * 8 + k) * 128
                self.mm(self.ps(bank, 128, T), slot[:, oo:oo + 128], self.hT[:, k, 0:T], k == 0, k == 7,
                        [slotB, self.hTB], [self.PB[bank]])
            yield
            if nm == "dg":
                self.act(gT[:, mi, 0:T], self.ps(bank, 128, T), AF.Silu, [self.PB[bank]], [gTB])
                return
            for s_, (c0, Ls, sidx) in enumerate(u.segs):
                self.cp(raw[:, kk, s_, 0:3], self.dnh[:, sidx, m, :], [self.dnhB[sidx]], [rawB[kk]])
            self.cp(raw[:, kk, :, 3:3 + L], self.ps(bank, 128, T).rearrange("p (s w) -> p s w", s=nseg),
                    [self.PB[bank]], [rawB[kk]], eng="act")
            yield
            for s_, (c0, Ls, sidx) in enumerate(u.segs):
                self.cp(self.dnh[:, sidx, m, :], raw[:, kk, s_, L:L + 3], [rawB[kk]], [self.dnhB[sidx]])
            a3 = acc[:, kk, 0:T].rearrange("p (s w) -> p s w", s=nseg)
            self.ts(a3, raw[:, kk, :, 0:L], self.vcol(V_DNCW + m), None, ALU.mult, None, [rawB[kk], self.cB], [accB[kk]])
            for j in range(1, 4):
                self.stt(a3, raw[:, kk, :, j:j + L], self.vcol(V_DNCW + j * 24 + m), a3, ALU.mult, ALU.add,
                         [rawB[kk], accB[kk], self.cB], [accB[kk]])
            yield
            if nm == "dv":
                self.act(qkvT[:, m, 0:T], acc[:, kk, 0:T], AF.Silu, [accB[kk]], [qkvB[m]])
                return
            self.act(rs[:, kk, 0:T], acc[:, kk, 0:T], AF.Exp, [accB[kk]], [rsB[kk]], scale=-1.0)
            self.act(rs[:, kk, 0:T], rs[:, kk, 0:T], AF.Ln, [rsB[kk], self.cB], [rsB[kk]], bias=self.vcol(V_ONE))
            self.act(rs[:, kk, 0:T], rs[:, kk, 0:T], AF.Exp, [rsB[kk]], [rsB[kk]], scale=-1.0)
            yield
            self.tt(acc[:, kk, 0:T], acc[:, kk, 0:T], rs[:, kk, 0:T], ALU.mult, [accB[kk], rsB[kk]], [accB[kk]])
            yield
            self.act(sq[:, kk, 0:T], acc[:, kk, 0:T], AF.Square, [accB[kk]], [sqB[kk]])
            yield
            sbk = 2 + m % 2
            self.mm(self.ps(sbk, 128, T), self.onesb, sq[:, kk, 0:T], True, True, [sqB[kk], self.cB], [self.PB[sbk]])
            yield
            self.act(rs[:, kk, 0:T], self.ps(sbk, 128, T), AF.Ln, [self.PB[sbk], self.cB], [rsB[kk]],
                     bias=self.vcol(V_EPS))
            self.act(rs[:, kk, 0:T], rs[:, kk, 0:T], AF.Exp, [rsB[kk], self.cB], [rsB[kk]], scale=-0.5,
                     bias=self.vcol(V_LNQS if nm == "dq" else V_ZERO))
            yield
            self.tt(qkvT[:, m, 0:T], acc[:, kk, 0:T], rs[:, kk, 0:T], ALU.mult, [accB[kk], rsB[kk]], [qkvB[m]])

        def ab_chain():
            slot, slotB = slots["dab"]
            for n in range(nch):
                for k in range(8):
                    self.mm(self.ps(3, C, nch * 16)[:, n * 16:(n + 1) * 16], self.hT[:, k, n * C:(n + 1) * C],
                            slot[:, k * 16:(k + 1) * 16], k == 0, k == 7, [self.hTB, slotB], [self.PB[3]])
            ab3 = self.ps(3, C, nch * 16).rearrange("p (n c) -> p n c", n=nch)
            dtb = self.hsm[0:C, 8:16].unsqueeze(1).to_broadcast([C, nch, 8])
            self.tt(tp[0:C], ab3[:, :, 0:8], dtb, ALU.add, [self.PB[3], self.cB], [gbB])
            self.act(gb[0:C, :, 8:16], ab3[:, :, 8:16], AF.Sigmoid, [self.PB[3]], [gbB])
            yield
            self.ts(gb[0:C, :, 0:8], tp[0:C], -1.0, None, ALU.mult, None, [gbB], [gbB])
            self.tt(gb[0:C, :, 0:8], gb[0:C, :, 0:8], tp[0:C], ALU.max, [gbB], [gbB])
            yield
            self.act(gb[0:C, :, 0:8], gb[0:C, :, 0:8], AF.Exp, [gbB], [gbB], scale=-1.0)
            yield
            yv = gb[0:C, :, 0:8]
            pl = tp2[0:C]
            mk = tp3[0:C]
            self.ts(pl, yv, 0.2, -0.25, ALU.mult, ALU.add, [gbB], [gbB])
            for cst_ in (1.0 / 3, -0.5, 1.0):
                self.tt(pl, pl, yv, ALU.mult, [gbB], [gbB])
                self.ts(pl, pl, cst_, None, ALU.add, None, [gbB], [gbB])
            self.tt(pl, pl, yv, ALU.mult, [gbB], [gbB])
            self.ts(mk, yv, 0.0625, None, ALU.is_lt, None, [gbB], [gbB])
            yield
            self.act(yv, yv, AF.Ln, [gbB, self.cB], [gbB], bias=self.vcol(V_ONE, C))
            yield
            self.tt(pl, pl, yv, ALU.subtract, [gbB], [gbB])
            self.tt(pl, pl, mk, ALU.mult, [gbB], [gbB])
            self.tt(yv, yv, pl, ALU.add, [gbB], [gbB])
            self.ts(tp[0:C], tp[0:C], 0.0, None, ALU.max, None, [gbB], [gbB])
            self.tt(gb[0:C, :, 0:8], gb[0:C, :, 0:8], tp[0:C], ALU.add, [gbB], [gbB])
            self.tt(gb[0:C, :, 0:8], gb[0:C, :, 0:8], self.nega[0:C, :].unsqueeze(1).to_broadcast([C, nch, 8]), ALU.mult,
                    [gbB, self.cB], [gbB])

        slots["dab"] = self.wload(("dab",))
        yield
        live.append((None, ab_chain()))
        while pend or live:
            if pend and free:
                ti, nm, mi = pend.pop(0)
                if nm not in slots:
                    slots[nm] = self.wload((nm,))
                    yield
                kk = free.pop(0)
                live.append((kk, proj_chunk(ti, nm, mi, kk)))
            nxt = []
            for kk, g_ in live:
                try:
                    next(g_)
                    nxt.append((kk, g_))
                except StopIteration:
                    if kk is not None:
                        free.append(kk)
            live = nxt
            yield
        for s, (c0, Ls, sidx) in enumerate(u.segs):
            if u.kind == "P" and not u.last:
                continue
            dst = self.p_dnc[u.seq] if u.kind == "P" else self.s_dnc[u.sbase + s]
            for part in range(3):
                for mm_ in range(8):
                    m = part * 8 + mm_
                    bank = 2 + m % 2
                    self.tr(self.ps(bank, 3, 128), self.dnh[:, sidx, m, :], self.identf, [self.dnhB[sidx], self.cB], [self.PB[bank]])
                    self.cp(ob[:, mm_ * 128:(mm_ + 1) * 128], self.ps(bank, 3, 128), [self.PB[bank]], [obB])
                self.store(dst[:, part * D:(part + 1) * D], ob, [obB])
            yield
        self.new_phase()
        HC = 8 * C
        c_ = u.c

        def h3(a):
            return a.rearrange("p (h c) -> p h c", h=8)

        def hd(a):
            return a.rearrange("p (h d) -> p h d", h=8)
        o = o_fixed

        def car(n, parts=128):
            nonlocal o
            a = self.arena[0:parts, o:o + n]
            o += n
            return a
        sets = []
        for which in range(2):
            Z = {}
            if which == 0:
                gUr, Dmr, DmTr = car(2 * HC, 64), car(HC, 64), car(HC, 64)
                EGr, cfr = car(2 * HC), car(128, 64)
                M0r, Wr, QKr = car(HC, 64), car(HC, 64), car(HC, 64)
                KBr, Kdr, VBr, Ubr = car(1024, 64), car(1024, 64), car(1024, 64), car(1024, 64)
                nkr, qdr = car(HC), car(HC)
            else:
                EGr, nkr, qdr, cfr = car(2 * HC), car(HC), car(HC), car(128, 64)
                hTf = c_.hT[:, :, :].rearrange("p c t -> p (c t)")
                xsf = c_.xs_sc[:, :, :].rearrange("p k d -> p (k d)")
                gpf = c_.gpost[:, :].bitcast(BF16)
                tmf = c_.tmpf[:, :].bitcast(BF16)
                KBr, Kdr = hTf[0:64, 0:1024], hTf[0:64, 1024:2048]
                VBr, Ubr = xsf[0:64, 0:1024], xsf[0:64, 1024:2048]
                gUr, Dmr, DmTr = gpf[0:64, 0:2 * HC], gpf[0:64, 1024:1024 + HC], gpf[0:64, 1536:1536 + HC]
                M0r, Wr = c_.junk[0:64, 0:HC], c_.junk[0:64, 512:512 + HC]
                QKr = tmf[0:64, 0:HC]
            Z["gU2"] = gUr.bitcast(F32)
            Z["gU"] = h3(Z["gU2"])
            Z["Yt"] = h3(gUr[:, 0:HC])
            Z["Dm"], Z["Em"] = h3(Dmr), h3(Dmr)
            Z["DmT"], Z["Xt"] = h3(DmTr), h3(DmTr)
            Z["EGb"] = h3(EGr.bitcast(F32))
            Z["cf"] = cfr.bitcast(F32)
            Z["M0"], Z["Wt"], Z["QKm"] = h3(M0r), h3(Wr), h3(QKr)
            Z["KBe"], Z["Kdec"], Z["VB"], Z["Ub"] = hd(KBr), hd(Kdr), hd(VBr), hd(Ubr)
            Z["nkc"], Z["qd"] = h3(nkr), h3(qdr)
            Z["B"] = {n_: self.abuf(n_ + str(which)) for n_ in ("gU", "Dm", "DmT", "EGb", "cf", "M0", "W", "QKm", "KBe", "Kdec",
                                                                "VB", "Ub", "nkc", "qd")}
            Z["bm"] = (0, 1) if which == 0 else (2, 3)
            sets.append(Z)
        assert o <= self.c.an, o
        PB = self.PB
        for n0 in range(0, nch, 2):
            pair = [n_ for n_ in (n0, n0 + 1) if n_ < nch]
            live = [self.gdn_prep(u, n_, sets[i], C, L, qkvT, qkvB, gb, gbB) for i, n_ in enumerate(pair)]
            while live:
                nxt = []
                for g_ in live:
                    try:
                        next(g_)
                        nxt.append(g_)
                    except StopIteration:
                        pass
                    yield
                live = nxt
            for i, n_ in enumerate(pair):
                yield from self.gdn_scan(u, n_, sets[i], C, L, oT, oTB)
        allB = [b_ for Z in sets[1:] for b_ in Z["B"].values()]
        self.memset(c_.hT[0:1, 0, 0:2], 0.0, [c_.hTB] + allB)
        self.memset(c_.xs_sc[0:1, 0, 0:2], 0.0, [c_.xs_scB[0], c_.xs_scB[1]] + allB)
        self.memset(c_.gpost[0:1, 0:2], 0.0, [c_.gpostB] + allB)
        self.memset(c_.junk[0:1, 0:2], 0.0, [c_.junkB] + allB)
        self.memset(c_.tmpf[0:1, 0:2], 0.0, [c_.tmpfB] + allB)
        yield
        self.new_phase()
        sqB, rsB, accB = [self.abuf("sq%d" % i) for i in range(3)], [self.abuf("rs%d" % i) for i in range(3)], [self.abuf("acc%d" % i) for i in range(3)]
        def onorm_head(h, kk):
            self.act(sq[:, kk, 0:T], oT[:, h, 0:T], AF.Square, oTB, [sqB[kk]])
            yield
            sbk = 2 + h % 2
            self.mm(self.ps(sbk, 128, T), self.onesb, sq[:, kk, 0:T], True, True, [sqB[kk], self.cB], [PB[sbk]])
            yield
            self.act(rs[:, kk, 0:T], self.ps(sbk, 128, T), AF.Ln, [PB[sbk], self.cB], [rsB[kk]], scale=1.0 / 128,
                     bias=self.vcol(V_EPS))
            self.act(rs[:, kk, 0:T], rs[:, kk, 0:T], AF.Exp, [rsB[kk]], [rsB[kk]], scale=-0.5)
            yield
            self.tt(acc[:, kk, 0:T], oT[:, h, 0:T], rs[:, kk, 0:T], ALU.mult, oTB + [rsB[kk]], [accB[kk]])
            yield
            self.stt(qkvT[:, h, 0:T], acc[:, kk, 0:T], self.vcol(V_DNG), gT[:, h, 0:T], ALU.mult, ALU.mult,
                     [accB[kk], gTB, self.cB], [qkvB[h]])

        pendh = list(range(8))
        liveh = []
        freeh = [0, 1, 2]
        while pendh or liveh:
            if pendh and freeh:
                h = pendh.pop(0)
                kk = freeh.pop(0)
                liveh.append((kk, onorm_head(h, kk)))
            nxt = []
            for kk, g_ in liveh:
                try:
                    next(g_)
                    nxt.append((kk, g_))
                except StopIteration:
                    freeh.append(kk)
            liveh = nxt
            yield
        self.gpost_load(1 * 4 + 1)
        slot, slotB = self.wload(("dout",))
        yield
        yield from self.proj_out(u, qkvT, qkvB[0:8], slot, slotB, False)


_CACHE = {}


def _program(cfg_key):
    if cfg_key not in _CACHE:
        _CACHE[cfg_key] = Builder(dict(cfg_key)).build()
    return _CACHE[cfg_key]


def run_cores(inp, n_cores, n_pseq, plen, n_sseq, stop=99):
    cfg = (("n_pseq", n_pseq), ("plen", plen), ("n_sseq", n_sseq), ("stop", stop))
    nc = _program(cfg)
    f = lambda a: np.ascontiguousarray(np.asarray(a, dtype=np.float32))
    wpack, offs = build_tiles({k: np.asarray(v, np.float32) for k, v in inp.items()})
    o2, _ = tile_offsets()
    assert offs == o2
    vec, bvec, hsm, cst = build_vecs({k: np.asarray(v, np.float32) for k, v in inp.items()})
    in_maps = []
    for c in range(n_cores):
        ps = slice(c * n_pseq, (c + 1) * n_pseq)
        m = {"xp": f(inp["x_prompt"][ps]), "memp": f(inp["mem_prompt"][ps]), "wpack": wpack, "vecs": vec,
             "bvec": bvec, "hsm": hsm, "cst": cst}
        if n_sseq:
            ss = slice(c * n_sseq, (c + 1) * n_sseq)
            m["xs"] = f(inp["x_sample"][ss]).reshape(n_sseq * 16, D)
            m["cconv"] = f(inp["cache_conv_a"][0, ss]).reshape(n_sseq * 30, D)
            m["sdn"] = f(inp["state_dn"][0, ss])
            m["cdn"] = f(inp["cache_dn_conv"][0, ss]).reshape(n_sseq * 3, 3 * D)
            m["cmk"] = f(inp["cache_mem_k"][:, ss]).reshape(2, n_sseq, NMEM, D)
            m["cmv"] = f(inp["cache_mem_v"][:, ss]).reshape(2, n_sseq, NMEM, D)
        in_maps.append(m)
    import os as _os
    if _os.environ.get("KTRACE"):
        res = run_bass_kernel_spmd(nc, in_maps, core_ids=list(range(n_cores)), trace=True)
        print("EXEC_NS", res.exec_time_ns)
    else:
        res = run_bass_kernel_spmd(nc, in_maps, core_ids=list(range(n_cores)))
    R = res.results
    cat = lambda k, ax=0: np.concatenate([r[k] for r in R], axis=ax)
    out = {}
    out["y_prompt"] = cat("yp")
    out["p_conv_a"] = cat("p_conv")[None]
    out["p_state_dn"] = cat("p_state")[None]
    out["p_dn_conv"] = cat("p_dnc")[None]
    out["p_mem_k"] = cat("p_mk", 1).reshape(2, -1, NMEM, 4, 256)
    out["p_mem_v"] = cat("p_mv", 1).reshape(2, -1, NMEM, 4, 256)
    if n_sseq:
        out["y_sample"] = cat("ys").reshape(-1, 16, D)
        out["s_conv_a"] = cat("s_conv")[None]
        out["s_state_dn"] = cat("s_state")[None]
        out["s_dn_conv"] = cat("s_dnc")[None]
    return out


def kernel(**inputs):
    o = run_cores(inputs, 8, 2, 2048, 4)
    return (o["y_prompt"], o["y_sample"], o["p_conv_a"], o["p_state_dn"], o["p_dn_conv"], o["p_mem_k"], o["p_mem_v"],
            o["s_conv_a"], o["s_state_dn"], o["s_dn_conv"])
```

```python
from contextlib import ExitStack
import math
import numpy as np
import concourse.bass as bass
import concourse.mybir as mybir
from concourse.bass_utils import run_bass_kernel_spmd

F32 = mybir.dt.float32
BF16 = mybir.dt.bfloat16
AF = mybir.ActivationFunctionType
ALU = mybir.AluOpType

D = 1024
DFF = 2816
NJ = 22
NMEM = 256
CONVW = 31
EPS = 1e-6
SEM_CHUNK = 30000
SLOT_COLS = 11264
NSLOT = 2
ARENA = 45056


class Buf:
    __slots__ = ("name", "w", "readers", "pre", "excl")

    def __init__(self, name, pre=(), excl=False):
        self.name = name
        self.excl = excl
        self.w = None
        self.readers = {}
        self.pre = pre


class Eng:
    def __init__(self, name):
        self.name = name
        self.ops = []
        self.count = 0
        self.waited = {}


class Lane:
    def __init__(self, key):
        self.key = key
        self.count = 0


class Sched:
    def __init__(self):
        self.eng = {n: Eng(n) for n in ("pe", "act", "dve", "pool", "sp")}
        self.sem_keys = []
        self.lanes = {}
        self.cur_stream = None
        self.fence_all = False
        self.last = {}

    def _prog_key(self, e, idx):
        k = ("prog", e.name, idx // SEM_CHUNK)
        if k not in self.sem_keys:
            self.sem_keys.append(k)
        return k

    def lane(self, name):
        if name not in self.lanes:
            k = ("lane", name)
            self.sem_keys.append(k)
            self.lanes[name] = Lane(k)
        return self.lanes[name]

    def _deps(self, e, reads, writes, extra):
        deps = {}

        def add(ev):
            if ev is None:
                return
            k, v = ev
            if e.name == "pe" and k[0] == "prog" and k[1] == "pe":
                return
            if deps.get(k, 0) < v:
                deps[k] = v

        for b in reads:
            add(b.w)
            for ev in b.pre:
                add(ev)
        for b in writes:
            add(b.w)
            for ev in b.pre:
                add(ev)
            for k, v in b.readers.items():
                add((k, v))
        for ev in extra:
            add(ev)
        out = []
        for k, v in deps.items():
            if e.waited.get(k, 0) < v:
                e.waited[k] = v
                out.append((k, v))
        return out

    def _commit(self, ev, reads, writes):
        for b in writes:
            b.w = ev
            b.readers = {}
        k, v = ev
        for b in reads:
            if b.readers.get(k, 0) < v:
                b.readers[k] = v

    def op(self, engine, fn, reads=(), writes=()):
        e = self.eng[engine]
        ex = [b for b in reads if b.excl]
        if ex:
            writes = list(writes) + ex
        waits = self._deps(e, reads, writes, ())
        idx = e.count
        e.count += 1
        k = self._prog_key(e, idx)
        ev = (k, idx % SEM_CHUNK + 1)
        self._commit(ev, reads, writes)
        e.ops.append((fn, waits, (k, 1)))
        self.last[(self.cur_stream, engine)] = ev
        return ev

    def dma(self, engine, lane_name, fn, reads=(), writes=()):
        e = self.eng[engine]
        ln = self.lane(lane_name)
        prev = (ln.key, ln.count) if ln.count else None
        waits = self._deps(e, reads, writes, (prev,))
        ln.count += 16
        ev = (ln.key, ln.count)
        self._commit(ev, reads, writes)
        e.ops.append((fn, waits, (ln.key, 16)))
        if not lane_name.startswith("w"):
            self.last[(self.cur_stream, lane_name)] = ev
        return ev

    def fence(self):
        return tuple(ev for (st, _), ev in self.last.items() if st == self.cur_stream or self.fence_all)

    def wait_all(self, engine, events):
        e = self.eng[engine]
        waits = self._deps(e, (), (), events)
        e.ops.append((None, waits, None))

    def check_deadlock(self):
        val = {}
        ptr = {n: 0 for n in self.eng}
        progress = True
        while progress:
            progress = False
            for n, e in self.eng.items():
                while ptr[n] < len(e.ops):
                    fn, waits, inc = e.ops[ptr[n]]
                    if any(val.get(k, 0) < v for k, v in waits):
                        break
                    if inc is not None:
                        val[inc[0]] = val.get(inc[0], 0) + inc[1]
                    ptr[n] += 1
                    progress = True
        stuck = {n: (ptr[n], len(e.ops)) for n, e in self.eng.items() if ptr[n] < len(e.ops)}
        if stuck:
            msg = []
            for n in stuck:
                fn, waits, inc = self.eng[n].ops[ptr[n]]
                msg.append("%s@%d waits %s" % (n, ptr[n], [(k, v, val.get(k, 0)) for k, v in waits if val.get(k, 0) < v]))
            raise RuntimeError("DEADLOCK: " + "; ".join(msg))

    def emit(self, nc, stack):
        self.check_deadlock()
        sems = {}
        for k in self.sem_keys:
            sems[k] = stack.enter_context(nc.semaphore("_".join(str(x) for x in k)))
        block = stack.enter_context(nc.Block())
        handles = {"pe": block.tensor, "act": block.scalar, "dve": block.vector,
                   "pool": block.gpsimd, "sp": block.sync}

        def make(e):
            def body(h):
                for fn, waits, inc in e.ops:
                    for k, v in waits:
                        h.wait_ge(sems[k], v)
                    if fn is None:
                        continue
                    fn(h).then_inc(sems[inc[0]], inc[1])
            return body

        for name, e in self.eng.items():
            if e.ops:
                handles[name](make(e))


V_NPRE = 0
V_CBIN = 64
V_CDW = 80
V_CDWB = 328
V_CLNG = 336
V_CLNB = 344
V_DNCW = 352
V_DNG = 448
V_EPS = 449
V_LNHALF = 450
V_ONE = 451
V_ZERO = 452
V_LNQS = 453
NV = 460
NCST = 512 + 6 * 64


def _modeA(W, m0, m1):
    K, N = W.shape
    a = W.reshape(K // 128, 128, N // 128, 128)[:, :, m0:m1, :]
    return np.ascontiguousarray(a.transpose(1, 2, 0, 3)).reshape(128, -1)


def _modeB(W, c0, c1):
    K, N = W.shape
    a = W.reshape(K // 128, 128, N)[:, :, c0:c1]
    return np.ascontiguousarray(a.transpose(1, 0, 2)).reshape(128, -1)


def build_tiles(inp):
    tiles = {}
    for l in range(2):
        for f in range(2):
            Wg, Wu, Wd = inp["ffn_w_gate"][l, f], inp["ffn_w_up"][l, f], inp["ffn_w_down"][l, f]
            ww = np.stack([Wg, Wu]).reshape(2, 8, 128, NJ, 128)
            ww = ww.transpose(2, 3, 0, 1, 4)
            for g in range(5):
                j0, j1 = 5 * g, min(NJ, 5 * g + 5)
                tiles[("U", l, f, g)] = np.ascontiguousarray(ww[:, j0:j1]).reshape(128, -1)
            for nh in range(2):
                tiles[("D", l, f, nh)] = _modeB(Wd, nh * 512, nh * 512 + 512)
        tiles[("wkA", l)] = _modeA(inp["xa_wk"][l], 0, 8)
        tiles[("wkB", l)] = _modeB(inp["xa_wk"][l], 0, D)
        tiles[("wvB", l)] = _modeB(inp["xa_wv"][l], 0, D)
        tiles[("wq", l)] = _modeA(inp["xa_wq"][l], 0, 8)
        tiles[("wo", l)] = _modeB(inp["xa_wo"][l], 0, D)
    win = inp["ca_w_in"][0]
    a = win.reshape(8, 128, 2, 8, 128)
    a = a.transpose(1, 3, 2, 0, 4)
    for g in range(2):
        tiles[("cin", g)] = np.ascontiguousarray(a[:, 4 * g:4 * g + 4]).reshape(128, -1)
    tiles[("cout",)] = _modeB(inp["ca_w_out"][0], 0, D)
    dw = inp["ca_dw"][0]
    ar = np.arange(128)
    for g in range(4):
        t = np.zeros((128, 2, CONVW, 128), np.float32)
        for ci in range(2):
            c = 2 * g + ci
            t[ar, ci, :, ar] = dw[:, c * 128:(c + 1) * 128].T
        tiles[("cdw", g)] = t.reshape(128, -1)
    dwin = inp["dn_w_in"][0]
    for i, nm in enumerate(("dq", "dk", "dv", "dg")):
        tiles[(nm,)] = _modeA(dwin[:, i * D:(i + 1) * D], 0, 8)
    tiles[("dab",)] = _modeB(dwin[:, 4 * D:4 * D + 16], 0, 16)
    tiles[("dout",)] = _modeB(inp["dn_w_out"][0], 0, D)
    offs = {}
    o = 0
    for k, v in tiles.items():
        assert v.shape[1] <= SLOT_COLS
        offs[k] = (o, v.shape[1])
        o += v.shape[1]
    wpack = np.concatenate([v.astype(np.float32) for v in tiles.values()], axis=1)
    return np.ascontiguousarray(wpack), offs


def tile_offsets():
    sizes = {}
    for l in range(2):
        for f in range(2):
            for g in range(5):
                sizes[("U", l, f, g)] = (min(NJ, 5 * g + 5) - 5 * g) * 2048
            for nh in range(2):
                sizes[("D", l, f, nh)] = NJ * 512
        for nm in ("wkA", "wkB", "wvB", "wq", "wo"):
            sizes[(nm, l)] = 8192
    for g in range(2):
        sizes[("cin", g)] = 8192
    sizes[("cout",)] = 8192
    for g in range(4):
        sizes[("cdw", g)] = 2 * CONVW * 128
    for nm in ("dq", "dk", "dv", "dg"):
        sizes[(nm,)] = 8192
    sizes[("dab",)] = 128
    sizes[("dout",)] = 8192
    offs = {}
    o = 0
    for k, n in sizes.items():
        offs[k] = (o, n)
        o += n
    return offs, o


def pcol(v):
    return np.ascontiguousarray(v.reshape(-1, 128).T)


def build_vecs(inp):
    vec = np.zeros((128, NV), np.float32)
    for l in range(2):
        for j in range(4):
            vec[:, V_NPRE + (l * 4 + j) * 8:V_NPRE + (l * 4 + j) * 8 + 8] = pcol(inp["norm_pre"][l, j])
    vec[:, V_CBIN:V_CBIN + 16] = pcol(inp["ca_b_in"][0])
    for j in range(CONVW):
        vec[:, V_CDW + j * 8:V_CDW + j * 8 + 8] = pcol(inp["ca_dw"][0, j])
    vec[:, V_CDWB:V_CDWB + 8] = pcol(inp["ca_dw_b"][0])
    vec[:, V_CLNG:V_CLNG + 8] = pcol(inp["ca_ln_g"][0])
    vec[:, V_CLNB:V_CLNB + 8] = pcol(inp["ca_ln_b"][0])
    for j in range(4):
        vec[:, V_DNCW + j * 24:V_DNCW + j * 24 + 24] = pcol(inp["dn_conv_w"][0, j])
    vec[:, V_DNG] = inp["dn_norm_g"][0]
    vec[:, V_EPS] = EPS
    vec[:, V_LNHALF] = math.log(0.5)
    vec[:, V_ONE] = 1.0
    vec[:, V_ZERO] = 0.0
    vec[:, V_LNQS] = math.log(128 ** -0.5)
    bvec = np.concatenate([inp["norm_post"].reshape(8, D), inp["ca_b_out"].reshape(1, D)], 0).astype(np.float32)
    hsm = np.concatenate([inp["dn_a_log"][0], inp["dn_dt_bias"][0]])[None, :].astype(np.float32)
    cst = np.zeros((128, NCST), np.float32)
    idx = np.arange(128)
    cst[:, 0:128] = np.eye(128)
    cst[:, 128:256] = (idx[:, None] <= idx[None, :])
    cst[:, 256:384] = (idx[:, None] > idx[None, :])
    cst[:, 384:512] = 1.0
    i64 = np.arange(64)
    for l in range(6):
        b = 2 ** l
        i, j = i64[:, None], i64[None, :]
        cst[0:64, 512 + l * 64:512 + (l + 1) * 64] = ((i // (2 * b) == j // (2 * b)) & ((i // b) % 2 == 1) & ((j // b) % 2 == 0))
    return vec, np.ascontiguousarray(bvec), hsm, cst


class Unit:
    pass


class Ctx:
    pass


class Builder:
    def __init__(self, cfg):
        self.cfg = cfg
        self.nc = bass.Bass("TRN2", target_bir_lowering=False)
        self.S = Sched()
        self.out_events = []
        self.cur_fence = ()
        self.olane = 0
        self.ilane = 0
        self._c = None

    @property
    def c(self):
        return self._c

    @c.setter
    def c(self, v):
        self._c = v
        self.S.cur_stream = v.idx

    arena = property(lambda s: s.c.arena)
    hT = property(lambda s: s.c.hT)
    hTB = property(lambda s: s.c.hTB)
    xs_sc = property(lambda s: s.c.xs_sc)
    xs_scB = property(lambda s: s.c.xs_scB)
    junk = property(lambda s: s.c.junk)
    junkB = property(lambda s: s.c.junkB)
    stat = property(lambda s: s.c.stat)
    tmpf = property(lambda s: s.c.tmpf)
    tmpfB = property(lambda s: s.c.tmpfB)
    gpost = property(lambda s: s.c.gpost)
    gpostB = property(lambda s: s.c.gpostB)
    Sf = property(lambda s: s.c.Sf)
    Sb = property(lambda s: s.c.Sb)
    SfB = property(lambda s: s.c.SfB)
    SbB = property(lambda s: s.c.SbB)
    convh = property(lambda s: s.c.convh)
    convhB = property(lambda s: s.c.convhB)
    dnh = property(lambda s: s.c.dnh)
    dnhB = property(lambda s: s.c.dnhB)
    PB = property(lambda s: s.c.PB)

    def abuf(self, name):
        return Buf(name, self.c.fence)

    def new_phase(self):
        self.c.fence = self.S.fence()

    def sb(self, name, shape, dt):
        return self.st.enter_context(self.nc.sbuf_tensor("sb_" + name, shape, dt))

    def act(self, out, in_, func, reads, writes, **kw):
        self.S.op("act", lambda h: h.activation(out=out, in_=in_, func=func, **kw), reads, writes)

    def amul(self, out, in_, mul, reads, writes):
        self.S.op("act", lambda h: h.mul(out=out, in_=in_, mul=mul), reads, writes)

    def memset(self, ap, val, writes):
        self.S.op("dve", lambda h: h.memset(ap, val), (), writes)

    def tt(self, out, in0, in1, op, reads, writes):
        self.S.op("dve", lambda h: h.tensor_tensor(out=out, in0=in0, in1=in1, op=op), reads, writes)

    def ts(self, out, in0, s1, s2, op0, op1, reads, writes):
        if s2 is None:
            self.S.op("dve", lambda h: h.tensor_scalar(out=out, in0=in0, scalar1=s1, scalar2=None, op0=op0), reads, writes)
        else:
            self.S.op("dve", lambda h: h.tensor_scalar(out=out, in0=in0, scalar1=s1, scalar2=s2, op0=op0, op1=op1), reads, writes)

    def stt(self, out, in0, scalar, in1, op0, op1, reads, writes):
        self.S.op("dve", lambda h: h.scalar_tensor_tensor(out=out, in0=in0, scalar=scalar, in1=in1, op0=op0, op1=op1), reads, writes)

    def cp(self, out, in_, reads, writes, eng="dve"):
        if eng == "dve":
            self.S.op("dve", lambda h: h.tensor_copy(out=out, in_=in_), reads, writes)
        else:
            self.S.op("act", lambda h: h.activation(out=out, in_=in_, func=AF.Identity), reads, writes)

    def mm(self, out, lhsT, rhs, start, stop, reads, writes):
        self.S.op("pe", lambda h: h.matmul(out, lhsT=lhsT, rhs=rhs, start=start, stop=stop), reads, writes)

    def tr(self, out, in_, ident, reads, writes):
        self.S.op("pe", lambda h: h.transpose(out=out, in_=in_, identity=ident), reads, writes)

    def load(self, out, in_, writes, reads=(), **kw):
        self.ilane = (self.ilane + 1) % 4
        return self.S.dma("sp", "in%d_%d" % (self.c.idx, self.ilane), lambda h: h.dma_start(out=out, in_=in_, **kw), reads, writes)

    def store(self, out, in_, reads, **kw):
        self.olane = (self.olane + 1) % 4
        ev = self.S.dma("sp", "out%d_%d" % (self.c.idx, self.olane), lambda h: h.dma_start(out=out, in_=in_, **kw), reads, [Buf("dram")])
        self.out_events.append(ev)
        return ev

    def ps(self, i, p=128, n=512):
        assert i < self.c.nbank
        i += self.c.bank0
        return self.pd[i // 2][0:p, (i % 2) * 512:(i % 2) * 512 + n]

    def psb(self, i, p=128, n=1024):
        assert i < self.c.nbank
        i += self.c.bank0
        return self.pd[i // 2][0:p, (i % 2) * 512:(i % 2) * 512 + 512].bitcast(BF16)[:, 0:n]

    def vcol(self, c, p=128, n=1):
        return self.vecs[0:p, c:c + n]

    def next_stat(self, n=1):
        assert n <= 16
        r = self.c.statn % 8
        self.c.statn += 1
        self.cur_statB = self.c.statB[r]
        return r * 16

    def wload(self, key):
        if key in self.wcache:
            ent = self.wcache[key]
            ent[2] += 1
            return ent[0], ent[1]
        off, n = self.woffs[key]
        s = self.wcount % NSLOT
        self.wcount += 1
        prev = self.slot_owner[s]
        if prev is not None:
            assert self.wcache[prev][2] == self.nstreams, ("ring slot reused before all streams read it", prev, key)
        slot = self.ring[s]
        buf = self.ringB[s]
        src = self.wpack[:, off:off + n]
        self.S.dma("pool", "w%d" % s,
                   lambda h: h.dma_start(out=slot[:, 0:n], in_=src, max_dma_last_dim=8192),
                   (), [buf])
        self.wcache[key] = [slot, buf, 1]
        self.slot_owner[s] = key
        return slot, buf

    def make_ctx(self, idx, a0, an, bank0, nbank, nseg):
        sb = self.sb
        c = Ctx()
        c.idx = idx
        c.fence = ()
        c.arena = self.arena_all[:, a0:a0 + an]
        c.an = an
        c.bank0, c.nbank = bank0, nbank
        c.PB = self.PBall[bank0:bank0 + nbank]
        n = "c%d" % idx
        c.x = sb("x" + n, [128, 2, D], F32)
        c.xB = Buf("x" + n)
        c.hT = sb("hT" + n, [128, 8, 256], BF16)
        c.hTB = Buf("hT" + n)
        c.xs_sc = sb("xs" + n, [128, 2, D], BF16)
        c.xs_scB = [Buf("xs0" + n), Buf("xs1" + n)]
        c.junk = sb("junk" + n, [128, D], BF16)
        c.junkB = Buf("junk" + n)
        c.stat = sb("stat" + n, [128, 128], F32)
        c.statB = [Buf("stat%d%s" % (i, n)) for i in range(8)]
        c.statn = 0
        c.tmpf = sb("tmpf" + n, [128, 512], F32)
        c.tmpfB = Buf("tmpf" + n)
        c.gpost = sb("gpost" + n, [128, D], F32)
        c.gpostB = Buf("gpost" + n)
        c.convh = sb("convh" + n, [128, 1, 8, 30], F32)
        c.convhB = [Buf("convh" + n)]
        c.dnh = sb("dnh" + n, [128, nseg, 24, 3], F32)
        c.dnhB = [Buf("dnh%d%s" % (i, n)) for i in range(nseg)]
        c.Sf = sb("Sf" + n, [128, 1, 8, 128], F32)
        c.Sb = sb("Sb" + n, [128, 8, 128], BF16)
        c.SfB = Buf("Sf" + n)
        c.SbB = Buf("Sb" + n)
        return c

    def build(self):
        cfg = self.cfg
        nc = self.nc
        NP, PL, NS = cfg["n_pseq"], cfg["plen"], cfg["n_sseq"]
        self.woffs, wtot = tile_offsets()
        dt = nc.dram_tensor

        def din(name, shape):
            return dt(name, shape, F32, kind="ExternalInput").ap()

        def dout(name, shape):
            return dt(name, shape, F32, kind="ExternalOutput").ap()

        self.xp = din("xp", [NP, PL, D])
        self.memp = din("memp", [NP, NMEM, D])
        self.wpack = din("wpack", [128, wtot])
        self.vecs_d = din("vecs", [128, NV])
        self.bvec_d = din("bvec", [9, D])
        self.hsm_d = din("hsm", [1, 16])
        self.cst_d = din("cst", [128, NCST])
        self.yp = dout("yp", [NP, PL, D])
        self.p_conv = dout("p_conv", [NP, 30, D])
        self.p_state = dout("p_state", [NP, 8, 128, 128])
        self.p_dnc = dout("p_dnc", [NP, 3, 3 * D])
        self.p_mk = dout("p_mk", [2, NP, NMEM, D])
        self.p_mv = dout("p_mv", [2, NP, NMEM, D])
        self.kvs = dt("kvs", [2, NP, 128, 4096], BF16, kind="Internal").ap()
        self.kvsB = {(l, q): Buf("kvs%d_%d" % (l, q)) for l in range(2) for q in range(NP)}
        if NS:
            self.xs = din("xs", [NS * 16, D])
            self.cconv = din("cconv", [NS * 30, D])
            self.sdn = din("sdn", [NS, 8, 128, 128])
            self.cdn = din("cdn", [NS * 3, 3 * D])
            self.cmk = din("cmk", [2, NS, NMEM, D])
            self.cmv = din("cmv", [2, NS, NMEM, D])
            self.ys = dout("ys", [NS * 16, D])
            self.s_conv = dout("s_conv", [NS, 30, D])
            self.s_state = dout("s_state", [NS, 8, 128, 128])
            self.s_dnc = dout("s_dnc", [NS, 3, 3 * D])

        with ExitStack() as st:
            self.st = st
            sb = self.sb
            self.pd = [st.enter_context(nc.psum_tensor("pd%d" % i, [128, 1024], F32)) for i in range(4)]
            self.PBall = [Buf("ps%d" % i, excl=True) for i in range(8)]
            self.vecs = sb("vecs", [128, NV], F32)
            self.cst = sb("cstf", [128, NCST], F32)
            self.cstb = sb("cstb", [128, NCST], BF16)
            self.hsm = sb("hsm", [64, 16], F32)
            self.nega = sb("nega", [64, 8], F32)
            self.boutb = sb("boutb", [1, D], BF16)
            self.ring = [sb("ring%d" % i, [128, SLOT_COLS], BF16) for i in range(NSLOT)]
            self.ringB = [Buf("ring%d" % i) for i in range(NSLOT)]
            self.wcount = 0
            self.wcache = {}
            self.slot_owner = [None] * NSLOT
            self.arena_all = sb("arena", [128, ARENA], BF16)
            cB = self.cB = Buf("consts")
            half = ARENA // 2
            ctxs = [self.make_ctx(0, 0, half, 0, 4, max(NS // 2, 1)), self.make_ctx(1, half, half, 4, 4, max(NS // 2, 1))]
            self.c = ctxs[0]

            self.load(self.vecs[:], self.vecs_d, [cB])
            self.load(self.cst[:], self.cst_d, [cB])
            self.load(self.hsm[:], self.hsm_d[0].partition_broadcast(64), [cB])
            boutf = ctxs[0].tmpf[0:1, :]
            for hf in range(2):
                self.load(boutf, self.bvec_d[8:9, hf * 512:(hf + 1) * 512], [ctxs[0].tmpfB])
                self.cp(self.boutb[:, hf * 512:(hf + 1) * 512], boutf, [ctxs[0].tmpfB], [cB])
            self.cp(self.cstb[:], self.cst[:], [cB], [cB])
            self.act(self.nega[:], self.hsm[:, 0:8], AF.Exp, [cB], [cB])
            self.ts(self.nega[:], self.nega[:], -1.0, None, ALU.mult, None, [cB], [cB])
            self.identb = self.cstb[:, 0:128]
            self.identf = self.cst[:, 0:128]
            self.onesb = self.cstb[:, 384:512]

            stop = cfg.get("stop", 99)
            TU = 256
            npass = PL // TU
            for s0 in range(0, NP, 2):
                seqs = list(range(s0, min(NP, s0 + 2)))
                for j in range(npass):
                    units = []
                    for si, s in enumerate(seqs):
                        u = Unit()
                        u.kind, u.seq, u.pos, u.T, u.first, u.last = "P", s, j, TU, j == 0, j == npass - 1
                        u.subs = [(i * 128, 128) for i in range(TU // 128)]
                        u.segs = [(0, TU, 0)]
                        u.c = ctxs[si]
                        units.append(u)
                    self.run_pass(units, stop)
            if NS:
                units = []
                nsp = 2 if NS % 2 == 0 else 1
                per = NS // nsp
                for si in range(nsp):
                    u = Unit()
                    u.kind, u.seq, u.pos, u.T, u.first, u.last = "S", 0, 0, per * 16, True, True
                    u.sbase = si * per
                    u.subs = [(0, per * 16)]
                    u.segs = [(i * 16, 16, i) for i in range(per)]
                    u.c = ctxs[si]
                    units.append(u)
                self.run_pass(units, stop)
            self.S.wait_all("sp", self.out_events)
            self.S.emit(nc, st)
        return nc

    def run_pass(self, units, stop):
        self.nstreams = len(units)
        self.wcache = {}
        self.slot_owner = [None] * NSLOT
        for u in units:
            self.c = u.c
            self.load_x(u)
        phases = [lambda u: self.ffn(u, 0, 0), lambda u: self.conf(u), lambda u: self.xattn(u, 0), lambda u: self.ffn(u, 0, 1),
                  lambda u: self.ffn(u, 1, 0), lambda u: self.gdn(u), lambda u: self.xattn(u, 1), lambda u: self.ffn(u, 1, 1)]
        for i, ph in enumerate(phases):
            if i >= stop:
                break
            gens = []
            for u in units:
                self.c = u.c
                gens.append((u, ph(u)))
            live = list(gens)
            while live:
                nxt = []
                for u, g in live:
                    self.c = u.c
                    try:
                        next(g)
                        nxt.append((u, g))
                    except StopIteration:
                        pass
                live = nxt
        for u in units:
            self.c = u.c
            self.store_x(u)

    def load_x(self, u):
        c = u.c
        if u.kind == "P":
            src = self.xp[u.seq, u.pos * u.T:(u.pos + 1) * u.T, :].rearrange("(s p) d -> p s d", p=128)
            self.load(c.x[:, :, :], src, [c.xB])
        else:
            self.load(c.x[0:u.T, 0, :], self.xs[u.sbase * 16:u.sbase * 16 + u.T, :], [c.xB])

    def store_x(self, u):
        c = u.c
        if u.kind == "P":
            dst = self.yp[u.seq, u.pos * u.T:(u.pos + 1) * u.T, :].rearrange("(s p) d -> p s d", p=128)
            self.store(dst, c.x[:, :, :], [c.xB])
        else:
            self.store(self.ys[u.sbase * 16:u.sbase * 16 + u.T, :], c.x[0:u.T, 0, :], [c.xB])

    def prenorm(self, u, l, j):
        c_ = u.c
        g0 = V_NPRE + (l * 4 + j) * 8
        st = self.stat
        cs = []
        for si, (t0, nt) in enumerate(u.subs):
            c = self.next_stat(3)
            cs.append((c, self.cur_statB))
        for si, (t0, nt) in enumerate(u.subs):
            c, sB = cs[si]
            self.act(self.junk[0:nt, :], c_.x[0:nt, si, :], AF.Square, [c_.xB], [self.junkB, sB], accum_out=st[0:nt, c:c + 1])
        for si, (t0, nt) in enumerate(u.subs):
            c, sB = cs[si]
            self.act(st[0:nt, c + 1:c + 2], st[0:nt, c:c + 1], AF.Ln, [sB, self.cB], [sB],
                     scale=1.0 / D, bias=self.vcol(V_EPS, nt))
        for si, (t0, nt) in enumerate(u.subs):
            c, sB = cs[si]
            self.act(st[0:nt, c + 2:c + 3], st[0:nt, c + 1:c + 2], AF.Exp, [sB], [sB], scale=-0.5)
        for si, (t0, nt) in enumerate(u.subs):
            c, sB = cs[si]
            k = si % 2
            self.ts(self.xs_sc[0:nt, k, :], c_.x[0:nt, si, :], st[0:nt, c + 2:c + 3], None, ALU.mult, None,
                    [c_.xB, sB], [self.xs_scB[k]])
        for si, (t0, nt) in enumerate(u.subs):
            k = si % 2
            bank = 2 + si % 2
            pb = self.psb(bank, 128, 8 * nt)
            for cc in range(8):
                self.tr(pb[:, cc * nt:(cc + 1) * nt], self.xs_sc[0:nt, k, cc * 128:(cc + 1) * 128],
                        self.cstb[0:nt, 0:nt], [self.xs_scB[k], self.cB], [self.PB[bank]])
        for si, (t0, nt) in enumerate(u.subs):
            bank = 2 + si % 2
            pb = self.psb(bank, 128, 8 * nt)
            gbc = self.vecs[:, g0:g0 + 8].unsqueeze(2).to_broadcast([128, 8, nt])
            self.tt(self.hT[:, :, t0:t0 + nt], pb.rearrange("p (c t) -> p c t", c=8), gbc, ALU.mult,
                    [self.PB[bank], self.cB], [self.hTB])

    def gpost_load(self, row):
        self.load(self.gpost[:, :], self.bvec_d[row].partition_broadcast(128), [self.gpostB])

    def postnorm(self, u, si, nt, banks, half_scale):
        c_ = u.c
        c = self.next_stat(5)
        sB = self.cur_statB
        st = self.stat
        for hf in range(2):
            self.act(self.junk[0:nt, 0:512], self.ps(banks[hf], nt), AF.Square, [self.PB[banks[hf]]],
                     [self.junkB, sB], accum_out=st[0:nt, c + hf:c + hf + 1])
        self.tt(st[0:nt, c + 2:c + 3], st[0:nt, c:c + 1], st[0:nt, c + 1:c + 2], ALU.add, [sB], [sB])
        self.act(st[0:nt, c + 3:c + 4], st[0:nt, c + 2:c + 3], AF.Ln, [sB, self.cB], [sB],
                 scale=1.0 / D, bias=self.vcol(V_EPS, nt))
        self.act(st[0:nt, c + 4:c + 5], st[0:nt, c + 3:c + 4], AF.Exp, [sB, self.cB], [sB], scale=-0.5,
                 bias=self.vcol(V_LNHALF if half_scale else V_ZERO, nt))
        for hf in range(2):
            tmp = self.tmpf[0:nt, :]
            self.tt(tmp, self.ps(banks[hf], nt), self.gpost[0:nt, hf * 512:(hf + 1) * 512], ALU.mult,
                    [self.PB[banks[hf]], self.gpostB], [self.tmpfB])
            xo = c_.x[0:nt, si, hf * 512:(hf + 1) * 512]
            self.stt(xo, tmp, st[0:nt, c + 4:c + 5], xo, ALU.mult, ALU.add, [self.tmpfB, sB, c_.xB], [c_.xB])

    def proj_out(self, u, actT, actBs, slot, slotB, half_scale, bias=False):
        for si, (t0, nt) in enumerate(u.subs):
            banks = (0, 1) if si % 2 == 0 else (2, 3)
            for hf in range(2):
                for k in range(8):
                    self.mm(self.ps(banks[hf], nt), actT[:, k, t0:t0 + nt],
                            slot[:, k * D + hf * 512:k * D + hf * 512 + 512], k == 0, (k == 7) and not bias,
                            [actBs[k], slotB], [self.PB[banks[hf]]])
                if bias:
                    self.mm(self.ps(banks[hf], nt), self.cstb[0:1, 384:384 + nt],
                            self.boutb[0:1, hf * 512:hf * 512 + 512], False, True,
                            [self.cB], [self.PB[banks[hf]]])
            self.postnorm(u, si, nt, banks, half_scale)
            yield

    def ffn(self, u, l, f):
        T = u.T
        self.new_phase()
        self.prenorm(u, l, 0 if f == 0 else 3)
        yield
        hid = self.arena[:, 0:NJ * T].rearrange("p (j t) -> p j t", j=NJ)
        hidB = [self.abuf("hid%d" % j) for j in range(NJ)]
        sg = self.arena[:, NJ * T:NJ * T + 4 * T].bitcast(F32).rearrange("p (k t) -> p k t", k=2)
        sgB = [self.abuf("sg0"), self.abuf("sg1")]
        for g in range(5):
            slot, slotB = self.wload(("U", l, f, g))
            yield
            for jj in range(min(NJ, 5 * g + 5) - 5 * g):
                j = 5 * g + jj
                bg, bu = (0, 1) if j % 2 == 0 else (2, 3)
                for w, bank in ((0, bg), (1, bu)):
                    for k in range(8):
                        o = ((jj * 2 + w) * 8 + k) * 128
                        self.mm(self.ps(bank, 128, T), slot[:, o:o + 128], self.hT[:, k, 0:T], k == 0, k == 7,
                                [slotB, self.hTB], [self.PB[bank]])
                kk = j % 2
                self.act(sg[:, kk, 0:T], self.ps(bg, 128, T), AF.Silu, [self.PB[bg]], [sgB[kk]])
                self.tt(hid[:, j, 0:T], sg[:, kk, 0:T], self.ps(bu, 128, T), ALU.mult,
                        [sgB[kk], self.PB[bu]], [hidB[j]])
                yield
        self.gpost_load(l * 4 + (0 if f == 0 else 3))
        slot0, slot0B = self.wload(("D", l, f, 0))
        yield
        for si, (t0, nt) in enumerate(u.subs):
            for j in range(NJ):
                self.mm(self.ps(si, nt), hid[:, j, t0:t0 + nt], slot0[:, j * 512:(j + 1) * 512], j == 0, j == NJ - 1,
                        [hidB[j], slot0B], [self.PB[si]])
            yield
        slot1, slot1B = self.wload(("D", l, f, 1))
        yield
        for si, (t0, nt) in enumerate(u.subs):
            b1 = 2 + si
            for j in range(NJ):
                self.mm(self.ps(b1, nt), hid[:, j, t0:t0 + nt], slot1[:, j * 512:(j + 1) * 512], j == 0, j == NJ - 1,
                        [hidB[j], slot1B], [self.PB[b1]])
            self.postnorm(u, si, nt, (si, b1), True)
            yield

    def conf(self, u):
        T = u.T
        self.new_phase()
        self.prenorm(u, 0, 1)
        yield
        nseg = len(u.segs)
        L = u.segs[0][1]
        HW = 30 + L
        o = 0
        histb = self.arena[:, o:o + 8 * nseg * HW].rearrange("p (c s w) -> p c s w", c=8, s=nseg)
        o += 8 * nseg * HW
        tail = self.arena[:, o:o + 2 * 8 * nseg * 30].bitcast(F32).rearrange("p (c s w) -> p c s w", c=8, s=nseg)
        o += 2 * 8 * nseg * 30
        cc = self.arena[:, o:o + 2 * 8 * T].bitcast(F32).rearrange("p (c t) -> p c t", c=8)
        o += 2 * 8 * T
        aT = self.arena[:, o:o + 8 * T].rearrange("p (c t) -> p c t", c=8)
        o += 8 * T
        sig = self.arena[:, o:o + 4 * T].bitcast(F32).rearrange("p (k t) -> p k t", k=2)
        o += 4 * T
        sqb = self.arena[:, o:o + 2 * T].rearrange("p (k t) -> p k t", k=2)
        o += 2 * T
        ccb = self.arena[:, o:o + 2 * T].rearrange("p (k t) -> p k t", k=2)
        o += 2 * T
        mean = self.arena[:, o:o + 2 * T].bitcast(F32)
        o += 2 * T
        rstd = self.arena[:, o:o + 2 * T].bitcast(F32)
        o += 2 * T
        ob = self.arena[0:30, o:o + 2 * D].bitcast(F32)
        obB = self.abuf("convout")
        o += 2 * D
        histB = [self.abuf("hist%d" % c) for c in range(8)]
        tailB = [self.abuf("tail%d" % c) for c in range(8)]
        ccB = [self.abuf("cc%d" % c) for c in range(8)]
        sigB = [self.abuf("sig0"), self.abuf("sig1")]
        sqbB = [self.abuf("sqb0"), self.abuf("sqb1")]
        ccbB = [self.abuf("ccb0"), self.abuf("ccb1")]
        stB = self.abuf("lnstat")
        KEEP = 30 - min(L, 30)
        if u.kind == "P":
            if u.first:
                self.memset(histb[:, :, 0, 0:30], 0.0, histB)
            else:
                self.cp(histb[:, :, 0, 0:30], self.convh[:, 0, :, :], [self.convhB[0]], histB)
        else:
            nr = nseg * 30
            ctm = self.arena[0:nr, o:o + 2 * D].bitcast(F32)
            o += 2 * D
            ctmB = self.abuf("ctm")
            self.load(ctm, self.cconv[u.sbase * 30:u.sbase * 30 + nr, :], [ctmB])
            for c in range(8):
                bank = c % 2
                self.tr(self.ps(bank, 128, nr), ctm[:, c * 128:(c + 1) * 128], self.cst[0:nr, 0:nr],
                        [ctmB, self.cB], [self.PB[bank]])
                p3 = self.ps(bank, 128, nr).rearrange("p (s w) -> p s w", s=nseg)
                self.cp(histb[:, c, :, 0:30], p3, [self.PB[bank]], [histB[c]])
                self.cp(tail[:, c, :, 0:KEEP], p3[:, :, 30 - KEEP:30], [self.PB[bank]], [tailB[c]], eng="act")
        assert o <= self.c.an, o
        yield
        NEW = min(L, 30)
        for g in range(2):
            slot, slotB = self.wload(("cin", g))
            yield
            for ci in range(4):
                c = 4 * g + ci
                bv, bg = (0, 1) if c % 2 == 0 else (2, 3)
                for w, bank in ((0, bv), (1, bg)):
                    for k in range(8):
                        oo = ((ci * 2 + w) * 8 + k) * 128
                        self.mm(self.ps(bank, 128, T), slot[:, oo:oo + 128], self.hT[:, k, 0:T], k == 0, k == 7,
                                [slotB, self.hTB], [self.PB[bank]])
                kk = c % 2
                self.act(sig[:, kk, 0:T], self.ps(bg, 128, T), AF.Sigmoid, [self.PB[bg], self.cB], [sigB[kk]],
                         bias=self.vcol(V_CBIN + 8 + c))
                pv3 = self.ps(bv, 128, T).rearrange("p (s w) -> p s w", s=nseg)
                sg3 = sig[:, kk, 0:T].rearrange("p (s w) -> p s w", s=nseg)
                self.stt(histb[:, c, :, 30:30 + L], pv3, self.vcol(V_CBIN + c), sg3,
                         ALU.add, ALU.mult, [self.PB[bv], sigB[kk], self.cB], [histB[c]])
                self.stt(tail[:, c, :, KEEP:30], pv3[:, :, L - NEW:L], self.vcol(V_CBIN + c), sg3[:, :, L - NEW:L],
                         ALU.add, ALU.mult, [self.PB[bv], sigB[kk], self.cB], [tailB[c]])
                yield
        for s, (c0, Ls, sidx) in enumerate(u.segs):
            if u.kind == "P" and not u.last:
                self.cp(self.convh[:, 0, :, :], tail[:, :, s, :], tailB, [self.convhB[0]])
            else:
                dst = self.p_conv[u.seq] if u.kind == "P" else self.s_conv[u.sbase + s]
                for c in range(8):
                    bank = 2 + c % 2
                    self.tr(self.ps(bank, 30, 128), tail[:, c, s, :], self.identf,
                            [tailB[c], self.cB], [self.PB[bank]])
                    self.cp(ob[:, c * 128:(c + 1) * 128], self.ps(bank, 30, 128), [self.PB[bank]], [obB])
                self.store(dst, ob, [obB])
            yield
        for g in range(4):
            slot, slotB = self.wload(("cdw", g))
            yield
            for ci in range(2):
                c = 2 * g + ci
                bank = c % 2
                for s in range(nseg):
                    for j in range(CONVW):
                        oo = (ci * CONVW + j) * 128
                        self.mm(self.ps(bank, 128, T)[:, s * L:(s + 1) * L], slot[:, oo:oo + 128], histb[:, c, s, j:j + L],
                                j == 0, j == CONVW - 1, [slotB, histB[c]], [self.PB[bank]])
                self.act(cc[:, c, 0:T], self.ps(bank, 128, T), AF.Identity, [self.PB[bank], self.cB], [ccB[c]],
                         bias=self.vcol(V_CDWB + c))
                yield
        for c in range(8):
            kk = c % 2
            self.cp(ccb[:, kk, 0:T], cc[:, c, 0:T], [ccB[c]], [ccbB[kk]], eng="act")
            self.act(sqb[:, kk, 0:T], cc[:, c, 0:T], AF.Square, [ccB[c]], [sqbB[kk]])
            self.mm(self.ps(2, 128, T), self.onesb, ccb[:, kk, 0:T], c == 0, c == 7, [ccbB[kk], self.cB], [self.PB[2]])
            self.mm(self.ps(3, 128, T), self.onesb, sqb[:, kk, 0:T], c == 0, c == 7, [sqbB[kk], self.cB], [self.PB[3]])
        yield
        self.amul(mean[:, 0:T], self.ps(2, 128, T), 1.0 / D, [self.PB[2]], [stB])
        self.tt(rstd[:, 0:T], mean[:, 0:T], mean[:, 0:T], ALU.mult, [stB], [stB])
        self.stt(rstd[:, 0:T], self.ps(3, 128, T), 1.0 / D, rstd[:, 0:T], ALU.mult, ALU.subtract, [self.PB[3], stB], [stB])
        self.act(rstd[:, 0:T], rstd[:, 0:T], AF.Ln, [stB, self.cB], [stB], bias=self.vcol(V_EPS))
        self.act(rstd[:, 0:T], rstd[:, 0:T], AF.Exp, [stB], [stB], scale=-0.5)
        yield
        aTBs = [self.abuf("aT%d" % c) for c in range(8)]
        for c in range(8):
            self.tt(cc[:, c, 0:T], cc[:, c, 0:T], mean[:, 0:T], ALU.subtract, [ccB[c], stB], [ccB[c]])
            self.tt(cc[:, c, 0:T], cc[:, c, 0:T], rstd[:, 0:T], ALU.mult, [ccB[c], stB], [ccB[c]])
            self.act(aT[:, c, 0:T], cc[:, c, 0:T], AF.Silu, [ccB[c], self.cB], [aTBs[c]],
                     scale=self.vcol(V_CLNG + c), bias=self.vcol(V_CLNB + c))
            if c % 2 == 1:
                yield
        self.gpost_load(0 * 4 + 1)
        slot, slotB = self.wload(("cout",))
        yield
        yield from self.proj_out(u, aT, aTBs, slot, slotB, False, bias=True)

    def xattn(self, u, l):
        T = u.T
        self.new_phase()
        self.prenorm(u, l, 2)
        yield
        nseg = len(u.segs) if u.kind == "S" else 1
        o = 0
        qT = self.arena[:, o:o + 8 * T].rearrange("p (c t) -> p c t", c=8)
        o += 8 * T
        oT = self.arena[:, o:o + 8 * T].rearrange("p (c t) -> p c t", c=8)
        o += 8 * T
        KT = self.arena[:, o:o + nseg * 2048].rearrange("p (s c m) -> p s c m", s=nseg, c=8)
        o += nseg * 2048
        Vt = self.arena[:, o:o + nseg * 2048].rearrange("p (s c d) -> p s c d", s=nseg, c=2)
        o += nseg * 2048
        memT = self.arena[:, o:o + 2048].rearrange("p (c m) -> p c m", c=8)
        o += 2048
        mtm = self.arena[:, o:o + 4 * D].bitcast(F32).rearrange("p (c d) -> p c d", c=2)
        o += 4 * D
        mtb = self.arena[:, o:o + 2 * D].rearrange("p (c d) -> p c d", c=2)
        o += 2 * D
        pex = self.arena[:, o:o + 2048].bitcast(F32).rearrange("p (k m) -> p k m", k=2)
        o += 2048
        pn = self.arena[:, o:o + 1024].rearrange("p (k m) -> p k m", k=2)
        o += 1024
        pT = self.arena[:, o:o + 4 * 2 * T].rearrange("p (h c t) -> p h c t", h=4, c=2)
        o += 8 * T
        pexb_ = mtm[:, 0, :].rearrange("p (k m) -> p k m", k=2)
        pnb_ = mtb[:, 0, :].rearrange("p (k m) -> p k m", k=2)
        assert o <= self.c.an, o
        kout = mtm
        qTB, KTB, VtB, memTB = self.abuf("qT"), self.abuf("KT"), self.abuf("Vt"), self.abuf("memT")
        oTBs = [self.abuf("oT%d" % k) for k in range(8)]
        mtmB, mtbB = [self.abuf("mtm0"), self.abuf("mtm1")], [self.abuf("mtb0"), self.abuf("mtb1")]
        pexB, pnB = [self.abuf("pex0"), self.abuf("pex1")], [self.abuf("pn0"), self.abuf("pn1")]
        pTB = [self.abuf("pT%d" % h) for h in range(4)]
        koutB = mtmB
        if u.kind == "P" and not u.first:
            kb = self.kvsB[(l, u.seq)]
            self.load(KT[:, 0, :, :], self.kvs[l, u.seq, :, 0:2048].rearrange("p (c m) -> p c m", c=8), [KTB], reads=[kb])
            self.load(Vt[:, 0, :, :], self.kvs[l, u.seq, :, 2048:4096].rearrange("p (c d) -> p c d", c=2), [VtB], reads=[kb])
            yield
        elif u.kind == "P":
            for c2 in range(2):
                self.load(mtm[:, c2, :], self.memp[u.seq, c2 * 128:(c2 + 1) * 128, :], [mtmB[c2]])
                self.cp(mtb[:, c2, :], mtm[:, c2, :], [mtmB[c2]], [mtbB[c2]])
                pb = self.psb(2 + c2, 128, 1024)
                for c in range(8):
                    self.tr(pb[:, c * 128:(c + 1) * 128], mtb[:, c2, c * 128:(c + 1) * 128], self.identb,
                            [mtbB[c2], self.cB], [self.PB[2 + c2]])
                self.cp(memT[:, :, c2 * 128:(c2 + 1) * 128], pb.rearrange("p (c m) -> p c m", c=8),
                        [self.PB[2 + c2]], [memTB])
            slot, slotB = self.wload(("wkA", l))
            yield
            for m in range(8):
                bank = m % 2
                for k in range(8):
                    oo = (m * 8 + k) * 128
                    self.mm(self.ps(bank, 128, 256), slot[:, oo:oo + 128], memT[:, k, :], k == 0, k == 7,
                            [slotB, memTB], [self.PB[bank]])
                self.cp(KT[:, 0, m, :], self.ps(bank, 128, 256), [self.PB[bank]], [KTB], eng="act")
                if m % 2 == 1:
                    yield
            for nm, dst, keep in (("wkB", self.p_mk, False), ("wvB", self.p_mv, True)):
                if not (keep or u.first):
                    continue
                slot, slotB = self.wload((nm, l))
                yield
                for c2 in range(2):
                    banks = (2, 3)
                    for hf in range(2):
                        for k in range(8):
                            self.mm(self.ps(banks[hf]), memT[:, k, c2 * 128:(c2 + 1) * 128],
                                    slot[:, k * D + hf * 512:k * D + hf * 512 + 512], k == 0, k == 7,
                                    [memTB, slotB], [self.PB[banks[hf]]])
                        if u.first:
                            self.cp(kout[:, c2, hf * 512:(hf + 1) * 512], self.ps(banks[hf]), [self.PB[banks[hf]]],
                                    [koutB[c2]], eng="act")
                        if keep:
                            self.cp(Vt[:, 0, c2, hf * 512:(hf + 1) * 512], self.ps(banks[hf]), [self.PB[banks[hf]]], [VtB])
                    if u.first:
                        self.store(dst[l, u.seq, c2 * 128:(c2 + 1) * 128, :], kout[:, c2, :], [koutB[c2]])
                    yield
            kb = self.kvsB[(l, u.seq)]
            KTs, Vts = KT[:, 0, :, :], Vt[:, 0, :, :]
            d1 = self.kvs[l, u.seq, :, 0:2048].rearrange("p (c m) -> p c m", c=8)
            d2 = self.kvs[l, u.seq, :, 2048:4096].rearrange("p (c d) -> p c d", c=2)
            self.olane = (self.olane + 1) % 4
            self.S.dma("sp", "out%d_%d" % (self.c.idx, self.olane), lambda h: h.dma_start(out=d1, in_=KTs), [KTB], [kb])
            self.olane = (self.olane + 1) % 4
            self.S.dma("sp", "out%d_%d" % (self.c.idx, self.olane), lambda h: h.dma_start(out=d2, in_=Vts), [VtB, kb], [kb])
        else:
            for s in range(nseg):
                for c2 in range(2):
                    self.load(mtm[:, c2, :], self.cmk[l, u.sbase + s, c2 * 128:(c2 + 1) * 128, :], [mtmB[c2]])
                    self.cp(mtb[:, c2, :], mtm[:, c2, :], [mtmB[c2]], [mtbB[c2]])
                    pb = self.psb(2 + c2, 128, 1024)
                    for c in range(8):
                        self.tr(pb[:, c * 128:(c + 1) * 128], mtb[:, c2, c * 128:(c + 1) * 128], self.identb,
                                [mtbB[c2], self.cB], [self.PB[2 + c2]])
                    self.cp(KT[:, s, :, c2 * 128:(c2 + 1) * 128], pb.rearrange("p (c m) -> p c m", c=8),
                            [self.PB[2 + c2]], [KTB])
                for c2 in range(2):
                    self.load(mtm[:, c2, :], self.cmv[l, u.sbase + s, c2 * 128:(c2 + 1) * 128, :], [mtmB[c2]])
                    self.cp(Vt[:, s, c2, :], mtm[:, c2, :], [mtmB[c2]], [VtB])
                yield
        slot, slotB = self.wload(("wq", l))
        yield
        for m in range(8):
            bank = m % 2
            for k in range(8):
                oo = (m * 8 + k) * 128
                self.mm(self.ps(bank, 128, T), slot[:, oo:oo + 128], self.hT[:, k, 0:T], k == 0, k == 7,
                        [slotB, self.hTB], [self.PB[bank]])
            self.amul(qT[:, m, 0:T], self.ps(bank, 128, T), 1.0 / 16, [self.PB[bank]], [qTB])
            if m % 2 == 1:
                yield
        if u.kind == "P":
            groups = [(t0, nt, 0) for (t0, nt) in u.subs]
        else:
            groups = [(c0, Ls, s) for s, (c0, Ls, _) in enumerate(u.segs)]
        def attn_group(gi, t0, nt, kv, ba, bb):
            c = self.next_stat(12)
            sB = self.cur_statB
            st = self.stat
            bk = (ba, bb)
            PBk = (self.PB[ba], self.PB[bb])
            pexg, png = pex4[gi % 2], pn4[gi % 2]
            pexGB, pnGB = pex4B[gi % 2], pn4B[gi % 2]
            for hp in range(2):
                for hh in range(2):
                    h = hp * 2 + hh
                    for dc in range(2):
                        self.mm(self.ps(bk[hp], nt)[:, hh * 256:(hh + 1) * 256], qT[:, 2 * h + dc, t0:t0 + nt],
                                KT[:, kv, 2 * h + dc, :], dc == 0, dc == 1, [qTB, KTB], [PBk[hp]])
            yield
            for hp in range(2):
                sc3 = self.ps(bk[hp], nt).rearrange("p (h m) -> p h m", h=2)
                mxo = st[0:nt, c + hp * 2:c + hp * 2 + 2]
                self.S.op("dve", lambda hd, sc3=sc3, mxo=mxo: hd.tensor_reduce(
                    out=mxo, in_=sc3, axis=mybir.AxisListType.X, op=ALU.max, negate=True),
                    [PBk[hp]], [sB, PBk[hp]])
            yield
            for hp in range(2):
                for hh in range(2):
                    h = hp * 2 + hh
                    self.act(pexg[0:nt, hp, hh * 256:(hh + 1) * 256], self.ps(bk[hp], nt)[:, hh * 256:(hh + 1) * 256], AF.Exp,
                             [PBk[hp], sB], [pexGB[hp], sB], bias=st[0:nt, c + h:c + h + 1],
                             accum_out=st[0:nt, c + 4 + h:c + 5 + h])
            yield
            rco, rci = st[0:nt, c + 8:c + 12], st[0:nt, c + 4:c + 8]
            self.S.op("dve", lambda hd, rco=rco, rci=rci: hd.reciprocal(out=rco, in_=rci), [sB], [sB])
            for hp in range(2):
                self.tt(png[0:nt, hp, 0:512].rearrange("p (h m) -> p h m", h=2),
                        pexg[0:nt, hp, 0:512].rearrange("p (h m) -> p h m", h=2),
                        st[0:nt, c + 8 + hp * 2:c + 10 + hp * 2].unsqueeze(2).to_broadcast([nt, 2, 256]), ALU.mult,
                        [pexGB[hp], sB], [pnGB[hp]])
            yield
            for hp in range(2):
                pb = self.psb(bk[hp], 128, 4 * nt)
                for hh in range(2):
                    for mc in range(2):
                        self.tr(pb[:, (hh * 2 + mc) * nt:(hh * 2 + mc + 1) * nt],
                                png[0:nt, hp, hh * 256 + mc * 128:hh * 256 + mc * 128 + 128], self.cstb[0:nt, 0:nt],
                                [pnGB[hp], self.cB], [PBk[hp]])
            yield
            for hp in range(2):
                pb = self.psb(bk[hp], 128, 4 * nt)
                self.cp(pT[:, hp * 2:hp * 2 + 2, :, t0:t0 + nt],
                        pb.rearrange("p (h c t) -> p h c t", h=2, c=2), [PBk[hp]], [pTB[gi % 2][hp * 2], pTB[gi % 2][hp * 2 + 1]],
                        eng="act" if hp == 0 else "dve")
            yield
            for h in range(4):
                for dc in range(2):
                    x_ = (h * 2 + dc) % 2
                    for mc in range(2):
                        self.mm(self.ps(bk[x_], 128, nt), Vt[:, kv, mc, h * 256 + dc * 128:h * 256 + dc * 128 + 128],
                                pT[:, h, mc, t0:t0 + nt], mc == 0, mc == 1, [VtB, pTB[gi % 2][h]], [PBk[x_]])
                    self.cp(oT[:, 2 * h + dc, t0:t0 + nt], self.ps(bk[x_], 128, nt), [PBk[x_]], [oTBs[2 * h + dc]],
                            eng="act" if dc else "dve")
                if h % 2 == 1:
                    yield

        pex4 = [pex, pexb_]
        pn4 = [pn, pnb_]
        pex4B = [pexB, [mtmB[0], mtmB[0]]]
        pn4B = [pnB, [mtbB[0], mtbB[0]]]
        pTB = [pTB, [self.abuf("pTb%d" % h) for h in range(4)]]
        pendg = list(enumerate(groups))
        liveg = []
        freeb = [(0, 1), (2, 3)]
        while pendg or liveg:
            if pendg and freeb:
                gi, (t0, nt, kv) = pendg.pop(0)
                bks = freeb.pop(0)
                liveg.append((bks, attn_group(gi, t0, nt, kv, bks[0], bks[1])))
            nxt = []
            for bks, g_ in liveg:
                try:
                    next(g_)
                    nxt.append((bks, g_))
                except StopIteration:
                    freeb.append(bks)
            liveg = nxt
            yield
        self.gpost_load(l * 4 + 2)
        slot, slotB = self.wload(("wo", l))
        yield
        yield from self.proj_out(u, oT, oTBs, slot, slotB, False)

    def gdn_prep(self, u, n, Z, C, L, qkvT, qkvB, gb, gbB):
        HC = 8 * C
        B = Z["B"]
        ba, bb = Z["bm"]
        PA, PBk = self.PB[ba], self.PB[bb]

        def psa(p=128, nn=512):
            return self.ps(ba, p, nn)

        def psbk(p=128, nn=512):
            return self.ps(bb, p, nn)

        def h3(a):
            return a.rearrange("p (h c) -> p h c", h=8)
        gU, gU2, tA, Dm, DmT, EGb, cf = Z["gU"], Z["gU2"], Z["gU"], Z["Dm"], Z["DmT"], Z["EGb"], Z["cf"]
        M0, Em, Wt, Xt, Yt, QKm = Z["M0"], Z["Em"], Z["Wt"], Z["Xt"], Z["Yt"], Z["QKm"]
        KBe, Kdec, VB, nkc, qd = Z["KBe"], Z["Kdec"], Z["VB"], Z["nkc"], Z["qd"]
        tri_le = self.cst[0:C, 128:128 + C]
        tri_gt = self.cst[0:C, 256:256 + C]
        onesCC = self.cst[0:C, 384:384 + C]
        nlev = int(math.log2(C))
        cols = slice(n * C, (n + 1) * C)
        g_n = gb[0:C, n, 0:8]
        be_n = gb[0:C, n, 8:16]
        self.tt(gU[0:C, :, 0:C], tri_le.unsqueeze(1).to_broadcast([C, 8, C]), g_n.unsqueeze(2).to_broadcast([C, 8, C]),
                ALU.mult, [gbB, self.cB], [B["gU"]])
        yield
        for h in range(8):
            self.mm(psa(C, HC)[:, h * C:(h + 1) * C], gU[0:C, h, 0:C], tri_gt, True, True, [B["gU"], self.cB], [PA])
        gUf = gU2[0:C, 0:HC]
        self.mm(psbk(C, HC), tri_gt, gUf, True, True, [B["gU"], self.cB], [PBk])
        yield
        self.act(Dm[0:C, :, 0:C], h3(psa(C, HC)), AF.Exp, [PA], [B["Dm"]])
        self.act(DmT[0:C, :, 0:C], h3(psbk(C, HC)), AF.Exp, [PBk], [B["DmT"]])
        yield
        self.mm(psa(128, HC), self.cst[0:C, 384:512], gUf, True, True, [B["gU"], self.cB], [PA])
        self.mm(psbk(C, 16)[:, 0:8], tri_le, g_n, True, True, [gbB, self.cB], [PBk])
        self.mm(psbk(C, 16)[:, 8:16], onesCC, g_n, True, True, [gbB, self.cB], [PBk])
        yield
        self.cp(cf[0:C, 32:48], psbk(C, 16), [PBk], [B["cf"]])
        self.act(EGb[:, :, 0:C], h3(psa(128, HC)), AF.Exp, [PA], [B["EGb"]])
        yield
        for h in range(8):
            kT_h = qkvT[:, 8 + h, cols]
            self.mm(psa(C, HC)[:, h * C:(h + 1) * C], kT_h, kT_h, True, True, [qkvB[8 + h]], [PA])
            self.mm(psbk(C, HC)[:, h * C:(h + 1) * C], kT_h, qkvT[:, h, cols], True, True,
                    [qkvB[8 + h], qkvB[h]], [PBk])
        self.act(cf[0:C, 0:8], cf[0:C, 32:40], AF.Exp, [B["cf"]], [B["cf"]])
        self.tt(cf[0:C, 8:16], cf[0:C, 40:48], cf[0:C, 32:40], ALU.subtract, [B["cf"]], [B["cf"]])
        self.act(cf[0:C, 8:16], cf[0:C, 8:16], AF.Exp, [B["cf"]], [B["cf"]])
        self.tt(cf[0:C, 16:24], cf[0:C, 0:8], be_n, ALU.mult, [B["cf"], gbB], [B["cf"]])
        self.ts(cf[0:C, 24:32], be_n, -1.0, None, ALU.mult, None, [gbB], [B["cf"]])
        yield
        self.tt(tA[0:C, :, 0:C], h3(psa(C, HC)), Dm[0:C, :, 0:C], ALU.mult, [PA, B["Dm"]], [B["gU"]])
        self.tt(tA[0:C, :, 0:C], tA[0:C, :, 0:C], cf[0:C, 24:32].unsqueeze(2).to_broadcast([C, 8, C]), ALU.mult,
                [B["gU"], B["cf"]], [B["gU"]])
        self.tt(M0[0:C, :, 0:C], tA[0:C, :, 0:C], tri_gt.unsqueeze(1).to_broadcast([C, 8, C]), ALU.mult,
                [B["gU"], self.cB], [B["M0"]])
        yield
        self.tt(tA[0:C, :, 0:C], h3(psbk(C, HC)), DmT[0:C, :, 0:C], ALU.mult, [PBk, B["DmT"], B["gU"]], [B["gU"]])
        self.tt(QKm[0:C, :, 0:C], tA[0:C, :, 0:C], tri_le.unsqueeze(1).to_broadcast([C, 8, C]), ALU.mult,
                [B["gU"], self.cB], [B["QKm"]])
        mk0 = self.cst[0:C, 512:512 + C].unsqueeze(1).to_broadcast([C, 8, C])
        self.tt(Em[0:C, :, 0:C], M0[0:C, :, 0:C], mk0, ALU.mult, [B["M0"], self.cB], [B["Dm"]])
        yield
        pbN = self.psb(ba, C, HC)
        for h in range(8):
            self.tr(pbN[:, h * C:(h + 1) * C], Em[0:C, h, 0:C], self.cstb[0:C, 0:C], [B["Dm"], self.cB], [PA])
        pbK = self.psb(bb, C, 1024)
        for h in range(8):
            self.tr(pbK[:, h * 128:(h + 1) * 128], qkvT[:, 8 + h, cols], self.identb, [qkvB[8 + h], self.cB], [PBk])
        yield
        self.tt(Wt[0:C, :, 0:C], h3(pbN), self.cst[0:C, 0:C].unsqueeze(1).to_broadcast([C, 8, C]), ALU.add,
                [PA, self.cB], [B["W"]])
        pbK3 = pbK.rearrange("p (h d) -> p h d", h=8)
        self.tt(KBe[0:C], pbK3, cf[0:C, 16:24].unsqueeze(2).to_broadcast([C, 8, 128]), ALU.mult, [PBk, B["cf"]], [B["KBe"]])
        self.tt(Kdec[0:C], pbK3, cf[0:C, 8:16].unsqueeze(2).to_broadcast([C, 8, 128]), ALU.mult, [PBk, B["cf"]], [B["Kdec"]])
        self.tt(qd[:, :, 0:C], qkvT[:, 0:8, cols], EGb[:, :, 0:C], ALU.mult, qkvB[0:8] + [B["EGb"]], [B["qd"]])
        yield
        pbV = self.psb(bb, C, 1024)
        for h in range(8):
            self.tr(pbV[:, h * 128:(h + 1) * 128], qkvT[:, 16 + h, cols], self.identb, [qkvB[16 + h], self.cB], [PBk])
        yield
        pbV3 = pbV.rearrange("p (h d) -> p h d", h=8)
        self.tt(VB[0:C], pbV3, be_n.unsqueeze(2).to_broadcast([C, 8, 128]), ALU.mult, [PBk, gbB], [B["VB"]])
        for lv in range(1, nlev):
            mk = self.cst[0:C, 512 + lv * 64:512 + lv * 64 + C].unsqueeze(1).to_broadcast([C, 8, C])
            self.tt(Em[0:C, :, 0:C], M0[0:C, :, 0:C], mk, ALU.mult, [B["M0"], self.cB], [B["Dm"]])
            pbX = self.psb(ba, C, HC)
            for h in range(8):
                self.tr(pbX[:, h * C:(h + 1) * C], Wt[0:C, h, 0:C], self.cstb[0:C, 0:C], [B["W"], self.cB], [PA])
            yield
            self.cp(Xt[0:C, :, 0:C], h3(pbX), [PA], [B["DmT"]], eng="act")
            for h in range(8):
                self.mm(psbk(C, HC)[:, h * C:(h + 1) * C], Em[0:C, h, 0:C], Wt[0:C, h, 0:C], True, True,
                        [B["Dm"], B["W"]], [PBk])
            yield
            self.cp(Yt[0:C, :, 0:C], h3(psbk(C, HC)), [PBk], [B["gU"]], eng="act")
            yield
            for h in range(8):
                self.mm(psa(C, HC)[:, h * C:(h + 1) * C], Xt[0:C, h, 0:C], Yt[0:C, h, 0:C], True, True,
                        [B["DmT"], B["gU"]], [PA])
            yield
            self.tt(Wt[0:C, :, 0:C], Wt[0:C, :, 0:C], h3(psa(C, HC)), ALU.add, [B["W"], PA], [B["W"]])
            yield
        for h in range(8):
            self.mm(psbk(128, HC)[:, h * C:(h + 1) * C], KBe[0:C, h, :], Wt[0:C, h, 0:C], True, True,
                    [B["KBe"], B["W"]], [PBk])
        yield
        self.amul(nkc[:, :, 0:C], h3(psbk(128, HC)), -1.0, [PBk], [B["nkc"]])
        yield

    def gdn_scan(self, u, n, Z, C, L, oT, oTB):
        HC = 8 * C
        B = Z["B"]
        PB = self.PB

        def h3(a):
            return a.rearrange("p (h c) -> p h c", h=8)
        Wt, QKm, KBe, Kdec, VB, Ub, nkc, qd, EGb = (Z["Wt"], Z["QKm"], Z["KBe"], Z["Kdec"], Z["VB"], Z["Ub"], Z["nkc"],
                                                   Z["qd"], Z["EGb"])
        seg = (n * C) // L
        cols = slice(n * C, (n + 1) * C)
        first_chunk = (n * C) % L == 0 and (u.kind == "S" or u.first)
        Sf = self.Sf[:, 0, :, :]
        if first_chunk:
            if u.kind == "P":
                self.memset(Sf, 0.0, [self.SfB])
            else:
                self.load(Sf, self.sdn[u.sbase + seg].rearrange("h k v -> k h v"), [self.SfB])
            self.cp(self.Sb[:], Sf, [self.SfB], [self.SbB], eng="act")
        for h in range(8):
            bank = 1 + h // 4
            out = self.ps(bank, C)[:, (h % 4) * 128:(h % 4 + 1) * 128]
            self.mm(out, Wt[0:C, h, 0:C], VB[0:C, h, :], True, False, [B["W"], B["VB"]], [PB[bank]])
            self.mm(out, nkc[:, h, 0:C], self.Sb[:, h, :], False, True, [B["nkc"], self.SbB], [PB[bank]])
        yield
        for hf in range(2):
            self.cp(Ub[0:C, hf * 4:hf * 4 + 4, :], self.ps(1 + hf, C).rearrange("p (h d) -> p h d", h=4),
                    [PB[1 + hf]], [B["Ub"]], eng="act" if hf else "dve")
        yield
        for h in range(8):
            out = self.ps(2, 128, HC)[:, h * C:(h + 1) * C]
            self.mm(out, Ub[0:C, h, :], QKm[0:C, h, 0:C], True, False, [B["Ub"], B["QKm"]], [PB[2]])
            self.mm(out, self.Sb[:, h, :], qd[:, h, 0:C], False, True, [self.SbB, B["qd"]], [PB[2]])
        for h in range(8):
            bank = 0 if h < 4 else 3
            self.mm(self.ps(bank)[:, (h % 4) * 128:(h % 4 + 1) * 128], Kdec[0:C, h, :], Ub[0:C, h, :], True, True,
                    [B["Kdec"], B["Ub"]], [PB[bank]])
        yield
        self.cp(oT[:, :, cols], h3(self.ps(2, 128, HC)), [PB[2]], [oTB[n]], eng="act")
        self.tt(Sf, Sf, EGb[:, :, C - 1:C].to_broadcast([128, 8, 128]), ALU.mult, [self.SfB, B["EGb"]], [self.SfB])
        for hf in range(2):
            bank = 0 if hf == 0 else 3
            self.tt(Sf[:, hf * 4:hf * 4 + 4, :], Sf[:, hf * 4:hf * 4 + 4, :],
                    self.ps(bank).rearrange("p (h d) -> p h d", h=4), ALU.add, [self.SfB, PB[bank]], [self.SfB])
        self.cp(self.Sb[:], Sf, [self.SfB], [self.SbB], eng="act")
        last_chunk = ((n + 1) * C) % L == 0 and (u.kind == "S" or u.last)
        if last_chunk:
            dst = self.p_state[u.seq] if u.kind == "P" else self.s_state[u.sbase + seg]
            self.store(dst.rearrange("h k v -> k h v"), Sf, [self.SfB])
        yield

    def gdn(self, u):
        T = u.T
        self.new_phase()
        self.prenorm(u, 1, 1)
        yield
        nseg = len(u.segs)
        L = u.segs[0][1]
        C = min(64, L)
        nch = T // C
        o = 0
        qkvT = self.arena[:, o:o + 24 * T].rearrange("p (c t) -> p c t", c=24)
        o += 24 * T
        gT = self.arena[:, o:o + 8 * T].rearrange("p (c t) -> p c t", c=8)
        o += 8 * T
        oT = self.arena[:, o:o + 8 * T].rearrange("p (c t) -> p c t", c=8)
        o += 8 * T
        gb = self.arena[0:64, o:o + 2 * nch * 16].bitcast(F32).rearrange("p (n c) -> p n c", n=nch)
        o += 2 * nch * 16
        o_fixed = o
        RW = 3 + L
        raw = self.arena[:, o:o + 2 * 3 * nseg * RW].bitcast(F32).rearrange("p (k s w) -> p k s w", k=3, s=nseg)
        o += 6 * nseg * RW
        acc = self.arena[:, o:o + 6 * T].bitcast(F32).rearrange("p (k t) -> p k t", k=3)
        o += 6 * T
        sq = self.arena[:, o:o + 3 * T].rearrange("p (k t) -> p k t", k=3)
        o += 3 * T
        rs = self.arena[:, o:o + 6 * T].bitcast(F32).rearrange("p (k t) -> p k t", k=3)
        o += 6 * T
        tp = self.arena[0:64, o:o + 2 * nch * 8].bitcast(F32).rearrange("p (n c) -> p n c", n=nch)
        o += 2 * nch * 8
        tp2 = self.arena[0:64, o:o + 2 * nch * 8].bitcast(F32).rearrange("p (n c) -> p n c", n=nch)
        o += 2 * nch * 8
        tp3 = self.arena[0:64, o:o + 2 * nch * 8].bitcast(F32).rearrange("p (n c) -> p n c", n=nch)
        o += 2 * nch * 8
        ob = self.arena[0:3, o:o + 2 * D].bitcast(F32)
        o += 2 * D
        qkvB = [self.abuf("qkv%d" % m) for m in range(24)]
        gTB, oTB = self.abuf("gT"), [self.abuf("oT%d" % n) for n in range(nch)]
        rawB, accB = [self.abuf("raw%d" % i) for i in range(3)], [self.abuf("acc%d" % i) for i in range(3)]
        sqB, rsB = [self.abuf("sq%d" % i) for i in range(3)], [self.abuf("rs%d" % i) for i in range(3)]
        obB = self.abuf("dncout")
        gbB = self.abuf("gb")
        if u.kind == "S":
            nr = nseg * 3
            ctm = self.arena[0:nr, o:o + 2 * 3 * D].bitcast(F32)
            o += 2 * 3 * D
            ctmB = self.abuf("dctm")
            self.load(ctm, self.cdn[u.sbase * 3:u.sbase * 3 + nr, :], [ctmB])
            for m in range(24):
                bank = 2 + m % 2
                self.tr(self.ps(bank, 128, nr), ctm[:, m * 128:(m + 1) * 128], self.cst[0:nr, 0:nr],
                        [ctmB, self.cB], [self.PB[bank]])
                for s in range(nseg):
                    self.cp(self.dnh[:, s, m, :], self.ps(bank, 128, nr)[:, s * 3:s * 3 + 3],
                            [self.PB[bank]], [self.dnhB[s]])
        elif u.first:
            self.memset(self.dnh[:, 0, :, :], 0.0, [self.dnhB[0]])
        assert o <= self.c.an, o
        yield
        NBUF = 3
        free = list(range(NBUF))
        pend = [(ti, nm, mi) for ti, nm in enumerate(("dq", "dk", "dv", "dg")) for mi in range(8)]
        slots = {}
        live = []

        def proj_chunk(ti, nm, mi, kk):
            slot, slotB = slots[nm]
            m = ti * 8 + mi
            bank = m % 2
            for k in range(8):
                oo = (mi * 8 + k) * 128
                self.mm(self.ps(bank, 128, T), slot[:, oo:oo + 128], self.hT[:, k, 0:T], k == 0, k == 7,
                        [slotB, self.hTB], [self.PB[bank]])
            yield
            if nm == "dg":
                self.act(gT[:, mi, 0:T], self.ps(bank, 128, T), AF.Silu, [self.PB[bank]], [gTB])
                return
            for s_, (c0, Ls, sidx) in enumerate(u.segs):
                self.cp(raw[:, kk, s_, 0:3], self.dnh[:, sidx, m, :], [self.dnhB[sidx]], [rawB[kk]])
            self.cp(raw[:, kk, :, 3:3 + L], self.ps(bank, 128, T).rearrange("p (s w) -> p s w", s=nseg),
                    [self.PB[bank]], [rawB[kk]], eng="act")
            yield
            for s_, (c0, Ls, sidx) in enumerate(u.segs):
                self.cp(self.dnh[:, sidx, m, :], raw[:, kk, s_, L:L + 3], [rawB[kk]], [self.dnhB[sidx]])
            a3 = acc[:, kk, 0:T].rearrange("p (s w) -> p s w", s=nseg)
            self.ts(a3, raw[:, kk, :, 0:L], self.vcol(V_DNCW + m), None, ALU.mult, None, [rawB[kk], self.cB], [accB[kk]])
            for j in range(1, 4):
                self.stt(a3, raw[:, kk, :, j:j + L], self.vcol(V_DNCW + j * 24 + m), a3, ALU.mult, ALU.add,
                         [rawB[kk], accB[kk], self.cB], [accB[kk]])
            yield
            if nm == "dv":
                self.act(qkvT[:, m, 0:T], acc[:, kk, 0:T], AF.Silu, [accB[kk]], [qkvB[m]])
                return
            self.act(rs[:, kk, 0:T], acc[:, kk, 0:T], AF.Exp, [accB[kk]], [rsB[kk]], scale=-1.0)
            self.act(rs[:, kk, 0:T], rs[:, kk, 0:T], AF.Ln, [rsB[kk], self.cB], [rsB[kk]], bias=self.vcol(V_ONE))
            self.act(rs[:, kk, 0:T], rs[:, kk, 0:T], AF.Exp, [rsB[kk]], [rsB[kk]], scale=-1.0)
            yield
            self.tt(acc[:, kk, 0:T], acc[:, kk, 0:T], rs[:, kk, 0:T], ALU.mult, [accB[kk], rsB[kk]], [accB[kk]])
            yield
            self.act(sq[:, kk, 0:T], acc[:, kk, 0:T], AF.Square, [accB[kk]], [sqB[kk]])
            yield
            sbk = 2 + m % 2
            self.mm(self.ps(sbk, 128, T), self.onesb, sq[:, kk, 0:T], True, True, [sqB[kk], self.cB], [self.PB[sbk]])
            yield
            self.act(rs[:, kk, 0:T], self.ps(sbk, 128, T), AF.Ln, [self.PB[sbk], self.cB], [rsB[kk]],
                     bias=self.vcol(V_EPS))
            self.act(rs[:, kk, 0:T], rs[:, kk, 0:T], AF.Exp, [rsB[kk], self.cB], [rsB[kk]], scale=-0.5,
                     bias=self.vcol(V_LNQS if nm == "dq" else V_ZERO))
            yield
            self.tt(qkvT[:, m, 0:T], acc[:, kk, 0:T], rs[:, kk, 0:T], ALU.mult, [accB[kk], rsB[kk]], [qkvB[m]])

        def ab_chain():
            slot, slotB = slots["dab"]
            for n in range(nch):
                for k in range(8):
                    self.mm(self.ps(3, C, nch * 16)[:, n * 16:(n + 1) * 16], self.hT[:, k, n * C:(n + 1) * C],
                            slot[:, k * 16:(k + 1) * 16], k == 0, k == 7, [self.hTB, slotB], [self.PB[3]])
            ab3 = self.ps(3, C, nch * 16).rearrange("p (n c) -> p n c", n=nch)
            dtb = self.hsm[0:C, 8:16].unsqueeze(1).to_broadcast([C, nch, 8])
            self.tt(tp[0:C], ab3[:, :, 0:8], dtb, ALU.add, [self.PB[3], self.cB], [gbB])
            self.act(gb[0:C, :, 8:16], ab3[:, :, 8:16], AF.Sigmoid, [self.PB[3]], [gbB])
            yield
            self.ts(gb[0:C, :, 0:8], tp[0:C], -1.0, None, ALU.mult, None, [gbB], [gbB])
            self.tt(gb[0:C, :, 0:8], gb[0:C, :, 0:8], tp[0:C], ALU.max, [gbB], [gbB])
            yield
            self.act(gb[0:C, :, 0:8], gb[0:C, :, 0:8], AF.Exp, [gbB], [gbB], scale=-1.0)
            yield
            yv = gb[0:C, :, 0:8]
            pl = tp2[0:C]
            mk = tp3[0:C]
            self.ts(pl, yv, 0.2, -0.25, ALU.mult, ALU.add, [gbB], [gbB])
            for cst_ in (1.0 / 3, -0.5, 1.0):
                self.tt(pl, pl, yv, ALU.mult, [gbB], [gbB])
                self.ts(pl, pl, cst_, None, ALU.add, None, [gbB], [gbB])
            self.tt(pl, pl, yv, ALU.mult, [gbB], [gbB])
            self.ts(mk, yv, 0.0625, None, ALU.is_lt, None, [gbB], [gbB])
            yield
            self.act(yv, yv, AF.Ln, [gbB, self.cB], [gbB], bias=self.vcol(V_ONE, C))
            yield
            self.tt(pl, pl, yv, ALU.subtract, [gbB], [gbB])
            self.tt(pl, pl, mk, ALU.mult, [gbB], [gbB])
            self.tt(yv, yv, pl, ALU.add, [gbB], [gbB])
            self.ts(tp[0:C], tp[0:C], 0.0, None, ALU.max, None, [gbB], [gbB])
            self.tt(gb[0:C, :, 0:8], gb[0:C, :, 0:8], tp[0:C], ALU.add, [gbB], [gbB])
            self.tt(gb[0:C, :, 0:8], gb[0:C, :, 0:8], self.nega[0:C, :].unsqueeze(1).to_broadcast([C, nch, 8]), ALU.mult,
                    [gbB, self.cB], [gbB])

        slots["dab"] = self.wload(("dab",))
        yield
        live.append((None, ab_chain()))
        while pend or live:
            if pend and free:
                ti, nm, mi = pend.pop(0)
                if nm not in slots:
                    slots[nm] = self.wload((nm,))
                    yield
                kk = free.pop(0)
                live.append((kk, proj_chunk(ti, nm, mi, kk)))
            nxt = []
            for kk, g_ in live:
                try:
                    next(g_)
                    nxt.append((kk, g_))
                except StopIteration:
                    if kk is not None:
                        free.append(kk)
            live = nxt
            yield
        for s, (c0, Ls, sidx) in enumerate(u.segs):
            if u.kind == "P" and not u.last:
                continue
            dst = self.p_dnc[u.seq] if u.kind == "P" else self.s_dnc[u.sbase + s]
            for part in range(3):
                for mm_ in range(8):
                    m = part * 8 + mm_
                    bank = 2 + m % 2
                    self.tr(self.ps(bank, 3, 128), self.dnh[:, sidx, m, :], self.identf, [self.dnhB[sidx], self.cB], [self.PB[bank]])
                    self.cp(ob[:, mm_ * 128:(mm_ + 1) * 128], self.ps(bank, 3, 128), [self.PB[bank]], [obB])
                self.store(dst[:, part * D:(part + 1) * D], ob, [obB])
            yield
        self.new_phase()
        HC = 8 * C
        c_ = u.c

        def h3(a):
            return a.rearrange("p (h c) -> p h c", h=8)

        def hd(a):
            return a.rearrange("p (h d) -> p h d", h=8)
        o = o_fixed

        def car(n, parts=128):
            nonlocal o
            a = self.arena[0:parts, o:o + n]
            o += n
            return a
        sets = []
        for which in range(2):
            Z = {}
            if which == 0:
                gUr, Dmr, DmTr = car(2 * HC, 64), car(HC, 64), car(HC, 64)
                EGr, cfr = car(2 * HC), car(128, 64)
                M0r, Wr, QKr = car(HC, 64), car(HC, 64), car(HC, 64)
                KBr, Kdr, VBr, Ubr = car(1024, 64), car(1024, 64), car(1024, 64), car(1024, 64)
                nkr, qdr = car(HC), car(HC)
            else:
                EGr, nkr, qdr, cfr = car(2 * HC), car(HC), car(HC), car(128, 64)
                hTf = c_.hT[:, :, :].rearrange("p c t -> p (c t)")
                xsf = c_.xs_sc[:, :, :].rearrange("p k d -> p (k d)")
                gpf = c_.gpost[:, :].bitcast(BF16)
                tmf = c_.tmpf[:, :].bitcast(BF16)
                KBr, Kdr = hTf[0:64, 0:1024], hTf[0:64, 1024:2048]
                VBr, Ubr = xsf[0:64, 0:1024], xsf[0:64, 1024:2048]
                gUr, Dmr, DmTr = gpf[0:64, 0:2 * HC], gpf[0:64, 1024:1024 + HC], gpf[0:64, 1536:1536 + HC]
                M0r, Wr = c_.junk[0:64, 0:HC], c_.junk[0:64, 512:512 + HC]
                QKr = tmf[0:64, 0:HC]
            Z["gU2"] = gUr.bitcast(F32)
            Z["gU"] = h3(Z["gU2"])
            Z["Yt"] = h3(gUr[:, 0:HC])
            Z["Dm"], Z["Em"] = h3(Dmr), h3(Dmr)
            Z["DmT"], Z["Xt"] = h3(DmTr), h3(DmTr)
            Z["EGb"] = h3(EGr.bitcast(F32))
            Z["cf"] = cfr.bitcast(F32)
            Z["M0"], Z["Wt"], Z["QKm"] = h3(M0r), h3(Wr), h3(QKr)
            Z["KBe"], Z["Kdec"], Z["VB"], Z["Ub"] = hd(KBr), hd(Kdr), hd(VBr), hd(Ubr)
            Z["nkc"], Z["qd"] = h3(nkr), h3(qdr)
            Z["B"] = {n_: self.abuf(n_ + str(which)) for n_ in ("gU", "Dm", "DmT", "EGb", "cf", "M0", "W", "QKm", "KBe", "Kdec",
                                                                "VB", "Ub", "nkc", "qd")}
            Z["bm"] = (0, 1) if which == 0 else (2, 3)
            sets.append(Z)
        assert o <= self.c.an, o
        PB = self.PB
        for n0 in range(0, nch, 2):
            pair = [n_ for n_ in (n0, n0 + 1) if n_ < nch]
            live = [self.gdn_prep(u, n_, sets[i], C, L, qkvT, qkvB, gb, gbB) for i, n_ in enumerate(pair)]
            while live:
                nxt = []
                for g_ in live:
                    try:
                        next(g_)
                        nxt.append(g_)
                    except StopIteration:
                        pass
                    yield
                live = nxt
            for i, n_ in enumerate(pair):
                yield from self.gdn_scan(u, n_, sets[i], C, L, oT, oTB)
        allB = [b_ for Z in sets[1:] for b_ in Z["B"].values()]
        self.memset(c_.hT[0:1, 0, 0:2], 0.0, [c_.hTB] + allB)
        self.memset(c_.xs_sc[0:1, 0, 0:2], 0.0, [c_.xs_scB[0], c_.xs_scB[1]] + allB)
        self.memset(c_.gpost[0:1, 0:2], 0.0, [c_.gpostB] + allB)
        self.memset(c_.junk[0:1, 0:2], 0.0, [c_.junkB] + allB)
        self.memset(c_.tmpf[0:1, 0:2], 0.0, [c_.tmpfB] + allB)
        yield
        self.new_phase()
        sqB, rsB, accB = [self.abuf("sq%d" % i) for i in range(3)], [self.abuf("rs%d" % i) for i in range(3)], [self.abuf("acc%d" % i) for i in range(3)]
        def onorm_head(h, kk):
            self.act(sq[:, kk, 0:T], oT[:, h, 0:T], AF.Square, oTB, [sqB[kk]])
            yield
            sbk = 2 + h % 2
            self.mm(self.ps(sbk, 128, T), self.onesb, sq[:, kk, 0:T], True, True, [sqB[kk], self.cB], [PB[sbk]])
            yield
            self.act(rs[:, kk, 0:T], self.ps(sbk, 128, T), AF.Ln, [PB[sbk], self.cB], [rsB[kk]], scale=1.0 / 128,
                     bias=self.vcol(V_EPS))
            self.act(rs[:, kk, 0:T], rs[:, kk, 0:T], AF.Exp, [rsB[kk]], [rsB[kk]], scale=-0.5)
            yield
            self.tt(acc[:, kk, 0:T], oT[:, h, 0:T], rs[:, kk, 0:T], ALU.mult, oTB + [rsB[kk]], [accB[kk]])
            yield
            self.stt(qkvT[:, h, 0:T], acc[:, kk, 0:T], self.vcol(V_DNG), gT[:, h, 0:T], ALU.mult, ALU.mult,
                     [accB[kk], gTB, self.cB], [qkvB[h]])

        pendh = list(range(8))
        liveh = []
        freeh = [0, 1, 2]
        while pendh or liveh:
            if pendh and freeh:
                h = pendh.pop(0)
                kk = freeh.pop(0)
                liveh.append((kk, onorm_head(h, kk)))
            nxt = []
            for kk, g_ in liveh:
                try:
                    next(g_)
                    nxt.append((kk, g_))
                except StopIteration:
                    freeh.append(kk)
            liveh = nxt
            yield
        self.gpost_load(1 * 4 + 1)
        slot, slotB = self.wload(("dout",))
        yield
        yield from self.proj_out(u, qkvT, qkvB[0:8], slot, slotB, False)


_CACHE = {}


def _program(cfg_key):
    if cfg_key not in _CACHE:
        _CACHE[cfg_key] = Builder(dict(cfg_key)).build()
    return _CACHE[cfg_key]


def run_cores(inp, n_cores, n_pseq, plen, n_sseq, stop=99):
    cfg = (("n_pseq", n_pseq), ("plen", plen), ("n_sseq", n_sseq), ("stop", stop))
    nc = _program(cfg)
    f = lambda a: np.ascontiguousarray(np.asarray(a, dtype=np.float32))
    wpack, offs = build_tiles({k: np.asarray(v, np.float32) for k, v in inp.items()})
    o2, _ = tile_offsets()
    assert offs == o2
    vec, bvec, hsm, cst = build_vecs({k: np.asarray(v, np.float32) for k, v in inp.items()})
    in_maps = []
    for c in range(n_cores):
        ps = slice(c * n_pseq, (c + 1) * n_pseq)
        m = {"xp": f(inp["x_prompt"][ps]), "memp": f(inp["mem_prompt"][ps]), "wpack": wpack, "vecs": vec,
             "bvec": bvec, "hsm": hsm, "cst": cst}
        if n_sseq:
            ss = slice(c * n_sseq, (c + 1) * n_sseq)
            m["xs"] = f(inp["x_sample"][ss]).reshape(n_sseq * 16, D)
            m["cconv"] = f(inp["cache_conv_a"][0, ss]).reshape(n_sseq * 30, D)
            m["sdn"] = f(inp["state_dn"][0, ss])
            m["cdn"] = f(inp["cache_dn_conv"][0, ss]).reshape(n_sseq * 3, 3 * D)
            m["cmk"] = f(inp["cache_mem_k"][:, ss]).reshape(2, n_sseq, NMEM, D)
            m["cmv"] = f(inp["cache_mem_v"][:, ss]).reshape(2, n_sseq, NMEM, D)
        in_maps.append(m)
    import os as _os
    if _os.environ.get("KTRACE"):
        res = run_bass_kernel_spmd(nc, in_maps, core_ids=list(range(n_cores)), trace=True)
        print("EXEC_NS", res.exec_time_ns)
    else:
        res = run_bass_kernel_spmd(nc, in_maps, core_ids=list(range(n_cores)))
    R = res.results
    cat = lambda k, ax=0: np.concatenate([r[k] for r in R], axis=ax)
    out = {}
    out["y_prompt"] = cat("yp")
    out["p_conv_a"] = cat("p_conv")[None]
    out["p_state_dn"] = cat("p_state")[None]
    out["p_dn_conv"] = cat("p_dnc")[None]
    out["p_mem_k"] = cat("p_mk", 1).reshape(2, -1, NMEM, 4, 256)
    out["p_mem_v"] = cat("p_mv", 1).reshape(2, -1, NMEM, 4, 256)
    if n_sseq:
        out["y_sample"] = cat("ys").reshape(-1, 16, D)
        out["s_conv_a"] = cat("s_conv")[None]
        out["s_state_dn"] = cat("s_state")[None]
        out["s_dn_conv"] = cat("s_dnc")[None]
    return out


def kernel(**inputs):
    o = run_cores(inputs, 8, 2, 2048, 4)
    return (o["y_prompt"], o["y_sample"], o["p_conv_a"], o["p_state_dn"], o["p_dn_conv"], o["p_mem_k"], o["p_mem_v"],
            o["s_conv_a"], o["s_state_dn"], o["s_dn_conv"])
```
